# Optimizing a Trainium2 kernel written in Bass

```python
import jax, jax.numpy as jnp
from jax import lax
import numpy as np

D_MODEL = 1024
BATCH = 2
SEQ = 8192
DEPTH = 4

HEAD_DIM = 64
N_Q_HEADS = D_MODEL // (2 * HEAD_DIM)
N_KV_HEADS = max(1, N_Q_HEADS // 4)
GQA_GROUP = N_Q_HEADS // N_KV_HEADS
WINDOW = 128
ATTN_BLOCK = 128
GM_HEADS = N_Q_HEADS
GM_HEAD_DIM = HEAD_DIM
CHUNK = 128
ATTN_W = N_Q_HEADS * HEAD_DIM
KV_W = N_KV_HEADS * HEAD_DIM
GM_W = GM_HEADS * GM_HEAD_DIM
D_MIX = ATTN_W + GM_W
D_IN = ATTN_W + 2 * KV_W + 2 * GM_W
D_FF = ((8 * D_MODEL + 3 * 256 - 1) // (3 * 256)) * 256
PLE_DIM = 256
NORM_EPS = 1e-6
NEG_BIG = -1e30

kernel_name = "hymba_swa_gmlp_hybrid"


def rmsnorm(x, g):
    xf = x.astype(jnp.float32)
    y = xf * lax.rsqrt(jnp.mean(xf * xf, axis=-1, keepdims=True) + NORM_EPS)
    return (y * g.astype(jnp.float32)).astype(x.dtype)


def layernorm(x, g, b):
    xf = x.astype(jnp.float32)
    mu = jnp.mean(xf, axis=-1, keepdims=True)
    xc = xf - mu
    y = xc * lax.rsqrt(jnp.mean(xc * xc, axis=-1, keepdims=True) + NORM_EPS)
    return (y * g.astype(jnp.float32) + b.astype(jnp.float32)).astype(x.dtype)


def alibi_slopes(n_heads):
    return jnp.exp2(-8.0 * (jnp.arange(n_heads, dtype=jnp.float32) + 1.0) / n_heads)


def sliding_window_attention(q, k, v, sinks):
    B, S = q.shape[0], q.shape[1]
    nb = S // ATTN_BLOCK
    qb = q.reshape(B, nb, ATTN_BLOCK, N_KV_HEADS, GQA_GROUP, HEAD_DIM)
    kb = k.reshape(B, nb, ATTN_BLOCK, N_KV_HEADS, HEAD_DIM)
    vb = v.reshape(B, nb, ATTN_BLOCK, N_KV_HEADS, HEAD_DIM)
    pad = ((0, 0), (1, 0), (0, 0), (0, 0), (0, 0))
    kk = jnp.concatenate([jnp.pad(kb[:, :-1], pad), kb], axis=2)
    vv = jnp.concatenate([jnp.pad(vb[:, :-1], pad), vb], axis=2)
    scale = HEAD_DIM ** -0.5
    s = jnp.einsum('bnqkgd,bnskd->bnkgqs', qb, kk,
                   preferred_element_type=jnp.float32) * scale
    qi = jnp.arange(ATTN_BLOCK)[:, None]
    kj = jnp.arange(2 * ATTN_BLOCK)[None, :]
    dist = qi + ATTN_BLOCK - kj
    band = (dist >= 0) & (dist < WINDOW)
    blk = jnp.arange(nb)[:, None, None]
    valid = band[None] & ((blk > 0) | (kj >= ATTN_BLOCK)[None])
    slopes = alibi_slopes(N_Q_HEADS).reshape(N_KV_HEADS, GQA_GROUP)
    bias = -slopes[:, :, None, None] * dist.astype(jnp.float32)
    s = jnp.where(valid[None, :, None, None], s + bias[None, None], NEG_BIG)
    sink = sinks.astype(jnp.float32).reshape(N_KV_HEADS, GQA_GROUP)[None, None, :, :, None, None]
    m = jnp.maximum(jnp.max(s, axis=-1, keepdims=True), sink)
    e = jnp.exp(s - m)
    pr = e / (jnp.sum(e, axis=-1, keepdims=True) + jnp.exp(sink - m))
    o = jnp.einsum('bnkgqs,bnskd->bnqkgd', pr.astype(vv.dtype), vv)
    return o.reshape(B, S, ATTN_W)


def chunked_spatial_gating(zu, zv, ln_g, ln_b, ws, bs):
    B, S = zu.shape[0], zu.shape[1]
    nc = S // CHUNK
    zv = layernorm(zv, ln_g, ln_b)
    vh = zv.reshape(B, nc, CHUNK, GM_HEADS, GM_HEAD_DIM)
    causal = jnp.tril(jnp.ones((CHUNK, CHUNK), dtype=bool))
    w = jnp.where(causal[None], ws, jnp.zeros_like(ws))
    mixed = jnp.einsum('hts,bnshc->bnthc', w, vh) + bs.T[None, None, :, :, None]
    return zu * mixed.reshape(B, S, GM_W)


def setup_inputs(seed: int = 0) -> dict:
    key = jax.random.key(seed)
    ks = jax.random.split(key, 24)
    f32 = jnp.float32
    nrm = lambda k, shape, s: jax.random.normal(k, shape, f32) * s
    gain = lambda k, n: 1.0 + 0.05 * jax.random.normal(k, (DEPTH, n), f32)
    return {
        "x": nrm(ks[0], (BATCH, SEQ, D_MODEL), 1.0),
        "p": nrm(ks[1], (DEPTH, BATCH, SEQ, PLE_DIM), 1.0),
        "ln_mix_pre": gain(ks[2], D_MODEL),
        "w_in": nrm(ks[3], (DEPTH, D_MODEL, D_IN), D_MODEL ** -0.5),
        "attn_sinks": nrm(ks[4], (DEPTH, N_Q_HEADS), 1.0),
        "gm_ln_g": gain(ks[5], GM_W),
        "gm_ln_b": nrm(ks[6], (DEPTH, GM_W), 0.01),
        "gm_ws": nrm(ks[7], (DEPTH, GM_HEADS, CHUNK, CHUNK), CHUNK ** -0.5),
        "gm_bs": 1.0 + nrm(ks[8], (DEPTH, GM_HEADS, CHUNK), 0.01),
        "g_attn_out": gain(ks[9], ATTN_W),
        "g_gm_out": gain(ks[10], GM_W),
        "w_out": nrm(ks[11], (DEPTH, D_MIX, D_MODEL), D_MIX ** -0.5),
        "ln_mix_post": gain(ks[12], D_MODEL),
        "ln_ffn_pre": gain(ks[13], D_MODEL),
        "w_ffn_gate": nrm(ks[14], (DEPTH, D_MODEL, D_FF), D_MODEL ** -0.5),
        "w_ffn_up": nrm(ks[15], (DEPTH, D_MODEL, D_FF), D_MODEL ** -0.5),
        "w_ffn_down": nrm(ks[16], (DEPTH, D_FF, D_MODEL), D_FF ** -0.5),
        "ln_ffn_post": gain(ks[17], D_MODEL),
        "w_ple": nrm(ks[18], (DEPTH, PLE_DIM, D_MODEL), PLE_DIM ** -0.5),
        "ln_ple_gate": gain(ks[19], D_MODEL),
        "w_ple_gate": nrm(ks[20], (DEPTH, D_MODEL, D_MODEL), D_MODEL ** -0.5),
    }


def reference(x, p, ln_mix_pre, w_in, attn_sinks, gm_ln_g, gm_ln_b, gm_ws, gm_bs,
              g_attn_out, g_gm_out, w_out, ln_mix_post, ln_ffn_pre, w_ffn_gate,
              w_ffn_up, w_ffn_down, ln_ffn_post, w_ple, ln_ple_gate, w_ple_gate):
    h = x
    splits = [ATTN_W, ATTN_W + KV_W, ATTN_W + 2 * KV_W, ATTN_W + 2 * KV_W + GM_W]
    for i in range(DEPTH):
        a = rmsnorm(h, ln_mix_pre[i])
        z = a @ w_in[i]
        q, k, v, zu, zv = jnp.split(z, splits, axis=-1)
        attn = sliding_window_attention(q, k, v, attn_sinks[i])
        gm = chunked_spatial_gating(jax.nn.gelu(zu), jax.nn.gelu(zv),
                                    gm_ln_g[i], gm_ln_b[i], gm_ws[i], gm_bs[i])
        heads = jnp.concatenate([rmsnorm(attn, g_attn_out[i]),
                                 rmsnorm(gm, g_gm_out[i])], axis=-1)
        h = h + rmsnorm(heads @ w_out[i], ln_mix_post[i])
        f = rmsnorm(h, ln_ffn_pre[i])
        f = (jax.nn.silu(f @ w_ffn_gate[i]) * (f @ w_ffn_up[i])) @ w_ffn_down[i]
        h = h + rmsnorm(f, ln_ffn_post[i])
        gate = jax.nn.sigmoid(rmsnorm(h, ln_ple_gate[i]) @ w_ple_gate[i])
        h = h + (p[i] @ w_ple[i]) * gate
    return h
```

```python
import numpy as np
import concourse.bass as bass
import concourse.mybir as mybir
from concourse.bass_utils import run_bass_kernel_spmd
from contextlib import ExitStack

F32 = mybir.dt.float32
BF16 = mybir.dt.bfloat16
AF = mybir.ActivationFunctionType
ALU = mybir.AluOpType
AX = mybir.AxisListType

D = 1024
KC = 8
DFF = 2816
NFC = 22
EPS = 1e-6
NCORES = 8
OWN = 16
FUSED = True
NS = 9
NP = 49
NHEADS = 1


class Buf:
    __slots__ = ("name", "w", "r", "psum", "last")

    def __init__(self, name, psum=False):
        self.name = name
        self.psum = psum
        self.last = -1
        self.w = None
        self.r = []


class Prog:
    ENGS = ("pe", "act", "dve", "pool", "sp")

    def __init__(self, nc, stack):
        self.nc = nc
        self.stack = stack
        self.streams = {e: [] for e in self.ENGS}
        self.sems = {}
        for e in self.ENGS:
            self.sems[e] = stack.enter_context(nc.semaphore("sem_" + e))
        self.cnt = {e: 0 for e in self.ENGS}
        self.idx = {e: 0 for e in self.ENGS}
        self.unsig = {e: False for e in self.ENGS}
        self.known = {e: {} for e in self.ENGS}
        self.dma_sems = {}
        self.dma_cnt = {}
        self.gidx = 0

    def dma_sem(self, name):
        if name not in self.dma_sems:
            self.dma_sems[name] = self.stack.enter_context(self.nc.semaphore("dsem_" + name))
            self.dma_cnt[name] = 0
            self.sems["dma:" + name] = self.dma_sems[name]
        return name

    def _need(self, eng, dep):
        key, val, deng, didx = dep
        if deng == eng:
            if eng in ("pe", "sp"):
                return
        if self.known[eng].get(key, 0) >= val:
            return
        self.known[eng][key] = val
        self.streams[eng].append(("wait", key, val))

    def _deps(self, eng, reads, writes):
        for b in reads:
            if b.w is not None:
                self._need(eng, b.w)
            if b.psum:
                for r in b.r:
                    if r[2] != eng:
                        self._need(eng, r)
        for b in writes:
            if b.w is not None:
                self._need(eng, b.w)
            for r in b.r:
                self._need(eng, r)

    def op(self, eng, fn, reads=(), writes=(), signal=True):
        self._deps(eng, reads, writes)
        if signal:
            self.cnt[eng] += 1
            self.unsig[eng] = False
        else:
            self.unsig[eng] = True
        tick = self.cnt[eng] if signal else self.cnt[eng] + 1
        me = (eng, tick, eng, self.idx[eng])
        self.idx[eng] += 1
        self.streams[eng].append(("op", fn, signal))
        self.gidx += 1
        for b in reads:
            b.last = self.gidx
            if not b.r or b.r[-1][:2] != me[:2]:
                b.r.append(me)
        for b in writes:
            b.last = self.gidx
            b.w = me
            b.r = []

    def dma(self, eng, fn, semname, reads=(), writes=()):
        self.dma_sem(semname)
        self._deps(eng, reads, writes)
        self.dma_cnt[semname] += 16
        key = "dma:" + semname
        me = (key, self.dma_cnt[semname], "dma", -1)
        self.idx[eng] += 1
        self.streams[eng].append(("dma", fn, semname))
        for b in reads:
            b.r.append(me)
        for b in writes:
            b.w = me
            b.r = []

    def force_signal(self, eng):
        if not self.unsig[eng]:
            return
        st = self.streams[eng]
        for k in range(len(st) - 1, -1, -1):
            if st[k][0] == "op":
                assert not st[k][2]
                st[k] = ("op", st[k][1], True)
                break
        self.cnt[eng] += 1
        self.unsig[eng] = False

    def wait_all(self, eng, bufs):
        best = {}
        for b in bufs:
            for dep in ([b.w] if b.w is not None else []) + list(b.r):
                if dep[0] not in best or best[dep[0]][1] < dep[1]:
                    best[dep[0]] = dep
        for dep in best.values():
            self._need(eng, dep)

    def emit(self):
        for e in ("pe", "act", "dve", "pool"):
            assert not self.unsig[e], f"engine {e} ends with unsignalled instruction"
        nc = self.nc
        with nc.Block() as block:
            def run(engname):
                def body(eng):
                    mysem = self.sems[engname]
                    for ent in self.streams[engname]:
                        if ent[0] == "wait":
                            eng.wait_ge(self.sems[ent[1]], ent[2])
                        elif ent[0] == "op":
                            ins = ent[1](eng)
                            if ent[2]:
                                ins.then_inc(mysem, 1)
                        else:
                            ins = ent[1](eng)
                            ins.then_inc(self.dma_sems[ent[2]], 16)
                return body
            block.tensor(run("pe"))
            block.scalar(run("act"))
            block.vector(run("dve"))
            block.gpsimd(run("pool"))
            block.sync(run("sp"))


class RR:
    def __init__(self, items):
        self.items = items
        self.i = 0

    def get(self):
        it = self.items[self.i % len(self.items)]
        self.i += 1
        return it


def build_program(L, H, OWN=OWN):
    NB = H + OWN
    NT = NB * 128
    nc = bass.Bass("TRN2", target_bir_lowering=False)
    dt = lambda name, shape, dtype=F32, kind="ExternalInput": nc.dram_tensor(name, shape, dtype, kind=kind)
    x_d = dt("x", [NT, D])
    pT_d = dt("pT", [L, 2, 128, NT])
    w_in_d = dt("w_in", [L, D, 1792])
    w_out_d = dt("w_out", [L, D, D])
    w_gate_d = dt("w_gate", [L, D, DFF])
    w_up_d = dt("w_up", [L, D, DFF])
    w_down_d = dt("w_down", [L, DFF, D])
    w_ple_d = dt("w_ple", [L, 256, D])
    w_pg_d = dt("w_pg", [L, D, D])
    ws_d = dt("gm_ws", [L, 8, 128, 128])
    colg_d = dt("colg", [L, 128, 32])
    rowv_d = dt("rowv", [L, 1, 3072])
    bsT_d = dt("bsT", [L, 128, 8])
    sinks_d = dt("sinks", [L, 1, 8])
    ident_d = dt("ident", [128, 128])
    bias_d = dt("bias", [128, 2048])
    tril_d = dt("tril", [128, 128])
    flag_d = dt("flag", [128, 1])
    y_d = dt("y", [OWN * 128, D], F32, "ExternalOutput")
    wsc_d = dt("wsc", [L, NP, 128, 2048], BF16, "Internal")

    with ExitStack() as st:
        P = Prog(nc, st)
        sb = lambda name, shape, dtype: st.enter_context(nc.sbuf_tensor(name, shape, dtype))

        h = sb("h", [128, NB, D], F32)
        b_h = [Buf(f"h{i}") for i in range(NB)]
        ring = sb("ring", [128, NS, 2048], BF16)
        b_ring = [Buf(f"ring{i}") for i in range(NS)]
        xTa = sb("xTa", [128, KC, 512], BF16)
        b_xTa = [Buf(f"xTa{i}") for i in range(4)]
        xTb = sb("xTb", [128, KC, 512], BF16)
        b_xTb = [Buf(f"xTb{i}") for i in range(4)]
        qT = sb("qT", [128, 4, 512], BF16)
        b_qT = [Buf(f"qT{i}") for i in range(4)]
        kT = sb("kT", [128, 8 * 128], BF16)
        b_kT = [Buf(f"kT{i}") for i in range(8)]
        vA = sb("vA", [128, 8, 130], BF16)
        b_vA = [Buf(f"vA{i}") for i in range(8)]
        actT = sb("actT", [128, NFC, 512], BF16)
        b_actT = [Buf(f"actT{i}") for i in range(NFC)]
        pTt = [qT[:, 2 * i:2 * i + 2, :] for i in range(2)]
        b_pTt = [[b_qT[2 * i], b_qT[2 * i + 1]] for i in range(2)]
        scr32 = RR([(sb(f"s32_{i}", [128, D], F32), Buf(f"s32_{i}")) for i in range(4)])
        scr16 = RR([(sb(f"s16_{i}", [128, D], BF16), Buf(f"s16_{i}")) for i in range(3)])
        headsp = RR([(sb(f"heads_{i}", [128, D], BF16), Buf(f"heads_{i}")) for i in range(NHEADS)])
        stat = RR([(sb(f"st_{i}", [128, 8], F32), Buf(f"st_{i}")) for i in range(24)])
        def _a32(i):
            return (actT[:, 4 * i:4 * i + 4, :].rearrange("p a b -> p (a b)").bitcast(F32), b_actT[4 * i:4 * i + 4])

        def _a16(i):
            return (actT[:, 16 + 2 * i:18 + 2 * i, :].rearrange("p a b -> p (a b)"), b_actT[16 + 2 * i:18 + 2 * i])

        d32 = [(t_[:], [b_]) for (t_, b_) in scr32.items]
        d16 = [(t_[:], [b_]) for (t_, b_) in scr16.items]
        gx_pool = RR([_a32(0), _a32(1), _a32(2)])
        stmp_pool = RR([_a32(3), d32[0], d32[1]])
        at_pool = RR([d32[2], d32[3]])
        E_pool = RR([_a16(0), _a16(1), _a16(2), d16[0]])
        vh_pool = RR([d16[1], d16[2]])
        d16_pool = RR([d16[0], d16[1], d16[2]])
        junk = RR([(sb(f"junk_{i}", [128, D], BF16), Buf(f"junk_{i}")) for i in range(1)])
        ident = sb("identb", [128, 128], BF16); b_ident = Buf("ident")
        bias = sb("biasb", [128, 2048], BF16); b_bias = Buf("bias")
        tril = sb("trilb", [128, 128], F32); b_tril = Buf("tril")
        flag = sb("flagb", [128, 1], F32); b_flag = Buf("flag")
        mhalf = sb("mhalf", [128, 1], F32); b_mhalf = Buf("mhalf")
        colg = sb("colgb", [128, 32], F32); b_colg = Buf("colg")
        rowv = sb("rowvb", [128, 3072], F32); b_rowv = Buf("rowv")
        bsT = sb("bsTb", [128, 8], F32); b_bsT = Buf("bsT")
        esink = sb("esink", [128, 8], F32); b_esink = Buf("esink")
        wsT = sb("wsT", [128, 8, 128], BF16); b_wsT = Buf("wsT")

        psd = [st.enter_context(nc.psum_tensor(f"ps{i}", [128, 1024], F32)) for i in range(4)]
        b_bank = [Buf(f"bank{i}", True) for i in range(8)]
        bank_ptr = [0]

        def bank1():
            i = min(range(8), key=lambda k: b_bank[k].last)
            b_bank[i].last = P.gidx + 0.5
            return psd[i // 2][:, (i % 2) * 512:(i % 2) * 512 + 512], [b_bank[i]]

        def bank2():
            d = min(range(4), key=lambda k: max(b_bank[2 * k].last, b_bank[2 * k + 1].last))
            b_bank[2 * d].last = b_bank[2 * d + 1].last = P.gidx + 0.5
            return psd[d][:, :], [b_bank[2 * d], b_bank[2 * d + 1]]

        b_wsc = {}

        conv_pending = []

        def conv_pump(n):
            for _ in range(min(n, len(conv_pending))):
                conv_pending.pop(0)()

        def conv_layer(l):
            def cv(batch, piece, dst_view, src_ap):
                def thunk():
                    key = (l, batch)
                    if key not in b_wsc:
                        b_wsc[key] = Buf(f"wsc{l}{batch}")
                    P.dma("pool", lambda e, d=dst_view, s=src_ap: e.dma_start(out=d, in_=s),
                          f"cv{l}{batch}", writes=[])
                    b_wsc[key].w = ("dma:" + f"cv{l}{batch}", P.dma_cnt[f"cv{l}{batch}"], "dma", -1)
                conv_pending.append(thunk)

            def colpiece(batch, piece, wd, c0, ncols=256, dst_off=0, kcn=KC):
                dst = wsc_d[l, piece, :, :].rearrange("p (k n) -> p k n", k=kcn)[:, :, dst_off:dst_off + ncols]
                src = wd[l, :, :].rearrange("(k p) n -> p k n", p=128)[:, :, c0:c0 + ncols]
                cv(batch, piece, dst, src)

            def kpiece(batch, piece, wd, c0, q):
                dst = wsc_d[l, piece, :, :].rearrange("p (k n) -> p k n", k=4)
                src = wd[l, :, :].rearrange("(k p) n -> p k n", p=128)[:, 4 * q:4 * q + 4, c0:c0 + 512]
                cv(batch, piece, dst, src)

            for pc in range(2):
                for cc in range(2):
                    c = pc * 2 + cc
                    for j in range(2):
                        colpiece("A", pc, w_in_d, 64 * (c + 4 * j), 64, cc * 128 + j * 64)
            colpiece("A", 2, w_in_d, 512)
            for i in range(4):
                kpiece("A", 3 + i, w_in_d, 768 + 512 * (i // 2), i % 2)
            for i in range(4):
                kpiece("A", 7 + i, w_out_d, 512 * (i // 2), i % 2)
            for i in range(11):
                colpiece("B", 11 + 2 * i, w_gate_d, 256 * i)
                colpiece("B", 12 + 2 * i, w_up_d, 256 * i)
            for i in range(11):
                dst = wsc_d[l, 33 + i, :, :].rearrange("p (k n) -> p k n", k=2)
                src = w_down_d[l, 256 * i:256 * i + 256, :].rearrange("(k p) n -> p k n", p=128)
                cv("B", 33 + i, dst, src)
            dst = wsc_d[l, 44, :, :].rearrange("p (k n) -> p k n", k=2)
            src = w_ple_d[l, :, :].rearrange("(k p) n -> p k n", p=128)
            cv("C", 44, dst, src)
            for i in range(4):
                kpiece("C", 45 + i, w_pg_d, 512 * (i // 2), i % 2)

        def piece_batch(piece):
            return "A" if piece < 11 else ("B" if piece < 44 else "C")

        def layer_groups(l):
            kvb = H - (L - l)
            groups = []
            g0 = (kvb // 4) * 4
            for s in range(g0, NB, 4):
                blks = [b for b in range(s, min(s + 4, NB)) if b >= kvb]
                groups.append(blks)
            return kvb, groups

        seq = []
        for l in range(L):
            kvb, groups = layer_groups(l)
            for blks in groups:
                full = [b for b in blks if b > kvb]
                if not full:
                    seq.append((l, 2))
                    continue
                seq += [(l, p) for p in range(0, 33)]
                seq += [(l, p) for p in range(33, 44)]
                seq += [(l, p) for p in range(44, 49)]

        class Ring:
            def __init__(self):
                self.load_ptr = 0
                self.use_ptr = 0
                self.released = [False] * len(seq)

            def pump(self):
                while self.load_ptr < len(seq) and (self.load_ptr < NS or self.released[self.load_ptr - NS]):
                    i = self.load_ptr
                    l, p = seq[i]
                    s = i % NS
                    P.dma("sp", lambda e, s=s, l=l, p=p: e.dma_start(out=ring[:, s, :], in_=wsc_d[l, p, :, :]),
                          f"ring{s}", reads=[b_wsc[(l, piece_batch(p))]], writes=[b_ring[s]])
                    self.load_ptr += 1

            def acquire(self, l, p):
                i = self.use_ptr
                assert seq[i] == (l, p), (seq[i], l, p)
                self.pump()
                assert self.load_ptr > i, "ring resident set exceeds NS"
                self.use_ptr += 1
                return i

            def release(self, i):
                P.force_signal("pe")
                self.released[i] = True
                self.pump()

        R = Ring()

        def rstd_from(ss_ap, b_ss, n, eps=EPS):
            t, b = stat.get()
            P.op("pool", lambda e: e.tensor_scalar(out=t[:, 0:1], in0=ss_ap, scalar1=1.0 / n, scalar2=eps,
                                                   op0=ALU.mult, op1=ALU.add), reads=[b_ss], writes=[b])
            P.op("pool", lambda e: e.tensor_tensor(out=t[:, 1:2], in0=t[:, 0:1], in1=mhalf[:, 0:1], op=ALU.pow),
                 reads=[b, b_mhalf], writes=[b])
            return t[:, 1:2], b

        def sumsq(in_ap, rd_bufs, ncols):
            t, b = stat.get()
            jt, b_jt = junk.get()
            P.op("act", lambda e: e.activation(out=jt[:, 0:ncols], in_=in_ap, func=AF.Square,
                                               accum_out=t[:, 0:1]), reads=rd_bufs, writes=[b, b_jt])
            return t[:, 0:1], b

        def transpose_to(src16, b_src, dstT, b_dst, gp, gsel):
            ps, bb = bank1()
            psb = ps.bitcast(BF16)
            for kc in range(KC):
                P.op("pe", lambda e, kc=kc: e.transpose(out=psb[:, kc * 128:(kc + 1) * 128],
                                                        in_=src16[:, kc * 128:(kc + 1) * 128], identity=ident[:]),
                     reads=(b_src if isinstance(b_src, list) else [b_src]) + [b_ident], writes=bb, signal=(kc == KC - 1))
            P.op("dve", lambda e: e.tensor_tensor(out=dstT[:, :, gp * 128:(gp + 1) * 128],
                                                  in0=psb.rearrange("p (k t) -> p k t", k=KC),
                                                  in1=colg[:, gsel:gsel + 8].unsqueeze(2).to_broadcast([128, KC, 128]),
                                                  op=ALU.mult),
                 reads=bb + [b_colg], writes=[b_dst])

        def norm_transpose_group(blist, gps, dstT, b_dsts, gsel, lag, a16pool=None):
            n = len(blist)
            rss = {}
            for step in range(n + lag):
                if step < n:
                    blk = blist[step]
                    ss, b_ss = sumsq(h[:, blk, :], [b_h[blk]], D)
                    rss[step] = rstd_from(ss, b_ss, D)
                i = step - lag
                if 0 <= i < n:
                    blk = blist[i]
                    rs, b_rs = rss[i]
                    a16, bl_a16 = (a16pool or E_pool).get()
                    if i % 2 == 0:
                        P.op("act", lambda e, a16=a16, blk=blk, rs=rs: e.activation(out=a16, in_=h[:, blk, :], func=AF.Copy, scale=rs),
                             reads=[b_h[blk], b_rs], writes=bl_a16)
                    else:
                        P.op("dve", lambda e, a16=a16, blk=blk, rs=rs: e.tensor_scalar(out=a16, in0=h[:, blk, :], scalar1=rs, scalar2=None, op0=ALU.mult),
                             reads=[b_h[blk], b_rs], writes=bl_a16)
                    transpose_to(a16, bl_a16, dstT, b_dsts[i], gps[i], gsel)

        def residual_epilogue(ps2, bb2, blk, goff):
            ss, b_ss = sumsq(ps2, bb2, D)
            rs, b_rs = rstd_from(ss, b_ss, D)
            t, b_t = scr32.get()
            P.op("dve", lambda e: e.scalar_tensor_tensor(out=t[:], in0=ps2, scalar=rs, in1=rowv[:, goff:goff + D],
                                                         op0=ALU.mult, op1=ALU.mult),
                 reads=bb2 + [b_rs, b_rowv], writes=[b_t])
            P.op("dve", lambda e: e.tensor_tensor(out=h[:, blk, :], in0=h[:, blk, :], in1=t[:], op=ALU.add),
                 reads=[b_t, b_h[blk]], writes=[b_h[blk]])

        conv_layer(0)
        conv_pump(10 ** 6)
        P.dma("sp", lambda e: e.dma_start(out=h[:, :, :], in_=x_d[:, :].rearrange("(b p) d -> p b d", p=128)),
              "xin", writes=b_h)
        P.dma("pool", lambda e: e.dma_start(out=ident[:], in_=ident_d[:, :]), "c_ident", writes=[b_ident])
        P.dma("pool", lambda e: e.dma_start(out=bias[:], in_=bias_d[:, :]), "c_bias", writes=[b_bias])
        P.dma("sp", lambda e: e.dma_start(out=tril[:], in_=tril_d[:, :]), "c_tril", writes=[b_tril])
        P.dma("sp", lambda e: e.dma_start(out=flag[:], in_=flag_d[:, :]), "c_flag", writes=[b_flag])
        P.op("pool", lambda e: e.memset(mhalf[:], -0.5), writes=[b_mhalf])

        def layer_setup(l):
            P.dma("sp", lambda e: e.dma_start(out=colg[:], in_=colg_d[l, :, :]), "l_colg", writes=[b_colg])
            P.dma("sp", lambda e: e.dma_start(out=rowv[:], in_=rowv_d[l, 0:1, :].partition_broadcast(128)),
                  "l_rowv", writes=[b_rowv])
            P.dma("sp", lambda e: e.dma_start(out=bsT[:], in_=bsT_d[l, :, :]), "l_bsT", writes=[b_bsT])
            P.dma("sp", lambda e: e.dma_start(out=esink[:], in_=sinks_d[l, 0:1, :].partition_broadcast(128)),
                  "l_sink", writes=[b_esink])
            P.op("act", lambda e: e.activation(out=esink[:], in_=esink[:], func=AF.Exp), reads=[b_esink], writes=[b_esink])
            wt, b_wt = scr32.get()
            P.dma("sp", lambda e: e.dma_start(out=wt[:].rearrange("p (h s) -> p h s", h=8),
                                              in_=ws_d[l, :, :, :].rearrange("h t s -> t h s")),
                  "l_ws", writes=[b_wt])
            wm, b_wm = scr16.get()
            P.op("dve", lambda e: e.tensor_tensor(out=wm[:].rearrange("p (h s) -> p h s", h=8),
                                                  in0=wt[:].rearrange("p (h s) -> p h s", h=8),
                                                  in1=tril[:].unsqueeze(1).to_broadcast([128, 8, 128]), op=ALU.mult),
                 reads=[b_wt, b_tril], writes=[b_wm])
            ps, bb = bank1()
            psb = ps.bitcast(BF16)
            for hh in range(8):
                P.op("pe", lambda e, hh=hh: e.transpose(out=psb[:, hh * 128:(hh + 1) * 128],
                                                        in_=wm[:, hh * 128:(hh + 1) * 128], identity=ident[:]),
                     reads=[b_wm, b_ident], writes=bb, signal=(hh == 7))
            P.op("act", lambda e: e.activation(out=wsT[:, :, :], in_=psb.rearrange("p (h t) -> p h t", h=8), func=AF.Copy),
                 reads=bb, writes=[b_wsT])

        def do_group(l, kvb, blks, last_layer, next_blks, m1_done):
            G = len(blks)
            Gt = G * 128
            full = [b for b in blks if b > kvb]
            gp_of = {b: i for i, b in enumerate(blks)}

            if not m1_done:
                norm_transpose_group(blks, [gp_of[b] for b in blks], xTa, [b_xTa[gp_of[b]] for b in blks], 0, len(blks))
            b_xa = [b_xTa[gp_of[b]] for b in blks]

            def fm_chunk(slot, coff, evac):
                ps, bb = bank1()
                for kc in range(KC):
                    P.op("pe", lambda e, kc=kc: e.matmul(ps[:, 0:Gt], lhsT=ring[:, slot, kc * 256 + coff:kc * 256 + coff + 128],
                                                         rhs=xTa[:, kc, 0:Gt], start=(kc == 0), stop=(kc == KC - 1)),
                         reads=[b_ring[slot]] + b_xa, writes=bb, signal=(kc == KC - 1))
                evac(ps, bb)

            if full:
                for pc in range(2):
                    it = R.acquire(l, pc)
                    slot = it % NS
                    for cc in range(2):
                        c = pc * 2 + cc
                        fm_chunk(slot, cc * 128,
                                 lambda ps, bb, c=c: P.op("act", lambda e: e.activation(out=qT[:, c, 0:Gt], in_=ps[:, 0:Gt], func=AF.Copy),
                                                          reads=bb, writes=[b_qT[c]]))
                    R.release(it)
            it_kv = R.acquire(l, 2)
            s_kv = it_kv % NS
            b0 = blks[0]
            fm_chunk(s_kv, 0,
                     lambda ps, bb: P.op("act", lambda e: e.activation(out=kT[:, (b0 % 8) * 128:(b0 % 8) * 128 + Gt], in_=ps[:, 0:Gt], func=AF.Copy),
                                         reads=bb, writes=[b_kT[b % 8] for b in blks]))

            def do_v(b):
                gp = gp_of[b]
                ps, bb = bank1()
                for kc in range(KC):
                    P.op("pe", lambda e, kc=kc: e.matmul(ps[:, 0:128], lhsT=xTa[:, kc, gp * 128:(gp + 1) * 128],
                                                         rhs=ring[:, s_kv, kc * 256 + 128:kc * 256 + 256],
                                                         start=(kc == 0), stop=(kc == KC - 1)),
                         reads=[b_ring[s_kv], b_xTa[gp]], writes=bb, signal=(kc == KC - 1))
                outv = vA[:, b % 8, :].rearrange("p (j e) -> p j e", j=2)[:, :, 0:64]
                onev = vA[:, b % 8, :].rearrange("p (j e) -> p j e", j=2)[:, :, 64:65]
                inv = ps[:, 0:128].rearrange("p (j e) -> p j e", j=2)
                if b == H - 1:
                    P.op("dve", lambda e: e.tensor_scalar(out=outv, in0=inv, scalar1=flag[:, 0:1], scalar2=None, op0=ALU.mult),
                         reads=bb + [b_flag], writes=[b_vA[b % 8]])
                    P.op("dve", lambda e: e.tensor_copy(out=onev, in_=flag[:, 0:1].unsqueeze(1).to_broadcast([128, 2, 1])),
                         reads=[b_flag], writes=[b_vA[b % 8]])
                else:
                    P.op("dve", lambda e: e.tensor_copy(out=outv, in_=inv), reads=bb, writes=[b_vA[b % 8]])
                    P.op("dve", lambda e: e.memset(onev, 1.0), writes=[b_vA[b % 8]])

            if not full:
                for b in blks:
                    do_v(b)
                R.release(it_kv)
                return False

            it_zu = [R.acquire(l, 3), R.acquire(l, 4)]
            it_zv = [R.acquire(l, 5), R.acquire(l, 6)]

            def tm_proj(gp, its):
                ps, bb = bank1()
                for q, it in enumerate(its):
                    s = it % NS
                    for kcl in range(4):
                        P.op("pe", lambda e, kcl=kcl, s=s, q=q: e.matmul(ps, lhsT=xTa[:, 4 * q + kcl, gp * 128:(gp + 1) * 128],
                                                                         rhs=ring[:, s, kcl * 512:(kcl + 1) * 512],
                                                                         start=(q == 0 and kcl == 0), stop=(q == 1 and kcl == 3)),
                             reads=[b_ring[s], b_xTa[gp]], writes=bb, signal=(q == 1 and kcl == 3))
                return ps, bb

            for b in blks:
                do_v(b)

            stt = {}

            def stage_A(b):
                gp = gp_of[b]
                sd = stt[b] = {}
                E = []
                for j in range(2):
                    stmp, bl_stmp = stmp_pool.get()
                    for kb, kblk in enumerate((b - 1, b)):
                        ps, bb = bank1()
                        P.op("pe", lambda e, j=j, kblk=kblk, ps=ps: e.matmul(ps, lhsT=kT[64 * j:64 * j + 64, (kblk % 8) * 128:(kblk % 8 + 1) * 128],
                                                                             rhs=qT[64 * j:64 * j + 64, :, gp * 128:(gp + 1) * 128],
                                                                             start=True, stop=True),
                             reads=[b_kT[kblk % 8]] + b_qT, writes=bb)
                        P.op("dve", lambda e, j=j, kb=kb, ps=ps, stmp=stmp: e.scalar_tensor_tensor(
                            out=stmp[:, kb * 512:(kb + 1) * 512], in0=ps, scalar=0.125,
                            in1=bias[:, kb * 1024 + j * 512:kb * 1024 + (j + 1) * 512], op0=ALU.mult, op1=ALU.add),
                            reads=bb + [b_bias], writes=bl_stmp)
                    Ej, bl_Ej = E_pool.get()
                    P.op("act", lambda e, Ej=Ej, stmp=stmp: e.activation(out=Ej, in_=stmp, func=AF.Exp), reads=bl_stmp, writes=bl_Ej)
                    E.append((Ej, bl_Ej))
                sd["E"] = E
                gx, bl_gx = gx_pool.get()
                sd["gx"] = (gx, bl_gx)
                ps_zu, bb_zu = tm_proj(gp, it_zu)
                P.op("act", lambda e: e.activation(out=gx[:, 0:512], in_=ps_zu, func=AF.Gelu_apprx_tanh), reads=bb_zu, writes=bl_gx)
                ps_zv, bb_zv = tm_proj(gp, it_zv)
                P.op("act", lambda e: e.activation(out=gx[:, 512:1024], in_=ps_zv, func=AF.Gelu_apprx_tanh), reads=bb_zv, writes=bl_gx)

            def stage_B(b):
                sd = stt[b]
                gx, bl_gx = sd["gx"]
                xv = gx[:, 512:1024]
                po = []
                for j in range(2):
                    Ej, bl_Ej = sd["E"][j]
                    ps, bb = bank1()
                    for c in range(4):
                        P.op("pe", lambda e, j=j, c=c, ps=ps, Ej=Ej: e.matmul(ps[:, c * 65:(c + 1) * 65], lhsT=Ej[:, c * 128:(c + 1) * 128],
                                                                              rhs=vA[:, (b - 1) % 8, 65 * j:65 * j + 65], start=True, stop=False),
                             reads=bl_Ej + [b_vA[(b - 1) % 8]], writes=bb, signal=False)
                        P.op("pe", lambda e, j=j, c=c, ps=ps, Ej=Ej: e.matmul(ps[:, c * 65:(c + 1) * 65], lhsT=Ej[:, 512 + c * 128:512 + (c + 1) * 128],
                                                                              rhs=vA[:, b % 8, 65 * j:65 * j + 65], start=False, stop=True),
                             reads=bl_Ej + [b_vA[b % 8]], writes=bb, signal=(c == 3))
                    po.append((ps, bb))
                st6, b_st6 = stat.get()
                P.op("dve", lambda e: e.bn_stats(out=st6[:, 0:6], in_=xv), reads=bl_gx, writes=[b_st6])
                mv, b_mv = stat.get()
                P.op("dve", lambda e: e.bn_aggr(out=mv[:, 0:2], in_=st6[:, 0:6]), reads=[b_st6], writes=[b_mv])
                rs_ln, b_rs_ln = rstd_from(mv[:, 1:2], b_mv, 1.0)
                den, b_den = stat.get()
                for j in range(2):
                    ps, bb = po[j]
                    P.op("dve", lambda e, j=j, ps=ps: e.tensor_tensor(out=den[:, 4 * j:4 * j + 4].unsqueeze(2),
                                                                      in0=ps[:, 0:260].rearrange("p (c e) -> p c e", e=65)[:, :, 64:65],
                                                                      in1=esink[:, 4 * j:4 * j + 4].unsqueeze(2), op=ALU.add),
                         reads=bb + [b_esink], writes=[b_den])
                rden, b_rden = stat.get()
                P.op("dve", lambda e: e.reciprocal(out=rden[:, 0:8], in_=den[:, 0:8]), reads=[b_den], writes=[b_rden])
                at, bl_at = at_pool.get()
                for j in range(2):
                    ps, bb = po[j]
                    P.op("dve", lambda e, j=j, ps=ps: e.tensor_tensor(out=at[:, j * 256:(j + 1) * 256].rearrange("p (c d) -> p c d", c=4),
                                                                      in0=ps[:, 0:260].rearrange("p (c e) -> p c e", e=65)[:, :, 0:64],
                                                                      in1=rden[:, 4 * j:4 * j + 4].unsqueeze(2).to_broadcast([128, 4, 64]),
                                                                      op=ALU.mult),
                         reads=bb + [b_rden], writes=bl_at)
                ss_a, b_ss_a = sumsq(at[:, 0:512], bl_at, 512)
                rs_a, b_rs_a = rstd_from(ss_a, b_ss_a, 512.0)
                sd["at"] = (at, bl_at, rs_a, b_rs_a)
                P.op("dve", lambda e: e.tensor_scalar(out=xv, in0=xv, scalar1=mv[:, 0:1], scalar2=rs_ln,
                                                      op0=ALU.subtract, op1=ALU.mult), reads=bl_gx + [b_mv, b_rs_ln], writes=bl_gx)
                P.op("dve", lambda e: e.tensor_tensor(out=xv, in0=xv, in1=rowv[:, 2048:2560], op=ALU.mult),
                     reads=bl_gx + [b_rowv], writes=bl_gx)
                vh, bl_vh = vh_pool.get()
                P.op("dve", lambda e: e.tensor_tensor(out=vh[:, 0:512], in0=xv, in1=rowv[:, 2560:3072], op=ALU.add),
                     reads=bl_gx + [b_rowv], writes=bl_vh)
                sd["vh"] = (vh, bl_vh)

            def stage_C(b):
                gp = gp_of[b]
                sd = stt.pop(b)
                gx, bl_gx = sd["gx"]
                gu = gx[:, 0:512]
                xv = gx[:, 512:1024]
                vh, bl_vh = sd["vh"]
                at, bl_at, rs_a, b_rs_a = sd["at"]
                heads, b_heads = headsp.get()
                ps_sp, bb_sp = bank1()
                for hh in range(8):
                    P.op("pe", lambda e, hh=hh: e.matmul(ps_sp[:, hh * 64:(hh + 1) * 64], lhsT=wsT[:, hh, :], rhs=vh[:, hh * 64:(hh + 1) * 64],
                                                         start=True, stop=True),
                         reads=[b_wsT] + bl_vh, writes=bb_sp, signal=(hh == 7))
                P.op("act", lambda e: e.activation(out=heads[:, 0:512], in_=at[:, 0:512], func=AF.Copy, scale=rs_a),
                     reads=bl_at + [b_rs_a], writes=[b_heads])
                P.op("dve", lambda e: e.tensor_tensor(out=xv.rearrange("p (h c) -> p h c", h=8),
                                                      in0=ps_sp.rearrange("p (h c) -> p h c", h=8),
                                                      in1=bsT[:, 0:8].unsqueeze(2).to_broadcast([128, 8, 64]), op=ALU.add),
                     reads=bb_sp + [b_bsT] + bl_gx, writes=bl_gx)
                P.op("dve", lambda e: e.tensor_tensor(out=gu, in0=gu, in1=xv, op=ALU.mult),
                     reads=bl_gx, writes=bl_gx)
                ss_g, b_ss_g = sumsq(gu, bl_gx, 512)
                rs_g, b_rs_g = rstd_from(ss_g, b_ss_g, 512.0)
                P.op("act", lambda e: e.activation(out=heads[:, 512:1024], in_=gu, func=AF.Copy, scale=rs_g),
                     reads=bl_gx + [b_rs_g], writes=[b_heads])
                transpose_to(heads, b_heads, xTb, b_xTb[gp], gp, 8)

            nf = len(full)
            for step in range(nf + 2):
                if step < nf:
                    stage_A(full[step])
                if 0 <= step - 1 < nf:
                    stage_B(full[step - 1])
                if 0 <= step - 2 < nf:
                    stage_C(full[step - 2])
            for it in [it_kv] + it_zu + it_zv:
                R.release(it)

            it_wo = [R.acquire(l, 7 + i) for i in range(4)]

            def m6_block(b):
                gp = gp_of[b]
                ps2, bb2 = bank2()
                for i, it in enumerate(it_wo):
                    s = it % NS
                    c, q = i // 2, i % 2
                    for kcl in range(4):
                        P.op("pe", lambda e, kcl=kcl, s=s, c=c, q=q: e.matmul(ps2[:, c * 512:(c + 1) * 512], lhsT=xTb[:, 4 * q + kcl, gp * 128:(gp + 1) * 128],
                                                                              rhs=ring[:, s, kcl * 512:(kcl + 1) * 512],
                                                                              start=(q == 0 and kcl == 0), stop=(q == 1 and kcl == 3)),
                             reads=[b_ring[s], b_xTb[gp]], writes=[bb2[c]], signal=(q == 1 and kcl == 3))
                residual_epilogue(ps2, bb2, b, 0)

            for b in full:
                m6_block(b)
            for it in it_wo:
                R.release(it)

            norm_transpose_group(full, [gp_of[b] for b in full], xTa, [b_xTa[gp_of[b]] for b in full], 16, 2)
            c0 = gp_of[full[0]] * 128
            Ft = len(full) * 128
            b_xf = [b_xTa[gp_of[b]] for b in full]

            def f2_chunk(ci, sub, it_g, it_u):
                pss = []
                for it in (it_g, it_u):
                    s = it % NS
                    ps, bb = bank1()
                    for kc in range(KC):
                        P.op("pe", lambda e, kc=kc, s=s, ps=ps: e.matmul(ps[:, 0:Ft], lhsT=ring[:, s, kc * 256 + sub * 128:kc * 256 + sub * 128 + 128],
                                                                         rhs=xTa[:, kc, c0:c0 + Ft], start=(kc == 0), stop=(kc == KC - 1)),
                             reads=[b_ring[s]] + b_xf, writes=bb, signal=(kc == KC - 1))
                    pss.append((ps, bb))
                sl, b_sl = scr32.get()
                ps_g, bb_g = pss[0]
                ps_u, bb_u = pss[1]
                P.op("act", lambda e: e.activation(out=sl[:, 0:Ft], in_=ps_g[:, 0:Ft], func=AF.Silu), reads=bb_g, writes=[b_sl])
                P.op("dve", lambda e: e.tensor_tensor(out=actT[:, ci, c0:c0 + Ft], in0=ps_u[:, 0:Ft], in1=sl[:, 0:Ft], op=ALU.mult),
                     reads=bb_u + [b_sl], writes=[b_actT[ci]])

            for i in range(11):
                it_g = R.acquire(l, 11 + 2 * i)
                it_u = R.acquire(l, 12 + 2 * i)
                for sub in range(2):
                    f2_chunk(2 * i + sub, sub, it_g, it_u)
                conv_pump(2)
                R.release(it_g)
                R.release(it_u)

            hoisted = False
            if next_blks is not None:
                norm_transpose_group(next_blks, list(range(len(next_blks))), xTa, [b_xTa[i] for i in range(len(next_blks))], 0,
                                     len(next_blks), d16_pool)
                hoisted = True

            acc = [bank2() for _ in full]
            for i in range(11):
                it = R.acquire(l, 33 + i)
                s = it % NS
                for bi, b in enumerate(full):
                    gp = gp_of[b]
                    ps2, bb2 = acc[bi]
                    for kcl in range(2):
                        for half in range(2):
                            P.op("pe", lambda e, kcl=kcl, half=half, s=s, ps2=ps2, gp=gp, i=i: e.matmul(
                                ps2[:, half * 512:(half + 1) * 512], lhsT=actT[:, 2 * i + kcl, gp * 128:(gp + 1) * 128],
                                rhs=ring[:, s, kcl * 1024 + half * 512:kcl * 1024 + (half + 1) * 512],
                                start=(i == 0 and kcl == 0), stop=(i == 10 and kcl == 1)),
                                reads=[b_ring[s], b_actT[2 * i + kcl]], writes=[bb2[half]],
                                signal=(i == 10 and kcl == 1 and half == 1))
                R.release(it)
            for bi, b in enumerate(full):
                ps2, bb2 = acc[bi]
                residual_epilogue(ps2, bb2, b, 1024)

            pti = do_group.pt_i % 2
            do_group.pt_i += 1
            pt, b_pt = pTt[pti], b_pTt[pti]
            t0 = full[0] * 128
            P.dma("pool", lambda e: e.dma_start(out=pt[:, :, 0:Ft], in_=pT_d[l, :, :, t0:t0 + Ft].rearrange("k p t -> p k t")),
                  f"pT{pti}", writes=b_pt)
            norm_transpose_group(full, [gp_of[b] for b in full], xTb, [b_xTb[gp_of[b]] for b in full], 24, 2)
            it_ple = R.acquire(l, 44)
            s_ple = it_ple % NS
            it_pg = [R.acquire(l, 45 + i) for i in range(4)]

            def p_block(bi, b):
                gp = gp_of[b]
                psg, bbg = bank2()
                for i, it in enumerate(it_pg):
                    s = it % NS
                    c, q = i // 2, i % 2
                    for kcl in range(4):
                        P.op("pe", lambda e, kcl=kcl, s=s, c=c, q=q: e.matmul(psg[:, c * 512:(c + 1) * 512], lhsT=xTb[:, 4 * q + kcl, gp * 128:(gp + 1) * 128],
                                                                              rhs=ring[:, s, kcl * 512:(kcl + 1) * 512],
                                                                              start=(q == 0 and kcl == 0), stop=(q == 1 and kcl == 3)),
                             reads=[b_ring[s], b_xTb[gp]], writes=[bbg[c]], signal=(q == 1 and kcl == 3))
                psp, bbp = bank2()
                for half in range(2):
                    for kc in range(2):
                        P.op("pe", lambda e, kc=kc, half=half: e.matmul(psp[:, half * 512:(half + 1) * 512], lhsT=pt[:, kc, bi * 128:(bi + 1) * 128],
                                                                        rhs=ring[:, s_ple, kc * 1024 + half * 512:kc * 1024 + (half + 1) * 512],
                                                                        start=(kc == 0), stop=(kc == 1)),
                             reads=[b_ring[s_ple]] + b_pt, writes=[bbp[half]], signal=(kc == 1))
                sg, b_sg = scr32.get()
                P.op("act", lambda e: e.activation(out=sg[:], in_=psg, func=AF.Sigmoid), reads=bbg, writes=[b_sg])
                P.op("dve", lambda e: e.tensor_tensor(out=sg[:], in0=psp, in1=sg[:], op=ALU.mult), reads=bbp + [b_sg], writes=[b_sg])
                P.op("dve", lambda e: e.tensor_tensor(out=h[:, b, :], in0=h[:, b, :], in1=sg[:], op=ALU.add),
                     reads=[b_sg, b_h[b]], writes=[b_h[b]])
                if last_layer and b >= H:
                    P.dma("sp", lambda e: e.dma_start(out=y_d[(b - H) * 128:(b - H + 1) * 128, :], in_=h[:, b, :]),
                          "yout", reads=[b_h[b]])

            for bi, b in enumerate(full):
                p_block(bi, b)
            R.release(it_ple)
            for it in it_pg:
                R.release(it)
            return hoisted

        do_group.pt_i = 0

        for l in range(L):
            layer_setup(l)
            kvb, groups = layer_groups(l)
            if l + 1 < L:
                conv_layer(l + 1)
            m1_done = False
            for gi, blks in enumerate(groups):
                nxt = groups[gi + 1] if gi + 1 < len(groups) else None
                m1_done = do_group(l, kvb, blks, l == L - 1, nxt, m1_done)
                if gi >= len(groups) - 2:
                    conv_pump(10 ** 6)
        P.wait_all("sp", b_h)
        P.emit()
    return nc


_PROG_CACHE = {}


def _get_prog(L, H):
    if (L, H) not in _PROG_CACHE:
        _PROG_CACHE[(L, H)] = build_program(L, H)
    return _PROG_CACHE[(L, H)]


def _constants():
    ident = np.eye(128, dtype=np.float32)
    s = np.arange(128)[:, None].astype(np.float32)
    t = np.arange(128)[None, :].astype(np.float32)
    slopes = np.exp2(-8.0 * (np.arange(8, dtype=np.float32) + 1.0) / 8.0).astype(np.float32)
    bias = np.zeros((128, 2, 8, 128), np.float32)
    for hh in range(8):
        dprev = t + 128.0 - s
        bias[:, 0, hh, :] = np.where(dprev < 128, -slopes[hh] * dprev, -30000.0)
        dcur = t - s
        bias[:, 1, hh, :] = np.where(dcur >= 0, -slopes[hh] * dcur, -30000.0)
    tril = (np.arange(128)[None, :] <= np.arange(128)[:, None]).astype(np.float32)
    return ident, bias.reshape(128, 2048), tril


def _run(hfull, layers, H, inp):
    L = len(layers)
    NB = H + OWN
    nc = _get_prog(L, H)
    ident, bias, tril = _constants()
    sl = slice(layers[0], layers[-1] + 1)
    c32 = lambda a: np.ascontiguousarray(a, dtype=np.float32)
    colg = np.stack([
        np.concatenate([
            inp["ln_mix_pre"][l].reshape(8, 128).T,
            np.concatenate([inp["g_attn_out"][l], inp["g_gm_out"][l]]).reshape(8, 128).T,
            inp["ln_ffn_pre"][l].reshape(8, 128).T,
            inp["ln_ple_gate"][l].reshape(8, 128).T], axis=1) for l in layers])
    rowv = np.stack([np.concatenate([inp["ln_mix_post"][l], inp["ln_ffn_post"][l], inp["gm_ln_g"][l], inp["gm_ln_b"][l]])[None, :]
                     for l in layers])
    bsT = np.stack([inp["gm_bs"][l].T for l in layers])
    sinks = np.stack([inp["attn_sinks"][l][None, :] for l in layers])
    shared = {
        "w_in": c32(inp["w_in"][sl]), "w_out": c32(inp["w_out"][sl]), "w_gate": c32(inp["w_ffn_gate"][sl]),
        "w_up": c32(inp["w_ffn_up"][sl]), "w_down": c32(inp["w_ffn_down"][sl]), "w_ple": c32(inp["w_ple"][sl]),
        "w_pg": c32(inp["w_ple_gate"][sl]), "gm_ws": c32(inp["gm_ws"][sl]),
        "colg": c32(colg), "rowv": c32(rowv), "bsT": c32(bsT), "sinks": c32(sinks),
        "ident": ident, "bias": bias, "tril": tril,
    }
    in_maps = []
    for c in range(NCORES):
        bt, j = c // 4, c % 4
        t0 = j * 2048 - H * 128
        xs = np.zeros((NB * 128, D), np.float32)
        ps = np.zeros((L, 2, 128, NB * 128), np.float32)
        lo = max(t0, 0)
        xs[lo - t0:] = hfull[bt, lo:(j + 1) * 2048]
        pp = inp["p"][sl, bt, lo:(j + 1) * 2048, :]
        ps[:, :, :, lo - t0:] = pp.transpose(0, 2, 1).reshape(L, 2, 128, -1)
        m = dict(shared)
        m["x"] = xs
        m["pT"] = ps
        m["flag"] = np.full((128, 1), 0.0 if j == 0 else 1.0, np.float32)
        in_maps.append(m)
    res = run_bass_kernel_spmd(nc, in_maps, core_ids=list(range(NCORES)))
    out = np.empty((2, 8192, D), np.float32)
    for c in range(NCORES):
        bt, j = c // 4, c % 4
        out[bt, j * 2048:(j + 1) * 2048] = res.results[c]["y"]
    return out


def kernel(**inputs):
    inp = {k: np.asarray(v) for k, v in inputs.items()}
    hcur = np.asarray(inp["x"], dtype=np.float32)
    if FUSED:
        return _run(hcur, [0, 1, 2, 3], 4, inp)
    for l in range(4):
        hcur = _run(hcur, [l], 1, inp)
    return hcur
```

```python
import numpy as np
import concourse.bass as bass
import concourse.mybir as mybir
from concourse.bass_utils import run_bass_kernel_spmd
from contextlib import ExitStack

F32 = mybir.dt.float32
BF16 = mybir.dt.bfloat16
AF = mybir.ActivationFunctionType
ALU = mybir.AluOpType
AX = mybir.AxisListType

D = 1024
KC = 8
DFF = 2816
NFC = 22
EPS = 1e-6
NCORES = 8
OWN = 16
FUSED = True
NS = 9
NP = 49
NHEADS = 1


class Buf:
    __slots__ = ("name", "w", "r", "psum", "last")

    def __init__(self, name, psum=False):
        self.name = name
        self.psum = psum
        self.last = -1
        self.w = None
        self.r = []


class Prog:
    ENGS = ("pe", "act", "dve", "pool", "sp")

    def __init__(self, nc, stack):
        self.nc = nc
        self.stack = stack
        self.streams = {e: [] for e in self.ENGS}
        self.sems = {}
        for e in self.ENGS:
            self.sems[e] = stack.enter_context(nc.semaphore("sem_" + e))
        self.cnt = {e: 0 for e in self.ENGS}
        self.idx = {e: 0 for e in self.ENGS}
        self.unsig = {e: False for e in self.ENGS}
        self.known = {e: {} for e in self.ENGS}
        self.dma_sems = {}
        self.dma_cnt = {}
        self.gidx = 0

    def dma_sem(self, name):
        if name not in self.dma_sems:
            self.dma_sems[name] = self.stack.enter_context(self.nc.semaphore("dsem_" + name))
            self.dma_cnt[name] = 0
            self.sems["dma:" + name] = self.dma_sems[name]
        return name

    def _need(self, eng, dep):
        key, val, deng, didx = dep
        if deng == eng:
            if eng in ("pe", "sp"):
                return
        if self.known[eng].get(key, 0) >= val:
            return
        self.known[eng][key] = val
        self.streams[eng].append(("wait", key, val))

    def _deps(self, eng, reads, writes):
        for b in reads:
            if b.w is not None:
                self._need(eng, b.w)
            if b.psum:
                for r in b.r:
                    if r[2] != eng:
                        self._need(eng, r)
        for b in writes:
            if b.w is not None:
                self._need(eng, b.w)
            for r in b.r:
                self._need(eng, r)

    def op(self, eng, fn, reads=(), writes=(), signal=True):
        self._deps(eng, reads, writes)
        if signal:
            self.cnt[eng] += 1
            self.unsig[eng] = False
        else:
            self.unsig[eng] = True
        tick = self.cnt[eng] if signal else self.cnt[eng] + 1
        me = (eng, tick, eng, self.idx[eng])
        self.idx[eng] += 1
        self.streams[eng].append(("op", fn, signal))
        self.gidx += 1
        for b in reads:
            b.last = self.gidx
            if not b.r or b.r[-1][:2] != me[:2]:
                b.r.append(me)
        for b in writes:
            b.last = self.gidx
            b.w = me
            b.r = []

    def dma(self, eng, fn, semname, reads=(), writes=()):
        self.dma_sem(semname)
        self._deps(eng, reads, writes)
        self.dma_cnt[semname] += 16
        key = "dma:" + semname
        me = (key, self.dma_cnt[semname], "dma", -1)
        self.idx[eng] += 1
        self.streams[eng].append(("dma", fn, semname))
        for b in reads:
            b.r.append(me)
        for b in writes:
            b.w = me
            b.r = []

    def force_signal(self, eng):
        if not self.unsig[eng]:
            return
        st = self.streams[eng]
        for k in range(len(st) - 1, -1, -1):
            if st[k][0] == "op":
                assert not st[k][2]
                st[k] = ("op", st[k][1], True)
                break
        self.cnt[eng] += 1
        self.unsig[eng] = False

    def wait_all(self, eng, bufs):
        best = {}
        for b in bufs:
            for dep in ([b.w] if b.w is not None else []) + list(b.r):
                if dep[0] not in best or best[dep[0]][1] < dep[1]:
                    best[dep[0]] = dep
        for dep in best.values():
            self._need(eng, dep)

    def emit(self):
        for e in ("pe", "act", "dve", "pool"):
            assert not self.unsig[e], f"engine {e} ends with unsignalled instruction"
        nc = self.nc
        with nc.Block() as block:
            def run(engname):
                def body(eng):
                    mysem = self.sems[engname]
                    for ent in self.streams[engname]:
                        if ent[0] == "wait":
                            eng.wait_ge(self.sems[ent[1]], ent[2])
                        elif ent[0] == "op":
                            ins = ent[1](eng)
                            if ent[2]:
                                ins.then_inc(mysem, 1)
                        else:
                            ins = ent[1](eng)
                            ins.then_inc(self.dma_sems[ent[2]], 16)
                return body
            block.tensor(run("pe"))
            block.scalar(run("act"))
            block.vector(run("dve"))
            block.gpsimd(run("pool"))
            block.sync(run("sp"))


class RR:
    def __init__(self, items):
        self.items = items
        self.i = 0

    def get(self):
        it = self.items[self.i % len(self.items)]
        self.i += 1
        return it


def build_program(L, H, OWN=OWN):
    NB = H + OWN
    NT = NB * 128
    nc = bass.Bass("TRN2", target_bir_lowering=False)
    dt = lambda name, shape, dtype=F32, kind="ExternalInput": nc.dram_tensor(name, shape, dtype, kind=kind)
    x_d = dt("x", [NT, D])
    pT_d = dt("pT", [L, 2, 128, NT])
    w_in_d = dt("w_in", [L, D, 1792])
    w_out_d = dt("w_out", [L, D, D])
    w_gate_d = dt("w_gate", [L, D, DFF])
    w_up_d = dt("w_up", [L, D, DFF])
    w_down_d = dt("w_down", [L, DFF, D])
    w_ple_d = dt("w_ple", [L, 256, D])
    w_pg_d = dt("w_pg", [L, D, D])
    ws_d = dt("gm_ws", [L, 8, 128, 128])
    colg_d = dt("colg", [L, 128, 32])
    rowv_d = dt("rowv", [L, 1, 3072])
    bsT_d = dt("bsT", [L, 128, 8])
    sinks_d = dt("sinks", [L, 1, 8])
    ident_d = dt("ident", [128, 128])
    bias_d = dt("bias", [128, 2048])
    tril_d = dt("tril", [128, 128])
    flag_d = dt("flag", [128, 1])
    y_d = dt("y", [OWN * 128, D], F32, "ExternalOutput")
    wsc_d = dt("wsc", [L, NP, 128, 2048], BF16, "Internal")

    with ExitStack() as st:
        P = Prog(nc, st)
        sb = lambda name, shape, dtype: st.enter_context(nc.sbuf_tensor(name, shape, dtype))

        h = sb("h", [128, NB, D], F32)
        b_h = [Buf(f"h{i}") for i in range(NB)]
        ring = sb("ring", [128, NS, 2048], BF16)
        b_ring = [Buf(f"ring{i}") for i in range(NS)]
        xTa = sb("xTa", [128, KC, 512], BF16)
        b_xTa = [Buf(f"xTa{i}") for i in range(4)]
        xTb = sb("xTb", [128, KC, 512], BF16)
        b_xTb = [Buf(f"xTb{i}") for i in range(4)]
        qT = sb("qT", [128, 4, 512], BF16)
        b_qT = [Buf(f"qT{i}") for i in range(4)]
        kT = sb("kT", [128, 8 * 128], BF16)
        b_kT = [Buf(f"kT{i}") for i in range(8)]
        vA = sb("vA", [128, 8, 130], BF16)
        b_vA = [Buf(f"vA{i}") for i in range(8)]
        actT = sb("actT", [128, NFC, 512], BF16)
        b_actT = [Buf(f"actT{i}") for i in range(NFC)]
        pTt = [qT[:, 2 * i:2 * i + 2, :] for i in range(2)]
        b_pTt = [[b_qT[2 * i], b_qT[2 * i + 1]] for i in range(2)]
        scr32 = RR([(sb(f"s32_{i}", [128, D], F32), Buf(f"s32_{i}")) for i in range(4)])
        scr16 = RR([(sb(f"s16_{i}", [128, D], BF16), Buf(f"s16_{i}")) for i in range(3)])
        headsp = RR([(sb(f"heads_{i}", [128, D], BF16), Buf(f"heads_{i}")) for i in range(NHEADS)])
        stat = RR([(sb(f"st_{i}", [128, 8], F32), Buf(f"st_{i}")) for i in range(24)])
        def _a32(i):
            return (actT[:, 4 * i:4 * i + 4, :].rearrange("p a b -> p (a b)").bitcast(F32), b_actT[4 * i:4 * i + 4])

        def _a16(i):
            return (actT[:, 16 + 2 * i:18 + 2 * i, :].rearrange("p a b -> p (a b)"), b_actT[16 + 2 * i:18 + 2 * i])

        d32 = [(t_[:], [b_]) for (t_, b_) in scr32.items]
        d16 = [(t_[:], [b_]) for (t_, b_) in scr16.items]
        gx_pool = RR([_a32(0), _a32(1), _a32(2)])
        stmp_pool = RR([_a32(3), d32[0], d32[1]])
        at_pool = RR([d32[2], d32[3]])
        E_pool = RR([_a16(0), _a16(1), _a16(2), d16[0]])
        vh_pool = RR([d16[1], d16[2]])
        d16_pool = RR([d16[0], d16[1], d16[2]])
        junk = RR([(sb(f"junk_{i}", [128, D], BF16), Buf(f"junk_{i}")) for i in range(1)])
        ident = sb("identb", [128, 128], BF16); b_ident = Buf("ident")
        bias = sb("biasb", [128, 2048], BF16); b_bias = Buf("bias")
        tril = sb("trilb", [128, 128], F32); b_tril = Buf("tril")
        flag = sb("flagb", [128, 1], F32); b_flag = Buf("flag")
        mhalf = sb("mhalf", [128, 1], F32); b_mhalf = Buf("mhalf")
        colg = sb("colgb", [128, 32], F32); b_colg = Buf("colg")
        rowv = sb("rowvb", [128, 3072], F32); b_rowv = Buf("rowv")
        bsT = sb("bsTb", [128, 8], F32); b_bsT = Buf("bsT")
        esink = sb("esink", [128, 8], F32); b_esink = Buf("esink")
        wsT = sb("wsT", [128, 8, 128], BF16); b_wsT = Buf("wsT")

        psd = [st.enter_context(nc.psum_tensor(f"ps{i}", [128, 1024], F32)) for i in range(4)]
        b_bank = [Buf(f"bank{i}", True) for i in range(8)]
        bank_ptr = [0]

        def bank1():
            i = min(range(8), key=lambda k: b_bank[k].last)
            b_bank[i].last = P.gidx + 0.5
            return psd[i // 2][:, (i % 2) * 512:(i % 2) * 512 + 512], [b_bank[i]]

        def bank2():
            d = min(range(4), key=lambda k: max(b_bank[2 * k].last, b_bank[2 * k + 1].last))
            b_bank[2 * d].last = b_bank[2 * d + 1].last = P.gidx + 0.5
            return psd[d][:, :], [b_bank[2 * d], b_bank[2 * d + 1]]

        b_wsc = {}

        conv_pending = []

        def conv_pump(n):
            for _ in range(min(n, len(conv_pending))):
                conv_pending.pop(0)()

        def conv_layer(l):
            def cv(batch, piece, dst_view, src_ap):
                def thunk():
                    key = (l, batch)
                    if key not in b_wsc:
                        b_wsc[key] = Buf(f"wsc{l}{batch}")
                    P.dma("pool", lambda e, d=dst_view, s=src_ap: e.dma_start(out=d, in_=s),
                          f"cv{l}{batch}", writes=[])
                    b_wsc[key].w = ("dma:" + f"cv{l}{batch}", P.dma_cnt[f"cv{l}{batch}"], "dma", -1)
                conv_pending.append(thunk)

            def colpiece(batch, piece, wd, c0, ncols=256, dst_off=0, kcn=KC):
                dst = wsc_d[l, piece, :, :].rearrange("p (k n) -> p k n", k=kcn)[:, :, dst_off:dst_off + ncols]
                src = wd[l, :, :].rearrange("(k p) n -> p k n", p=128)[:, :, c0:c0 + ncols]
                cv(batch, piece, dst, src)

            def kpiece(batch, piece, wd, c0, q):
                dst = wsc_d[l, piece, :, :].rearrange("p (k n) -> p k n", k=4)
                src = wd[l, :, :].rearrange("(k p) n -> p k n", p=128)[:, 4 * q:4 * q + 4, c0:c0 + 512]
                cv(batch, piece, dst, src)

            for pc in range(2):
                for cc in range(2):
                    c = pc * 2 + cc
                    for j in range(2):
                        colpiece("A", pc, w_in_d, 64 * (c + 4 * j), 64, cc * 128 + j * 64)
            colpiece("A", 2, w_in_d, 512)
            for i in range(4):
                kpiece("A", 3 + i, w_in_d, 768 + 512 * (i // 2), i % 2)
            for i in range(4):
                kpiece("A", 7 + i, w_out_d, 512 * (i // 2), i % 2)
            for i in range(11):
                colpiece("B", 11 + 2 * i, w_gate_d, 256 * i)
                colpiece("B", 12 + 2 * i, w_up_d, 256 * i)
            for i in range(11):
                dst = wsc_d[l, 33 + i, :, :].rearrange("p (k n) -> p k n", k=2)
                src = w_down_d[l, 256 * i:256 * i + 256, :].rearrange("(k p) n -> p k n", p=128)
                cv("B", 33 + i, dst, src)
            dst = wsc_d[l, 44, :, :].rearrange("p (k n) -> p k n", k=2)
            src = w_ple_d[l, :, :].rearrange("(k p) n -> p k n", p=128)
            cv("C", 44, dst, src)
            for i in range(4):
                kpiece("C", 45 + i, w_pg_d, 512 * (i // 2), i % 2)

        def piece_batch(piece):
            return "A" if piece < 11 else ("B" if piece < 44 else "C")

        def layer_groups(l):
            kvb = H - (L - l)
            groups = []
            g0 = (kvb // 4) * 4
            for s in range(g0, NB, 4):
                blks = [b for b in range(s, min(s + 4, NB)) if b >= kvb]
                groups.append(blks)
            return kvb, groups

        seq = []
        for l in range(L):
            kvb, groups = layer_groups(l)
            nfull = 0
            for blks in groups:
                full = [b for b in blks if b > kvb]
                if not full:
                    seq.append((l, 2))
                    continue
                nfull += 1
                seq += [(l, p) for p in range(0, 11)]
            for _ in range(nfull):
                seq += [(l, p) for p in range(11, 44)]
            for _ in range(nfull):
                seq += [(l, p) for p in range(44, 49)]

        class Ring:
            def __init__(self):
                self.load_ptr = 0
                self.use_ptr = 0
                self.released = [False] * len(seq)

            def pump(self):
                while self.load_ptr < len(seq) and (self.load_ptr < NS or self.released[self.load_ptr - NS]):
                    i = self.load_ptr
                    l, p = seq[i]
                    s = i % NS
                    P.dma("sp", lambda e, s=s, l=l, p=p: e.dma_start(out=ring[:, s, :], in_=wsc_d[l, p, :, :]),
                          f"ring{s}", reads=[b_wsc[(l, piece_batch(p))]], writes=[b_ring[s]])
                    self.load_ptr += 1

            def acquire(self, l, p):
                i = self.use_ptr
                assert seq[i] == (l, p), (seq[i], l, p)
                self.pump()
                assert self.load_ptr > i, "ring resident set exceeds NS"
                self.use_ptr += 1
                return i

            def release(self, i):
                P.force_signal("pe")
                self.released[i] = True
                self.pump()

        R = Ring()

        def rstd_from(ss_ap, b_ss, n, eps=EPS):
            t, b = stat.get()
            P.op("pool", lambda e: e.tensor_scalar(out=t[:, 0:1], in0=ss_ap, scalar1=1.0 / n, scalar2=eps,
                                                   op0=ALU.mult, op1=ALU.add), reads=[b_ss], writes=[b])
            P.op("pool", lambda e: e.tensor_tensor(out=t[:, 1:2], in0=t[:, 0:1], in1=mhalf[:, 0:1], op=ALU.pow),
                 reads=[b, b_mhalf], writes=[b])
            return t[:, 1:2], b

        def sumsq(in_ap, rd_bufs, ncols):
            t, b = stat.get()
            jt, b_jt = junk.get()
            P.op("act", lambda e: e.activation(out=jt[:, 0:ncols], in_=in_ap, func=AF.Square,
                                               accum_out=t[:, 0:1]), reads=rd_bufs, writes=[b, b_jt])
            return t[:, 0:1], b

        def transpose_to(src16, b_src, dstT, b_dst, gp, gsel):
            ps, bb = bank1()
            psb = ps.bitcast(BF16)
            for kc in range(KC):
                P.op("pe", lambda e, kc=kc: e.transpose(out=psb[:, kc * 128:(kc + 1) * 128],
                                                        in_=src16[:, kc * 128:(kc + 1) * 128], identity=ident[:]),
                     reads=(b_src if isinstance(b_src, list) else [b_src]) + [b_ident], writes=bb, signal=(kc == KC - 1))
            P.op("dve", lambda e: e.tensor_tensor(out=dstT[:, :, gp * 128:(gp + 1) * 128],
                                                  in0=psb.rearrange("p (k t) -> p k t", k=KC),
                                                  in1=colg[:, gsel:gsel + 8].unsqueeze(2).to_broadcast([128, KC, 128]),
                                                  op=ALU.mult),
                 reads=bb + [b_colg], writes=[b_dst])

        def norm_transpose_group(blist, gps, dstT, b_dsts, gsel, lag, a16pool=None):
            n = len(blist)
            rss = {}
            for step in range(n + lag):
                if step < n:
                    blk = blist[step]
                    ss, b_ss = sumsq(h[:, blk, :], [b_h[blk]], D)
                    rss[step] = rstd_from(ss, b_ss, D)
                i = step - lag
                if 0 <= i < n:
                    blk = blist[i]
                    rs, b_rs = rss[i]
                    a16, bl_a16 = (a16pool or E_pool).get()
                    if i % 2 == 0:
                        P.op("act", lambda e, a16=a16, blk=blk, rs=rs: e.activation(out=a16, in_=h[:, blk, :], func=AF.Copy, scale=rs),
                             reads=[b_h[blk], b_rs], writes=bl_a16)
                    else:
                        P.op("dve", lambda e, a16=a16, blk=blk, rs=rs: e.tensor_scalar(out=a16, in0=h[:, blk, :], scalar1=rs, scalar2=None, op0=ALU.mult),
                             reads=[b_h[blk], b_rs], writes=bl_a16)
                    transpose_to(a16, bl_a16, dstT, b_dsts[i], gps[i], gsel)

        def residual_epilogue(ps2, bb2, blk, goff):
            ss, b_ss = sumsq(ps2, bb2, D)
            rs, b_rs = rstd_from(ss, b_ss, D)
            t, b_t = scr32.get()
            P.op("dve", lambda e: e.scalar_tensor_tensor(out=t[:], in0=ps2, scalar=rs, in1=rowv[:, goff:goff + D],
                                                         op0=ALU.mult, op1=ALU.mult),
                 reads=bb2 + [b_rs, b_rowv], writes=[b_t])
            P.op("dve", lambda e: e.tensor_tensor(out=h[:, blk, :], in0=h[:, blk, :], in1=t[:], op=ALU.add),
                 reads=[b_t, b_h[blk]], writes=[b_h[blk]])

        conv_layer(0)
        conv_pump(10 ** 6)
        P.dma("sp", lambda e: e.dma_start(out=h[:, :, :], in_=x_d[:, :].rearrange("(b p) d -> p b d", p=128)),
              "xin", writes=b_h)
        P.dma("pool", lambda e: e.dma_start(out=ident[:], in_=ident_d[:, :]), "c_ident", writes=[b_ident])
        P.dma("pool", lambda e: e.dma_start(out=bias[:], in_=bias_d[:, :]), "c_bias", writes=[b_bias])
        P.dma("sp", lambda e: e.dma_start(out=tril[:], in_=tril_d[:, :]), "c_tril", writes=[b_tril])
        P.dma("sp", lambda e: e.dma_start(out=flag[:], in_=flag_d[:, :]), "c_flag", writes=[b_flag])
        P.op("pool", lambda e: e.memset(mhalf[:], -0.5), writes=[b_mhalf])

        def layer_setup(l):
            P.dma("sp", lambda e: e.dma_start(out=colg[:], in_=colg_d[l, :, :]), "l_colg", writes=[b_colg])
            P.dma("sp", lambda e: e.dma_start(out=rowv[:], in_=rowv_d[l, 0:1, :].partition_broadcast(128)),
                  "l_rowv", writes=[b_rowv])
            P.dma("sp", lambda e: e.dma_start(out=bsT[:], in_=bsT_d[l, :, :]), "l_bsT", writes=[b_bsT])
            P.dma("sp", lambda e: e.dma_start(out=esink[:], in_=sinks_d[l, 0:1, :].partition_broadcast(128)),
                  "l_sink", writes=[b_esink])
            P.op("act", lambda e: e.activation(out=esink[:], in_=esink[:], func=AF.Exp), reads=[b_esink], writes=[b_esink])
            wt, b_wt = scr32.get()
            P.dma("sp", lambda e: e.dma_start(out=wt[:].rearrange("p (h s) -> p h s", h=8),
                                              in_=ws_d[l, :, :, :].rearrange("h t s -> t h s")),
                  "l_ws", writes=[b_wt])
            wm, b_wm = scr16.get()
            P.op("dve", lambda e: e.tensor_tensor(out=wm[:].rearrange("p (h s) -> p h s", h=8),
                                                  in0=wt[:].rearrange("p (h s) -> p h s", h=8),
                                                  in1=tril[:].unsqueeze(1).to_broadcast([128, 8, 128]), op=ALU.mult),
                 reads=[b_wt, b_tril], writes=[b_wm])
            ps, bb = bank1()
            psb = ps.bitcast(BF16)
            for hh in range(8):
                P.op("pe", lambda e, hh=hh: e.transpose(out=psb[:, hh * 128:(hh + 1) * 128],
                                                        in_=wm[:, hh * 128:(hh + 1) * 128], identity=ident[:]),
                     reads=[b_wm, b_ident], writes=bb, signal=(hh == 7))
            P.op("act", lambda e: e.activation(out=wsT[:, :, :], in_=psb.rearrange("p (h t) -> p h t", h=8), func=AF.Copy),
                 reads=bb, writes=[b_wsT])

        def do_group(l, kvb, blks, last_layer, phase, nxt, pre_done, gidx):
            G = len(blks)
            Gt = G * 128
            full = [b for b in blks if b > kvb]
            gp_of = {b: i for i, b in enumerate(blks)}
            if full:
                c0 = gp_of[full[0]] * 128
                Ft = len(full) * 128
                b_xf = [b_xTa[gp_of[b]] for b in full]

            def part_mix():
                if not pre_done:
                    norm_transpose_group(blks, [gp_of[b] for b in blks], xTa, [b_xTa[gp_of[b]] for b in blks], 0, len(blks))
                b_xa = [b_xTa[gp_of[b]] for b in blks]

                def fm_chunk(slot, coff, evac):
                    ps, bb = bank1()
                    for kc in range(KC):
                        P.op("pe", lambda e, kc=kc: e.matmul(ps[:, 0:Gt], lhsT=ring[:, slot, kc * 256 + coff:kc * 256 + coff + 128],
                                                             rhs=xTa[:, kc, 0:Gt], start=(kc == 0), stop=(kc == KC - 1)),
                             reads=[b_ring[slot]] + b_xa, writes=bb, signal=(kc == KC - 1))
                    evac(ps, bb)

                if full:
                    for pc in range(2):
                        it = R.acquire(l, pc)
                        slot = it % NS
                        for cc in range(2):
                            c = pc * 2 + cc
                            fm_chunk(slot, cc * 128,
                                     lambda ps, bb, c=c: P.op("act", lambda e: e.activation(out=qT[:, c, 0:Gt], in_=ps[:, 0:Gt], func=AF.Copy),
                                                              reads=bb, writes=[b_qT[c]]))
                        R.release(it)
                it_kv = R.acquire(l, 2)
                s_kv = it_kv % NS
                b0 = blks[0]
                fm_chunk(s_kv, 0,
                         lambda ps, bb: P.op("act", lambda e: e.activation(out=kT[:, (b0 % 8) * 128:(b0 % 8) * 128 + Gt], in_=ps[:, 0:Gt], func=AF.Copy),
                                             reads=bb, writes=[b_kT[b % 8] for b in blks]))

                def do_v(b):
                    gp = gp_of[b]
                    ps, bb = bank1()
                    for kc in range(KC):
                        P.op("pe", lambda e, kc=kc: e.matmul(ps[:, 0:128], lhsT=xTa[:, kc, gp * 128:(gp + 1) * 128],
                                                             rhs=ring[:, s_kv, kc * 256 + 128:kc * 256 + 256],
                                                             start=(kc == 0), stop=(kc == KC - 1)),
                             reads=[b_ring[s_kv], b_xTa[gp]], writes=bb, signal=(kc == KC - 1))
                    outv = vA[:, b % 8, :].rearrange("p (j e) -> p j e", j=2)[:, :, 0:64]
                    onev = vA[:, b % 8, :].rearrange("p (j e) -> p j e", j=2)[:, :, 64:65]
                    inv = ps[:, 0:128].rearrange("p (j e) -> p j e", j=2)
                    if b == H - 1:
                        P.op("dve", lambda e: e.tensor_scalar(out=outv, in0=inv, scalar1=flag[:, 0:1], scalar2=None, op0=ALU.mult),
                             reads=bb + [b_flag], writes=[b_vA[b % 8]])
                        P.op("dve", lambda e: e.tensor_copy(out=onev, in_=flag[:, 0:1].unsqueeze(1).to_broadcast([128, 2, 1])),
                             reads=[b_flag], writes=[b_vA[b % 8]])
                    else:
                        P.op("dve", lambda e: e.tensor_copy(out=outv, in_=inv), reads=bb, writes=[b_vA[b % 8]])
                        P.op("dve", lambda e: e.memset(onev, 1.0), writes=[b_vA[b % 8]])

                if not full:
                    for b in blks:
                        do_v(b)
                    R.release(it_kv)
                    return False

                it_zu = [R.acquire(l, 3), R.acquire(l, 4)]
                it_zv = [R.acquire(l, 5), R.acquire(l, 6)]

                def tm_proj(gp, its):
                    ps, bb = bank1()
                    for q, it in enumerate(its):
                        s = it % NS
                        for kcl in range(4):
                            P.op("pe", lambda e, kcl=kcl, s=s, q=q: e.matmul(ps, lhsT=xTa[:, 4 * q + kcl, gp * 128:(gp + 1) * 128],
                                                                             rhs=ring[:, s, kcl * 512:(kcl + 1) * 512],
                                                                             start=(q == 0 and kcl == 0), stop=(q == 1 and kcl == 3)),
                                 reads=[b_ring[s], b_xTa[gp]], writes=bb, signal=(q == 1 and kcl == 3))
                    return ps, bb

                for b in blks:
                    do_v(b)

                stt = {}

                def stage_A(b):
                    gp = gp_of[b]
                    sd = stt[b] = {}
                    E = []
                    for j in range(2):
                        stmp, bl_stmp = stmp_pool.get()
                        for kb, kblk in enumerate((b - 1, b)):
                            ps, bb = bank1()
                            P.op("pe", lambda e, j=j, kblk=kblk, ps=ps: e.matmul(ps, lhsT=kT[64 * j:64 * j + 64, (kblk % 8) * 128:(kblk % 8 + 1) * 128],
                                                                                 rhs=qT[64 * j:64 * j + 64, :, gp * 128:(gp + 1) * 128],
                                                                                 start=True, stop=True),
                                 reads=[b_kT[kblk % 8]] + b_qT, writes=bb)
                            P.op("dve", lambda e, j=j, kb=kb, ps=ps, stmp=stmp: e.scalar_tensor_tensor(
                                out=stmp[:, kb * 512:(kb + 1) * 512], in0=ps, scalar=0.125,
                                in1=bias[:, kb * 1024 + j * 512:kb * 1024 + (j + 1) * 512], op0=ALU.mult, op1=ALU.add),
                                reads=bb + [b_bias], writes=bl_stmp)
                        Ej, bl_Ej = E_pool.get()
                        P.op("act", lambda e, Ej=Ej, stmp=stmp: e.activation(out=Ej, in_=stmp, func=AF.Exp), reads=bl_stmp, writes=bl_Ej)
                        E.append((Ej, bl_Ej))
                    sd["E"] = E
                    gx, bl_gx = gx_pool.get()
                    sd["gx"] = (gx, bl_gx)
                    ps_zu, bb_zu = tm_proj(gp, it_zu)
                    P.op("act", lambda e: e.activation(out=gx[:, 0:512], in_=ps_zu, func=AF.Gelu_apprx_tanh), reads=bb_zu, writes=bl_gx)
                    ps_zv, bb_zv = tm_proj(gp, it_zv)
                    P.op("act", lambda e: e.activation(out=gx[:, 512:1024], in_=ps_zv, func=AF.Gelu_apprx_tanh), reads=bb_zv, writes=bl_gx)

                def stage_B(b):
                    sd = stt[b]
                    gx, bl_gx = sd["gx"]
                    xv = gx[:, 512:1024]
                    po = []
                    for j in range(2):
                        Ej, bl_Ej = sd["E"][j]
                        ps, bb = bank1()
                        for c in range(4):
                            P.op("pe", lambda e, j=j, c=c, ps=ps, Ej=Ej: e.matmul(ps[:, c * 65:(c + 1) * 65], lhsT=Ej[:, c * 128:(c + 1) * 128],
                                                                                  rhs=vA[:, (b - 1) % 8, 65 * j:65 * j + 65], start=True, stop=False),
                                 reads=bl_Ej + [b_vA[(b - 1) % 8]], writes=bb, signal=False)
                            P.op("pe", lambda e, j=j, c=c, ps=ps, Ej=Ej: e.matmul(ps[:, c * 65:(c + 1) * 65], lhsT=Ej[:, 512 + c * 128:512 + (c + 1) * 128],
                                                                                  rhs=vA[:, b % 8, 65 * j:65 * j + 65], start=False, stop=True),
                                 reads=bl_Ej + [b_vA[b % 8]], writes=bb, signal=(c == 3))
                        po.append((ps, bb))
                    st6, b_st6 = stat.get()
                    P.op("dve", lambda e: e.bn_stats(out=st6[:, 0:6], in_=xv), reads=bl_gx, writes=[b_st6])
                    mv, b_mv = stat.get()
                    P.op("dve", lambda e: e.bn_aggr(out=mv[:, 0:2], in_=st6[:, 0:6]), reads=[b_st6], writes=[b_mv])
                    rs_ln, b_rs_ln = rstd_from(mv[:, 1:2], b_mv, 1.0)
                    den, b_den = stat.get()
                    for j in range(2):
                        ps, bb = po[j]
                        P.op("dve", lambda e, j=j, ps=ps: e.tensor_tensor(out=den[:, 4 * j:4 * j + 4].unsqueeze(2),
                                                                          in0=ps[:, 0:260].rearrange("p (c e) -> p c e", e=65)[:, :, 64:65],
                                                                          in1=esink[:, 4 * j:4 * j + 4].unsqueeze(2), op=ALU.add),
                             reads=bb + [b_esink], writes=[b_den])
                    rden, b_rden = stat.get()
                    P.op("dve", lambda e: e.reciprocal(out=rden[:, 0:8], in_=den[:, 0:8]), reads=[b_den], writes=[b_rden])
                    at, bl_at = at_pool.get()
                    for j in range(2):
                        ps, bb = po[j]
                        P.op("dve", lambda e, j=j, ps=ps: e.tensor_tensor(out=at[:, j * 256:(j + 1) * 256].rearrange("p (c d) -> p c d", c=4),
                                                                          in0=ps[:, 0:260].rearrange("p (c e) -> p c e", e=65)[:, :, 0:64],
                                                                          in1=rden[:, 4 * j:4 * j + 4].unsqueeze(2).to_broadcast([128, 4, 64]),
                                                                          op=ALU.mult),
                             reads=bb + [b_rden], writes=bl_at)
                    ss_a, b_ss_a = sumsq(at[:, 0:512], bl_at, 512)
                    rs_a, b_rs_a = rstd_from(ss_a, b_ss_a, 512.0)
                    sd["at"] = (at, bl_at, rs_a, b_rs_a)
                    P.op("dve", lambda e: e.tensor_scalar(out=xv, in0=xv, scalar1=mv[:, 0:1], scalar2=rs_ln,
                                                          op0=ALU.subtract, op1=ALU.mult), reads=bl_gx + [b_mv, b_rs_ln], writes=bl_gx)
                    P.op("dve", lambda e: e.tensor_tensor(out=xv, in0=xv, in1=rowv[:, 2048:2560], op=ALU.mult),
                         reads=bl_gx + [b_rowv], writes=bl_gx)
                    vh, bl_vh = vh_pool.get()
                    P.op("dve", lambda e: e.tensor_tensor(out=vh[:, 0:512], in0=xv, in1=rowv[:, 2560:3072], op=ALU.add),
                         reads=bl_gx + [b_rowv], writes=bl_vh)
                    sd["vh"] = (vh, bl_vh)

                def stage_C(b):
                    gp = gp_of[b]
                    sd = stt.pop(b)
                    gx, bl_gx = sd["gx"]
                    gu = gx[:, 0:512]
                    xv = gx[:, 512:1024]
                    vh, bl_vh = sd["vh"]
                    at, bl_at, rs_a, b_rs_a = sd["at"]
                    heads, b_heads = headsp.get()
                    ps_sp, bb_sp = bank1()
                    for hh in range(8):
                        P.op("pe", lambda e, hh=hh: e.matmul(ps_sp[:, hh * 64:(hh + 1) * 64], lhsT=wsT[:, hh, :], rhs=vh[:, hh * 64:(hh + 1) * 64],
                                                             start=True, stop=True),
                             reads=[b_wsT] + bl_vh, writes=bb_sp, signal=(hh == 7))
                    P.op("act", lambda e: e.activation(out=heads[:, 0:512], in_=at[:, 0:512], func=AF.Copy, scale=rs_a),
                         reads=bl_at + [b_rs_a], writes=[b_heads])
                    P.op("dve", lambda e: e.tensor_tensor(out=xv.rearrange("p (h c) -> p h c", h=8),
                                                          in0=ps_sp.rearrange("p (h c) -> p h c", h=8),
                                                          in1=bsT[:, 0:8].unsqueeze(2).to_broadcast([128, 8, 64]), op=ALU.add),
                         reads=bb_sp + [b_bsT] + bl_gx, writes=bl_gx)
                    P.op("dve", lambda e: e.tensor_tensor(out=gu, in0=gu, in1=xv, op=ALU.mult),
                         reads=bl_gx, writes=bl_gx)
                    ss_g, b_ss_g = sumsq(gu, bl_gx, 512)
                    rs_g, b_rs_g = rstd_from(ss_g, b_ss_g, 512.0)
                    P.op("act", lambda e: e.activation(out=heads[:, 512:1024], in_=gu, func=AF.Copy, scale=rs_g),
                         reads=bl_gx + [b_rs_g], writes=[b_heads])
                    transpose_to(heads, b_heads, xTb, b_xTb[gp], gp, 8)

                nf = len(full)
                for step in range(nf + 2):
                    if step < nf:
                        stage_A(full[step])
                    if 0 <= step - 1 < nf:
                        stage_B(full[step - 1])
                    if 0 <= step - 2 < nf:
                        stage_C(full[step - 2])
                for it in [it_kv] + it_zu + it_zv:
                    R.release(it)

                hoisted = False
                if nxt is not None:
                    norm_transpose_group(nxt, list(range(len(nxt))), xTa, [b_xTa[i] for i in range(len(nxt))], 0, len(nxt))
                    hoisted = True

                it_wo = [R.acquire(l, 7 + i) for i in range(4)]

                def m6_block(b):
                    gp = gp_of[b]
                    ps2, bb2 = bank2()
                    for i, it in enumerate(it_wo):
                        s = it % NS
                        c, q = i // 2, i % 2
                        for kcl in range(4):
                            P.op("pe", lambda e, kcl=kcl, s=s, c=c, q=q: e.matmul(ps2[:, c * 512:(c + 1) * 512], lhsT=xTb[:, 4 * q + kcl, gp * 128:(gp + 1) * 128],
                                                                                  rhs=ring[:, s, kcl * 512:(kcl + 1) * 512],
                                                                                  start=(q == 0 and kcl == 0), stop=(q == 1 and kcl == 3)),
                                 reads=[b_ring[s], b_xTb[gp]], writes=[bb2[c]], signal=(q == 1 and kcl == 3))
                    residual_epilogue(ps2, bb2, b, 0)

                for b in full:
                    m6_block(b)
                for it in it_wo:
                    R.release(it)

                return hoisted

            def part_ffn():
                if not pre_done:
                    norm_transpose_group(full, [gp_of[b] for b in full], xTa, [b_xTa[gp_of[b]] for b in full], 16, 1)
                c0 = gp_of[full[0]] * 128
                Ft = len(full) * 128
                b_xf = [b_xTa[gp_of[b]] for b in full]

                def f2_chunk(ci, sub, it_g, it_u):
                    pss = []
                    for it in (it_g, it_u):
                        s = it % NS
                        ps, bb = bank1()
                        for kc in range(KC):
                            P.op("pe", lambda e, kc=kc, s=s, ps=ps: e.matmul(ps[:, 0:Ft], lhsT=ring[:, s, kc * 256 + sub * 128:kc * 256 + sub * 128 + 128],
                                                                             rhs=xTa[:, kc, c0:c0 + Ft], start=(kc == 0), stop=(kc == KC - 1)),
                                 reads=[b_ring[s]] + b_xf, writes=bb, signal=(kc == KC - 1))
                        pss.append((ps, bb))
                    sl, b_sl = scr32.get()
                    ps_g, bb_g = pss[0]
                    ps_u, bb_u = pss[1]
                    P.op("act", lambda e: e.activation(out=sl[:, 0:Ft], in_=ps_g[:, 0:Ft], func=AF.Silu), reads=bb_g, writes=[b_sl])
                    P.op("dve", lambda e: e.tensor_tensor(out=actT[:, ci, c0:c0 + Ft], in0=ps_u[:, 0:Ft], in1=sl[:, 0:Ft], op=ALU.mult),
                         reads=bb_u + [b_sl], writes=[b_actT[ci]])

                for i in range(11):
                    it_g = R.acquire(l, 11 + 2 * i)
                    it_u = R.acquire(l, 12 + 2 * i)
                    for sub in range(2):
                        f2_chunk(2 * i + sub, sub, it_g, it_u)
                    conv_pump(2)
                    R.release(it_g)
                    R.release(it_u)

                hoisted = False
                if nxt is not None:
                    nblks, nfull = nxt
                    npos = {b: i for i, b in enumerate(nblks)}
                    norm_transpose_group(nfull, [npos[b] for b in nfull], xTa, [b_xTa[npos[b]] for b in nfull], 16, len(nfull), d16_pool)
                    hoisted = True

                acc = [bank2() for _ in full]
                for i in range(11):
                    it = R.acquire(l, 33 + i)
                    s = it % NS
                    for bi, b in enumerate(full):
                        gp = gp_of[b]
                        ps2, bb2 = acc[bi]
                        for kcl in range(2):
                            for half in range(2):
                                P.op("pe", lambda e, kcl=kcl, half=half, s=s, ps2=ps2, gp=gp, i=i: e.matmul(
                                    ps2[:, half * 512:(half + 1) * 512], lhsT=actT[:, 2 * i + kcl, gp * 128:(gp + 1) * 128],
                                    rhs=ring[:, s, kcl * 1024 + half * 512:kcl * 1024 + (half + 1) * 512],
                                    start=(i == 0 and kcl == 0), stop=(i == 10 and kcl == 1)),
                                    reads=[b_ring[s], b_actT[2 * i + kcl]], writes=[bb2[half]],
                                    signal=(i == 10 and kcl == 1 and half == 1))
                    R.release(it)
                for bi, b in enumerate(full):
                    ps2, bb2 = acc[bi]
                    residual_epilogue(ps2, bb2, b, 1024)

                return hoisted

            def part_ple():
                pti = do_group.pt_i % 2
                do_group.pt_i += 1
                pt, b_pt = pTt[pti], b_pTt[pti]
                t0 = full[0] * 128
                P.dma("pool", lambda e: e.dma_start(out=pt[:, :, 0:Ft], in_=pT_d[l, :, :, t0:t0 + Ft].rearrange("k p t -> p k t")),
                      f"pT{pti}", writes=b_pt)
                xTp, b_xTp = (xTb, b_xTb) if gidx % 2 == 0 else (xTa, b_xTa)
                xTn, b_xTn = (xTa, b_xTa) if gidx % 2 == 0 else (xTb, b_xTb)
                if not pre_done:
                    norm_transpose_group(full, [gp_of[b] for b in full], xTp, [b_xTp[gp_of[b]] for b in full], 24, 1)
                hoisted = False
                if nxt is not None:
                    nblks, nfull = nxt
                    npos = {b: i for i, b in enumerate(nblks)}
                    norm_transpose_group(nfull, [npos[b] for b in nfull], xTn, [b_xTn[npos[b]] for b in nfull], 24, len(nfull), d16_pool)
                    hoisted = True
                it_ple = R.acquire(l, 44)
                s_ple = it_ple % NS
                it_pg = [R.acquire(l, 45 + i) for i in range(4)]

                def p_block(bi, b):
                    gp = gp_of[b]
                    psg, bbg = bank2()
                    for i, it in enumerate(it_pg):
                        s = it % NS
                        c, q = i // 2, i % 2
                        for kcl in range(4):
                            P.op("pe", lambda e, kcl=kcl, s=s, c=c, q=q: e.matmul(psg[:, c * 512:(c + 1) * 512], lhsT=xTp[:, 4 * q + kcl, gp * 128:(gp + 1) * 128],
                                                                                  rhs=ring[:, s, kcl * 512:(kcl + 1) * 512],
                                                                                  start=(q == 0 and kcl == 0), stop=(q == 1 and kcl == 3)),
                                 reads=[b_ring[s], b_xTp[gp]], writes=[bbg[c]], signal=(q == 1 and kcl == 3))
                    psp, bbp = bank2()
                    for half in range(2):
                        for kc in range(2):
                            P.op("pe", lambda e, kc=kc, half=half: e.matmul(psp[:, half * 512:(half + 1) * 512], lhsT=pt[:, kc, bi * 128:(bi + 1) * 128],
                                                                            rhs=ring[:, s_ple, kc * 1024 + half * 512:kc * 1024 + (half + 1) * 512],
                                                                            start=(kc == 0), stop=(kc == 1)),
                                 reads=[b_ring[s_ple]] + b_pt, writes=[bbp[half]], signal=(kc == 1))
                    sg, b_sg = scr32.get()
                    P.op("act", lambda e: e.activation(out=sg[:], in_=psg, func=AF.Sigmoid), reads=bbg, writes=[b_sg])
                    P.op("dve", lambda e: e.tensor_tensor(out=sg[:], in0=psp, in1=sg[:], op=ALU.mult), reads=bbp + [b_sg], writes=[b_sg])
                    P.op("dve", lambda e: e.tensor_tensor(out=h[:, b, :], in0=h[:, b, :], in1=sg[:], op=ALU.add),
                         reads=[b_sg, b_h[b]], writes=[b_h[b]])
                    if last_layer and b >= H:
                        P.dma("sp", lambda e: e.dma_start(out=y_d[(b - H) * 128:(b - H + 1) * 128, :], in_=h[:, b, :]),
                              "yout", reads=[b_h[b]])

                for bi, b in enumerate(full):
                    p_block(bi, b)
                R.release(it_ple)
                for it in it_pg:
                    R.release(it)
                return hoisted

            Gt_unused = None
            return {"mix": part_mix, "ffn": part_ffn, "ple": part_ple}[phase]()

        do_group.pt_i = 0

        for l in range(L):
            layer_setup(l)
            kvb, groups = layer_groups(l)
            if l + 1 < L:
                conv_layer(l + 1)
            fulls = [[b for b in blks if b > kvb] for blks in groups]
            fg = [gi for gi in range(len(groups)) if fulls[gi]]
            done = False
            for gi, blks in enumerate(groups):
                nxt = groups[gi + 1] if gi + 1 < len(groups) else None
                done = do_group(l, kvb, blks, l == L - 1, "mix", nxt, done, gi)
            done = False
            for k, gi in enumerate(fg):
                nxt = (groups[fg[k + 1]], fulls[fg[k + 1]]) if k + 1 < len(fg) else None
                done = do_group(l, kvb, groups[gi], l == L - 1, "ffn", nxt, done, k)
            conv_pump(10 ** 6)
            done = False
            for k, gi in enumerate(fg):
                nxt = (groups[fg[k + 1]], fulls[fg[k + 1]]) if k + 1 < len(fg) else None
                done = do_group(l, kvb, groups[gi], l == L - 1, "ple", nxt, done, k)
        P.wait_all("sp", b_h)
        P.emit()
    return nc


_PROG_CACHE = {}


def _get_prog(L, H):
    if (L, H) not in _PROG_CACHE:
        _PROG_CACHE[(L, H)] = build_program(L, H)
    return _PROG_CACHE[(L, H)]


def _constants():
    ident = np.eye(128, dtype=np.float32)
    s = np.arange(128)[:, None].astype(np.float32)
    t = np.arange(128)[None, :].astype(np.float32)
    slopes = np.exp2(-8.0 * (np.arange(8, dtype=np.float32) + 1.0) / 8.0).astype(np.float32)
    bias = np.zeros((128, 2, 8, 128), np.float32)
    for hh in range(8):
        dprev = t + 128.0 - s
        bias[:, 0, hh, :] = np.where(dprev < 128, -slopes[hh] * dprev, -30000.0)
        dcur = t - s
        bias[:, 1, hh, :] = np.where(dcur >= 0, -slopes[hh] * dcur, -30000.0)
    tril = (np.arange(128)[None, :] <= np.arange(128)[:, None]).astype(np.float32)
    return ident, bias.reshape(128, 2048), tril


def _run(hfull, layers, H, inp):
    L = len(layers)
    NB = H + OWN
    nc = _get_prog(L, H)
    ident, bias, tril = _constants()
    sl = slice(layers[0], layers[-1] + 1)
    c32 = lambda a: np.ascontiguousarray(a, dtype=np.float32)
    colg = np.stack([
        np.concatenate([
            inp["ln_mix_pre"][l].reshape(8, 128).T,
            np.concatenate([inp["g_attn_out"][l], inp["g_gm_out"][l]]).reshape(8, 128).T,
            inp["ln_ffn_pre"][l].reshape(8, 128).T,
            inp["ln_ple_gate"][l].reshape(8, 128).T], axis=1) for l in layers])
    rowv = np.stack([np.concatenate([inp["ln_mix_post"][l], inp["ln_ffn_post"][l], inp["gm_ln_g"][l], inp["gm_ln_b"][l]])[None, :]
                     for l in layers])
    bsT = np.stack([inp["gm_bs"][l].T for l in layers])
    sinks = np.stack([inp["attn_sinks"][l][None, :] for l in layers])
    shared = {
        "w_in": c32(inp["w_in"][sl]), "w_out": c32(inp["w_out"][sl]), "w_gate": c32(inp["w_ffn_gate"][sl]),
        "w_up": c32(inp["w_ffn_up"][sl]), "w_down": c32(inp["w_ffn_down"][sl]), "w_ple": c32(inp["w_ple"][sl]),
        "w_pg": c32(inp["w_ple_gate"][sl]), "gm_ws": c32(inp["gm_ws"][sl]),
        "colg": c32(colg), "rowv": c32(rowv), "bsT": c32(bsT), "sinks": c32(sinks),
        "ident": ident, "bias": bias, "tril": tril,
    }
    in_maps = []
    for c in range(NCORES):
        bt, j = c // 4, c % 4
        t0 = j * 2048 - H * 128
        xs = np.zeros((NB * 128, D), np.float32)
        ps = np.zeros((L, 2, 128, NB * 128), np.float32)
        lo = max(t0, 0)
        xs[lo - t0:] = hfull[bt, lo:(j + 1) * 2048]
        pp = inp["p"][sl, bt, lo:(j + 1) * 2048, :]
        ps[:, :, :, lo - t0:] = pp.transpose(0, 2, 1).reshape(L, 2, 128, -1)
        m = dict(shared)
        m["x"] = xs
        m["pT"] = ps
        m["flag"] = np.full((128, 1), 0.0 if j == 0 else 1.0, np.float32)
        in_maps.append(m)
    res = run_bass_kernel_spmd(nc, in_maps, core_ids=list(range(NCORES)))
    out = np.empty((2, 8192, D), np.float32)
    for c in range(NCORES):
        bt, j = c // 4, c % 4
        out[bt, j * 2048:(j + 1) * 2048] = res.results[c]["y"]
    return out


def kernel(**inputs):
    inp = {k: np.asarray(v) for k, v in inputs.items()}
    hcur = np.asarray(inp["x"], dtype=np.float32)
    if FUSED:
        return _run(hcur, [0, 1, 2, 3], 4, inp)
    for l in range(4):
        hcur = _run(hcur, [l], 1, inp)
    return hcur
```

```python
import numpy as np
import concourse.bass as bass
import concourse.mybir as mybir
from concourse.bass_utils import run_bass_kernel_spmd
from contextlib import ExitStack

F32 = mybir.dt.float32
BF16 = mybir.dt.bfloat16
AF = mybir.ActivationFunctionType
ALU = mybir.AluOpType
AX = mybir.AxisListType

D = 1024
KC = 8
DFF = 2816
NFC = 22
EPS = 1e-6
NCORES = 8
OWN = 16
FUSED = True
NS = 9
NP = 49
NHEADS = 1


class Buf:
    __slots__ = ("name", "w", "r", "psum", "last")

    def __init__(self, name, psum=False):
        self.name = name
        self.psum = psum
        self.last = -1
        self.w = None
        self.r = []


class Prog:
    ENGS = ("pe", "act", "dve", "pool", "sp")

    def __init__(self, nc, stack):
        self.nc = nc
        self.stack = stack
        self.streams = {e: [] for e in self.ENGS}
        self.sems = {}
        for e in self.ENGS:
            self.sems[e] = stack.enter_context(nc.semaphore("sem_" + e))
        self.cnt = {e: 0 for e in self.ENGS}
        self.idx = {e: 0 for e in self.ENGS}
        self.unsig = {e: False for e in self.ENGS}
        self.known = {e: {} for e in self.ENGS}
        self.dma_sems = {}
        self.dma_cnt = {}
        self.gidx = 0

    def dma_sem(self, name):
        if name not in self.dma_sems:
            self.dma_sems[name] = self.stack.enter_context(self.nc.semaphore("dsem_" + name))
            self.dma_cnt[name] = 0
            self.sems["dma:" + name] = self.dma_sems[name]
        return name

    def _need(self, eng, dep):
        key, val, deng, didx = dep
        if deng == eng:
            if eng in ("pe", "sp"):
                return
        if self.known[eng].get(key, 0) >= val:
            return
        self.known[eng][key] = val
        self.streams[eng].append(("wait", key, val))

    def _deps(self, eng, reads, writes):
        for b in reads:
            if b.w is not None:
                self._need(eng, b.w)
            if b.psum:
                for r in b.r:
                    if r[2] != eng:
                        self._need(eng, r)
        for b in writes:
            if b.w is not None:
                self._need(eng, b.w)
            for r in b.r:
                self._need(eng, r)

    def op(self, eng, fn, reads=(), writes=(), signal=True):
        self._deps(eng, reads, writes)
        if signal:
            self.cnt[eng] += 1
            self.unsig[eng] = False
        else:
            self.unsig[eng] = True
        tick = self.cnt[eng] if signal else self.cnt[eng] + 1
        me = (eng, tick, eng, self.idx[eng])
        self.idx[eng] += 1
        self.streams[eng].append(("op", fn, signal))
        self.gidx += 1
        for b in reads:
            b.last = self.gidx
            if not b.r or b.r[-1][:2] != me[:2]:
                b.r.append(me)
        for b in writes:
            b.last = self.gidx
            b.w = me
            b.r = []

    def dma(self, eng, fn, semname, reads=(), writes=()):
        self.dma_sem(semname)
        self._deps(eng, reads, writes)
        self.dma_cnt[semname] += 16
        key = "dma:" + semname
        me = (key, self.dma_cnt[semname], "dma", -1)
        self.idx[eng] += 1
        self.streams[eng].append(("dma", fn, semname))
        for b in reads:
            b.r.append(me)
        for b in writes:
            b.w = me
            b.r = []

    def force_signal(self, eng):
        if not self.unsig[eng]:
            return
        st = self.streams[eng]
        for k in range(len(st) - 1, -1, -1):
            if st[k][0] == "op":
                assert not st[k][2]
                st[k] = ("op", st[k][1], True)
                break
        self.cnt[eng] += 1
        self.unsig[eng] = False

    def wait_all(self, eng, bufs):
        best = {}
        for b in bufs:
            for dep in ([b.w] if b.w is not None else []) + list(b.r):
                if dep[0] not in best or best[dep[0]][1] < dep[1]:
                    best[dep[0]] = dep
        for dep in best.values():
            self._need(eng, dep)

    def emit(self):
        for e in ("pe", "act", "dve", "pool"):
            assert not self.unsig[e], f"engine {e} ends with unsignalled instruction"
        nc = self.nc
        with nc.Block() as block:
            def run(engname):
                def body(eng):
                    mysem = self.sems[engname]
                    for ent in self.streams[engname]:
                        if ent[0] == "wait":
                            eng.wait_ge(self.sems[ent[1]], ent[2])
                        elif ent[0] == "op":
                            ins = ent[1](eng)
                            if ent[2]:
                                ins.then_inc(mysem, 1)
                        else:
                            ins = ent[1](eng)
                            ins.then_inc(self.dma_sems[ent[2]], 16)
                return body
            block.tensor(run("pe"))
            block.scalar(run("act"))
            block.vector(run("dve"))
            block.gpsimd(run("pool"))
            block.sync(run("sp"))


class RR:
    def __init__(self, items):
        self.items = items
        self.i = 0

    def get(self):
        it = self.items[self.i % len(self.items)]
        self.i += 1
        return it


def build_program(L, H, OWN=OWN):
    NB = H + OWN
    NT = NB * 128
    nc = bass.Bass("TRN2", target_bir_lowering=False)
    dt = lambda name, shape, dtype=F32, kind="ExternalInput": nc.dram_tensor(name, shape, dtype, kind=kind)
    x_d = dt("x", [NT, D])
    pT_d = dt("pT", [L, 2, 128, NT])
    w_in_d = dt("w_in", [L, D, 1792])
    w_out_d = dt("w_out", [L, D, D])
    w_gate_d = dt("w_gate", [L, D, DFF])
    w_up_d = dt("w_up", [L, D, DFF])
    w_down_d = dt("w_down", [L, DFF, D])
    w_ple_d = dt("w_ple", [L, 256, D])
    w_pg_d = dt("w_pg", [L, D, D])
    ws_d = dt("gm_ws", [L, 8, 128, 128])
    colg_d = dt("colg", [L, 128, 32])
    rowv_d = dt("rowv", [L, 1, 3072])
    bsT_d = dt("bsT", [L, 128, 8])
    sinks_d = dt("sinks", [L, 1, 8])
    ident_d = dt("ident", [128, 128])
    bias_d = dt("bias", [128, 2048])
    tril_d = dt("tril", [128, 128])
    flag_d = dt("flag", [128, 1])
    y_d = dt("y", [OWN * 128, D], F32, "ExternalOutput")
    wsc_d = dt("wsc", [L, NP, 128, 2048], BF16, "Internal")

    with ExitStack() as st:
        P = Prog(nc, st)
        sb = lambda name, shape, dtype: st.enter_context(nc.sbuf_tensor(name, shape, dtype))

        h = sb("h", [128, NB, D], F32)
        b_h = [Buf(f"h{i}") for i in range(NB)]
        ring = sb("ring", [128, NS, 2048], BF16)
        b_ring = [Buf(f"ring{i}") for i in range(NS)]
        xTa = sb("xTa", [128, KC, 512], BF16)
        b_xTa = [Buf(f"xTa{i}") for i in range(4)]
        xTb = sb("xTb", [128, KC, 512], BF16)
        b_xTb = [Buf(f"xTb{i}") for i in range(4)]
        qT = sb("qT", [128, 4, 512], BF16)
        b_qT = [Buf(f"qT{i}") for i in range(4)]
        kT = sb("kT", [128, 8 * 128], BF16)
        b_kT = [Buf(f"kT{i}") for i in range(8)]
        vA = sb("vA", [128, 8, 130], BF16)
        b_vA = [Buf(f"vA{i}") for i in range(8)]
        actT = sb("actT", [128, NFC, 512], BF16)
        b_actT = [Buf(f"actT{i}") for i in range(NFC)]
        pTt = [qT[:, 2 * i:2 * i + 2, :] for i in range(2)]
        b_pTt = [[b_qT[2 * i], b_qT[2 * i + 1]] for i in range(2)]
        scr32 = RR([(sb(f"s32_{i}", [128, D], F32), Buf(f"s32_{i}")) for i in range(4)])
        scr16 = RR([(sb(f"s16_{i}", [128, D], BF16), Buf(f"s16_{i}")) for i in range(3)])
        headsp = RR([(sb(f"heads_{i}", [128, D], BF16), Buf(f"heads_{i}")) for i in range(NHEADS)])
        stat = RR([(sb(f"st_{i}", [128, 8], F32), Buf(f"st_{i}")) for i in range(24)])
        def _a32(i):
            return (actT[:, 4 * i:4 * i + 4, :].rearrange("p a b -> p (a b)").bitcast(F32), b_actT[4 * i:4 * i + 4])

        def _a16(i):
            return (actT[:, 16 + 2 * i:18 + 2 * i, :].rearrange("p a b -> p (a b)"), b_actT[16 + 2 * i:18 + 2 * i])

        d32 = [(t_[:], [b_]) for (t_, b_) in scr32.items]
        d16 = [(t_[:], [b_]) for (t_, b_) in scr16.items]
        gx_pool = RR([_a32(0), _a32(1), _a32(2)])
        stmp_pool = RR([_a32(3), d32[0], d32[1]])
        at_pool = RR([d32[2], d32[3]])
        E_pool = RR([_a16(0), _a16(1), _a16(2), d16[0]])
        vh_pool = RR([d16[1], d16[2]])
        d16_pool = RR([d16[0], d16[1], d16[2], (headsp.items[0][0][:], [headsp.items[0][1]])])
        junk = RR([(sb(f"junk_{i}", [128, D], BF16), Buf(f"junk_{i}")) for i in range(1)])
        ident = sb("identb", [128, 128], BF16); b_ident = Buf("ident")
        bias = sb("biasb", [128, 2048], BF16); b_bias = Buf("bias")
        tril = sb("trilb", [128, 128], F32); b_tril = Buf("tril")
        flag = sb("flagb", [128, 1], F32); b_flag = Buf("flag")
        mhalf = sb("mhalf", [128, 1], F32); b_mhalf = Buf("mhalf")
        colg = sb("colgb", [128, 32], F32); b_colg = Buf("colg")
        rowv = sb("rowvb", [128, 3072], F32); b_rowv = Buf("rowv")
        bsT = sb("bsTb", [128, 8], F32); b_bsT = Buf("bsT")
        esink = sb("esink", [128, 8], F32); b_esink = Buf("esink")
        wsT = sb("wsT", [128, 8, 128], BF16); b_wsT = Buf("wsT")

        psd = [st.enter_context(nc.psum_tensor(f"ps{i}", [128, 1024], F32)) for i in range(4)]
        b_bank = [Buf(f"bank{i}", True) for i in range(8)]
        bank_ptr = [0]

        def bank1():
            i = min(range(8), key=lambda k: b_bank[k].last)
            b_bank[i].last = P.gidx + 0.5
            return psd[i // 2][:, (i % 2) * 512:(i % 2) * 512 + 512], [b_bank[i]]

        def bank2():
            d = min(range(4), key=lambda k: max(b_bank[2 * k].last, b_bank[2 * k + 1].last))
            b_bank[2 * d].last = b_bank[2 * d + 1].last = P.gidx + 0.5
            return psd[d][:, :], [b_bank[2 * d], b_bank[2 * d + 1]]

        b_wsc = {}

        conv_pending = []

        def conv_pump(n):
            for _ in range(min(n, len(conv_pending))):
                conv_pending.pop(0)()

        def conv_layer(l):
            def cv(batch, piece, dst_view, src_ap):
                def thunk():
                    key = (l, batch)
                    if key not in b_wsc:
                        b_wsc[key] = Buf(f"wsc{l}{batch}")
                    P.dma("pool", lambda e, d=dst_view, s=src_ap: e.dma_start(out=d, in_=s),
                          f"cv{l}{batch}", writes=[])
                    b_wsc[key].w = ("dma:" + f"cv{l}{batch}", P.dma_cnt[f"cv{l}{batch}"], "dma", -1)
                conv_pending.append(thunk)

            def colpiece(batch, piece, wd, c0, ncols=256, dst_off=0, kcn=KC):
                dst = wsc_d[l, piece, :, :].rearrange("p (k n) -> p k n", k=kcn)[:, :, dst_off:dst_off + ncols]
                src = wd[l, :, :].rearrange("(k p) n -> p k n", p=128)[:, :, c0:c0 + ncols]
                cv(batch, piece, dst, src)

            def kpiece(batch, piece, wd, c0, q):
                dst = wsc_d[l, piece, :, :].rearrange("p (k n) -> p k n", k=4)
                src = wd[l, :, :].rearrange("(k p) n -> p k n", p=128)[:, 4 * q:4 * q + 4, c0:c0 + 512]
                cv(batch, piece, dst, src)

            for pc in range(2):
                for cc in range(2):
                    c = pc * 2 + cc
                    for j in range(2):
                        colpiece("A", pc, w_in_d, 64 * (c + 4 * j), 64, cc * 128 + j * 64)
            colpiece("A", 2, w_in_d, 512)
            for i in range(4):
                kpiece("A", 3 + i, w_in_d, 768 + 512 * (i // 2), i % 2)
            for i in range(4):
                kpiece("A", 7 + i, w_out_d, 512 * (i // 2), i % 2)
            for i in range(11):
                colpiece("B", 11 + 2 * i, w_gate_d, 256 * i)
                colpiece("B", 12 + 2 * i, w_up_d, 256 * i)
            for i in range(11):
                dst = wsc_d[l, 33 + i, :, :].rearrange("p (k n) -> p k n", k=2)
                src = w_down_d[l, 256 * i:256 * i + 256, :].rearrange("(k p) n -> p k n", p=128)
                cv("B", 33 + i, dst, src)
            dst = wsc_d[l, 44, :, :].rearrange("p (k n) -> p k n", k=2)
            src = w_ple_d[l, :, :].rearrange("(k p) n -> p k n", p=128)
            cv("C", 44, dst, src)
            for i in range(4):
                kpiece("C", 45 + i, w_pg_d, 512 * (i // 2), i % 2)

        def piece_batch(piece):
            return "A" if piece < 11 else ("B" if piece < 44 else "C")

        def layer_groups(l):
            kvb = H - (L - l)
            groups = []
            g0 = (kvb // 4) * 4
            for s in range(g0, NB, 4):
                blks = [b for b in range(s, min(s + 4, NB)) if b >= kvb]
                groups.append(blks)
            return kvb, groups

        seq = []
        for l in range(L):
            kvb, groups = layer_groups(l)
            nfull = 0
            for blks in groups:
                full = [b for b in blks if b > kvb]
                if not full:
                    seq.append((l, 2))
                    continue
                nfull += 1
                seq += [(l, p) for p in range(0, 11)]
            for _ in range(nfull):
                seq += [(l, p) for p in range(11, 44)]
            for _ in range(nfull):
                seq += [(l, p) for p in range(44, 49)]

        class Ring:
            def __init__(self):
                self.load_ptr = 0
                self.use_ptr = 0
                self.released = [False] * len(seq)

            def pump(self):
                while self.load_ptr < len(seq) and (self.load_ptr < NS or self.released[self.load_ptr - NS]):
                    i = self.load_ptr
                    l, p = seq[i]
                    s = i % NS
                    P.dma("sp", lambda e, s=s, l=l, p=p: e.dma_start(out=ring[:, s, :], in_=wsc_d[l, p, :, :]),
                          f"ring{s}", reads=[b_wsc[(l, piece_batch(p))]], writes=[b_ring[s]])
                    self.load_ptr += 1

            def acquire(self, l, p):
                i = self.use_ptr
                assert seq[i] == (l, p), (seq[i], l, p)
                self.pump()
                assert self.load_ptr > i, "ring resident set exceeds NS"
                self.use_ptr += 1
                return i

            def release(self, i):
                P.force_signal("pe")
                self.released[i] = True
                self.pump()

        R = Ring()

        def rstd_from(ss_ap, b_ss, n, eps=EPS):
            t, b = stat.get()
            P.op("pool", lambda e: e.tensor_scalar(out=t[:, 0:1], in0=ss_ap, scalar1=1.0 / n, scalar2=eps,
                                                   op0=ALU.mult, op1=ALU.add), reads=[b_ss], writes=[b])
            P.op("pool", lambda e: e.tensor_tensor(out=t[:, 1:2], in0=t[:, 0:1], in1=mhalf[:, 0:1], op=ALU.pow),
                 reads=[b, b_mhalf], writes=[b])
            return t[:, 1:2], b

        def sumsq(in_ap, rd_bufs, ncols):
            t, b = stat.get()
            jt, b_jt = junk.get()
            P.op("act", lambda e: e.activation(out=jt[:, 0:ncols], in_=in_ap, func=AF.Square,
                                               accum_out=t[:, 0:1]), reads=rd_bufs, writes=[b, b_jt])
            return t[:, 0:1], b

        def transpose_to(src16, b_src, dstT, b_dst, gp, gsel):
            ps, bb = bank1()
            psb = ps.bitcast(BF16)
            for kc in range(KC):
                P.op("pe", lambda e, kc=kc: e.transpose(out=psb[:, kc * 128:(kc + 1) * 128],
                                                        in_=src16[:, kc * 128:(kc + 1) * 128], identity=ident[:]),
                     reads=(b_src if isinstance(b_src, list) else [b_src]) + [b_ident], writes=bb, signal=(kc == KC - 1))
            P.op("dve", lambda e: e.tensor_tensor(out=dstT[:, :, gp * 128:(gp + 1) * 128],
                                                  in0=psb.rearrange("p (k t) -> p k t", k=KC),
                                                  in1=colg[:, gsel:gsel + 8].unsqueeze(2).to_broadcast([128, KC, 128]),
                                                  op=ALU.mult),
                 reads=bb + [b_colg], writes=[b_dst])

        def norm_transpose_group(blist, gps, dstT, b_dsts, gsel, lag, a16pool=None, defer=False):
            n = len(blist)
            rss = {}
            pend = []
            for step in range(n + lag):
                if step < n:
                    blk = blist[step]
                    ss, b_ss = sumsq(h[:, blk, :], [b_h[blk]], D)
                    rss[step] = rstd_from(ss, b_ss, D)
                i = step - lag
                if 0 <= i < n:
                    blk = blist[i]
                    rs, b_rs = rss[i]
                    a16, bl_a16 = (a16pool or E_pool).get()
                    if i % 2 == 0:
                        P.op("act", lambda e, a16=a16, blk=blk, rs=rs: e.activation(out=a16, in_=h[:, blk, :], func=AF.Copy, scale=rs),
                             reads=[b_h[blk], b_rs], writes=bl_a16)
                    else:
                        P.op("dve", lambda e, a16=a16, blk=blk, rs=rs: e.tensor_scalar(out=a16, in0=h[:, blk, :], scalar1=rs, scalar2=None, op0=ALU.mult),
                             reads=[b_h[blk], b_rs], writes=bl_a16)
                    if defer:
                        pend.append((a16, bl_a16, i))
                    else:
                        transpose_to(a16, bl_a16, dstT, b_dsts[i], gps[i], gsel)

            def finish():
                for a16, bl_a16, i in pend:
                    transpose_to(a16, bl_a16, dstT, b_dsts[i], gps[i], gsel)
            return finish

        def residual_epilogue(ps2, bb2, blk, goff):
            ss, b_ss = sumsq(ps2, bb2, D)
            rs, b_rs = rstd_from(ss, b_ss, D)
            t, b_t = scr32.get()
            P.op("dve", lambda e: e.scalar_tensor_tensor(out=t[:], in0=ps2, scalar=rs, in1=rowv[:, goff:goff + D],
                                                         op0=ALU.mult, op1=ALU.mult),
                 reads=bb2 + [b_rs, b_rowv], writes=[b_t])
            P.op("dve", lambda e: e.tensor_tensor(out=h[:, blk, :], in0=h[:, blk, :], in1=t[:], op=ALU.add),
                 reads=[b_t, b_h[blk]], writes=[b_h[blk]])

        conv_layer(0)
        conv_pump(10 ** 6)
        P.dma("sp", lambda e: e.dma_start(out=h[:, :, :], in_=x_d[:, :].rearrange("(b p) d -> p b d", p=128)),
              "xin", writes=b_h)
        P.dma("pool", lambda e: e.dma_start(out=ident[:], in_=ident_d[:, :]), "c_ident", writes=[b_ident])
        P.dma("pool", lambda e: e.dma_start(out=bias[:], in_=bias_d[:, :]), "c_bias", writes=[b_bias])
        P.dma("sp", lambda e: e.dma_start(out=tril[:], in_=tril_d[:, :]), "c_tril", writes=[b_tril])
        P.dma("sp", lambda e: e.dma_start(out=flag[:], in_=flag_d[:, :]), "c_flag", writes=[b_flag])
        P.op("pool", lambda e: e.memset(mhalf[:], -0.5), writes=[b_mhalf])

        def layer_setup(l):
            P.dma("sp", lambda e: e.dma_start(out=colg[:], in_=colg_d[l, :, :]), "l_colg", writes=[b_colg])
            P.dma("sp", lambda e: e.dma_start(out=rowv[:], in_=rowv_d[l, 0:1, :].partition_broadcast(128)),
                  "l_rowv", writes=[b_rowv])
            P.dma("sp", lambda e: e.dma_start(out=bsT[:], in_=bsT_d[l, :, :]), "l_bsT", writes=[b_bsT])
            P.dma("sp", lambda e: e.dma_start(out=esink[:], in_=sinks_d[l, 0:1, :].partition_broadcast(128)),
                  "l_sink", writes=[b_esink])
            P.op("act", lambda e: e.activation(out=esink[:], in_=esink[:], func=AF.Exp), reads=[b_esink], writes=[b_esink])
            wt, b_wt = scr32.get()
            P.dma("sp", lambda e: e.dma_start(out=wt[:].rearrange("p (h s) -> p h s", h=8),
                                              in_=ws_d[l, :, :, :].rearrange("h t s -> t h s")),
                  "l_ws", writes=[b_wt])
            wm, b_wm = scr16.get()
            P.op("dve", lambda e: e.tensor_tensor(out=wm[:].rearrange("p (h s) -> p h s", h=8),
                                                  in0=wt[:].rearrange("p (h s) -> p h s", h=8),
                                                  in1=tril[:].unsqueeze(1).to_broadcast([128, 8, 128]), op=ALU.mult),
                 reads=[b_wt, b_tril], writes=[b_wm])
            ps, bb = bank1()
            psb = ps.bitcast(BF16)
            for hh in range(8):
                P.op("pe", lambda e, hh=hh: e.transpose(out=psb[:, hh * 128:(hh + 1) * 128],
                                                        in_=wm[:, hh * 128:(hh + 1) * 128], identity=ident[:]),
                     reads=[b_wm, b_ident], writes=bb, signal=(hh == 7))
            P.op("act", lambda e: e.activation(out=wsT[:, :, :], in_=psb.rearrange("p (h t) -> p h t", h=8), func=AF.Copy),
                 reads=bb, writes=[b_wsT])

        def do_group(l, kvb, blks, last_layer, phase, nxt, pre_done, gidx):
            G = len(blks)
            Gt = G * 128
            full = [b for b in blks if b > kvb]
            gp_of = {b: i for i, b in enumerate(blks)}
            if full:
                c0 = gp_of[full[0]] * 128
                Ft = len(full) * 128
                b_xf = [b_xTa[gp_of[b]] for b in full]

            def part_mix():
                if not pre_done:
                    norm_transpose_group(blks, [gp_of[b] for b in blks], xTa, [b_xTa[gp_of[b]] for b in blks], 0, len(blks))
                b_xa = [b_xTa[gp_of[b]] for b in blks]

                def fm_chunk(slot, coff, evac):
                    ps, bb = bank1()
                    for kc in range(KC):
                        P.op("pe", lambda e, kc=kc: e.matmul(ps[:, 0:Gt], lhsT=ring[:, slot, kc * 256 + coff:kc * 256 + coff + 128],
                                                             rhs=xTa[:, kc, 0:Gt], start=(kc == 0), stop=(kc == KC - 1)),
                             reads=[b_ring[slot]] + b_xa, writes=bb, signal=(kc == KC - 1))
                    evac(ps, bb)

                if full:
                    for pc in range(2):
                        it = R.acquire(l, pc)
                        slot = it % NS
                        for cc in range(2):
                            c = pc * 2 + cc
                            fm_chunk(slot, cc * 128,
                                     lambda ps, bb, c=c: P.op("act", lambda e: e.activation(out=qT[:, c, 0:Gt], in_=ps[:, 0:Gt], func=AF.Copy),
                                                              reads=bb, writes=[b_qT[c]]))
                        R.release(it)
                it_kv = R.acquire(l, 2)
                s_kv = it_kv % NS
                b0 = blks[0]
                fm_chunk(s_kv, 0,
                         lambda ps, bb: P.op("act", lambda e: e.activation(out=kT[:, (b0 % 8) * 128:(b0 % 8) * 128 + Gt], in_=ps[:, 0:Gt], func=AF.Copy),
                                             reads=bb, writes=[b_kT[b % 8] for b in blks]))

                def do_v(b):
                    gp = gp_of[b]
                    ps, bb = bank1()
                    for kc in range(KC):
                        P.op("pe", lambda e, kc=kc: e.matmul(ps[:, 0:128], lhsT=xTa[:, kc, gp * 128:(gp + 1) * 128],
                                                             rhs=ring[:, s_kv, kc * 256 + 128:kc * 256 + 256],
                                                             start=(kc == 0), stop=(kc == KC - 1)),
                             reads=[b_ring[s_kv], b_xTa[gp]], writes=bb, signal=(kc == KC - 1))
                    outv = vA[:, b % 8, :].rearrange("p (j e) -> p j e", j=2)[:, :, 0:64]
                    onev = vA[:, b % 8, :].rearrange("p (j e) -> p j e", j=2)[:, :, 64:65]
                    inv = ps[:, 0:128].rearrange("p (j e) -> p j e", j=2)
                    if b == H - 1:
                        P.op("dve", lambda e: e.tensor_scalar(out=outv, in0=inv, scalar1=flag[:, 0:1], scalar2=None, op0=ALU.mult),
                             reads=bb + [b_flag], writes=[b_vA[b % 8]])
                        P.op("dve", lambda e: e.tensor_copy(out=onev, in_=flag[:, 0:1].unsqueeze(1).to_broadcast([128, 2, 1])),
                             reads=[b_flag], writes=[b_vA[b % 8]])
                    else:
                        P.op("dve", lambda e: e.tensor_copy(out=outv, in_=inv), reads=bb, writes=[b_vA[b % 8]])
                        P.op("dve", lambda e: e.memset(onev, 1.0), writes=[b_vA[b % 8]])

                if not full:
                    for b in blks:
                        do_v(b)
                    R.release(it_kv)
                    return False

                it_zu = [R.acquire(l, 3), R.acquire(l, 4)]
                it_zv = [R.acquire(l, 5), R.acquire(l, 6)]

                def tm_proj(gp, its):
                    ps, bb = bank1()
                    for q, it in enumerate(its):
                        s = it % NS
                        for kcl in range(4):
                            P.op("pe", lambda e, kcl=kcl, s=s, q=q: e.matmul(ps, lhsT=xTa[:, 4 * q + kcl, gp * 128:(gp + 1) * 128],
                                                                             rhs=ring[:, s, kcl * 512:(kcl + 1) * 512],
                                                                             start=(q == 0 and kcl == 0), stop=(q == 1 and kcl == 3)),
                                 reads=[b_ring[s], b_xTa[gp]], writes=bb, signal=(q == 1 and kcl == 3))
                    return ps, bb

                for b in blks:
                    do_v(b)

                stt = {}

                def stage_A(b):
                    gp = gp_of[b]
                    sd = stt[b] = {}
                    E = []
                    for j in range(2):
                        stmp, bl_stmp = stmp_pool.get()
                        for kb, kblk in enumerate((b - 1, b)):
                            ps, bb = bank1()
                            P.op("pe", lambda e, j=j, kblk=kblk, ps=ps: e.matmul(ps, lhsT=kT[64 * j:64 * j + 64, (kblk % 8) * 128:(kblk % 8 + 1) * 128],
                                                                                 rhs=qT[64 * j:64 * j + 64, :, gp * 128:(gp + 1) * 128],
                                                                                 start=True, stop=True),
                                 reads=[b_kT[kblk % 8]] + b_qT, writes=bb)
                            P.op("dve", lambda e, j=j, kb=kb, ps=ps, stmp=stmp: e.scalar_tensor_tensor(
                                out=stmp[:, kb * 512:(kb + 1) * 512], in0=ps, scalar=0.125,
                                in1=bias[:, kb * 1024 + j * 512:kb * 1024 + (j + 1) * 512], op0=ALU.mult, op1=ALU.add),
                                reads=bb + [b_bias], writes=bl_stmp)
                        Ej, bl_Ej = E_pool.get()
                        P.op("act", lambda e, Ej=Ej, stmp=stmp: e.activation(out=Ej, in_=stmp, func=AF.Exp), reads=bl_stmp, writes=bl_Ej)
                        E.append((Ej, bl_Ej))
                    sd["E"] = E
                    gx, bl_gx = gx_pool.get()
                    sd["gx"] = (gx, bl_gx)
                    ps_zu, bb_zu = tm_proj(gp, it_zu)
                    P.op("act", lambda e: e.activation(out=gx[:, 0:512], in_=ps_zu, func=AF.Gelu_apprx_tanh), reads=bb_zu, writes=bl_gx)
                    ps_zv, bb_zv = tm_proj(gp, it_zv)
                    P.op("act", lambda e: e.activation(out=gx[:, 512:1024], in_=ps_zv, func=AF.Gelu_apprx_tanh), reads=bb_zv, writes=bl_gx)

                def stage_B(b):
                    sd = stt[b]
                    gx, bl_gx = sd["gx"]
                    xv = gx[:, 512:1024]
                    po = []
                    for j in range(2):
                        Ej, bl_Ej = sd["E"][j]
                        ps, bb = bank1()
                        for c in range(4):
                            P.op("pe", lambda e, j=j, c=c, ps=ps, Ej=Ej: e.matmul(ps[:, c * 65:(c + 1) * 65], lhsT=Ej[:, c * 128:(c + 1) * 128],
                                                                                  rhs=vA[:, (b - 1) % 8, 65 * j:65 * j + 65], start=True, stop=False),
                                 reads=bl_Ej + [b_vA[(b - 1) % 8]], writes=bb, signal=False)
                            P.op("pe", lambda e, j=j, c=c, ps=ps, Ej=Ej: e.matmul(ps[:, c * 65:(c + 1) * 65], lhsT=Ej[:, 512 + c * 128:512 + (c + 1) * 128],
                                                                                  rhs=vA[:, b % 8, 65 * j:65 * j + 65], start=False, stop=True),
                                 reads=bl_Ej + [b_vA[b % 8]], writes=bb, signal=(c == 3))
                        po.append((ps, bb))
                    st6, b_st6 = stat.get()
                    P.op("dve", lambda e: e.bn_stats(out=st6[:, 0:6], in_=xv), reads=bl_gx, writes=[b_st6])
                    mv, b_mv = stat.get()
                    P.op("dve", lambda e: e.bn_aggr(out=mv[:, 0:2], in_=st6[:, 0:6]), reads=[b_st6], writes=[b_mv])
                    rs_ln, b_rs_ln = rstd_from(mv[:, 1:2], b_mv, 1.0)
                    den, b_den = stat.get()
                    for j in range(2):
                        ps, bb = po[j]
                        P.op("dve", lambda e, j=j, ps=ps: e.tensor_tensor(out=den[:, 4 * j:4 * j + 4].unsqueeze(2),
                                                                          in0=ps[:, 0:260].rearrange("p (c e) -> p c e", e=65)[:, :, 64:65],
                                                                          in1=esink[:, 4 * j:4 * j + 4].unsqueeze(2), op=ALU.add),
                             reads=bb + [b_esink], writes=[b_den])
                    rden, b_rden = stat.get()
                    P.op("dve", lambda e: e.reciprocal(out=rden[:, 0:8], in_=den[:, 0:8]), reads=[b_den], writes=[b_rden])
                    at, bl_at = at_pool.get()
                    for j in range(2):
                        ps, bb = po[j]
                        P.op("dve", lambda e, j=j, ps=ps: e.tensor_tensor(out=at[:, j * 256:(j + 1) * 256].rearrange("p (c d) -> p c d", c=4),
                                                                          in0=ps[:, 0:260].rearrange("p (c e) -> p c e", e=65)[:, :, 0:64],
                                                                          in1=rden[:, 4 * j:4 * j + 4].unsqueeze(2).to_broadcast([128, 4, 64]),
                                                                          op=ALU.mult),
                             reads=bb + [b_rden], writes=bl_at)
                    ss_a, b_ss_a = sumsq(at[:, 0:512], bl_at, 512)
                    rs_a, b_rs_a = rstd_from(ss_a, b_ss_a, 512.0)
                    sd["at"] = (at, bl_at, rs_a, b_rs_a)
                    P.op("dve", lambda e: e.tensor_scalar(out=xv, in0=xv, scalar1=mv[:, 0:1], scalar2=rs_ln,
                                                          op0=ALU.subtract, op1=ALU.mult), reads=bl_gx + [b_mv, b_rs_ln], writes=bl_gx)
                    P.op("dve", lambda e: e.tensor_tensor(out=xv, in0=xv, in1=rowv[:, 2048:2560], op=ALU.mult),
                         reads=bl_gx + [b_rowv], writes=bl_gx)
                    vh, bl_vh = vh_pool.get()
                    P.op("dve", lambda e: e.tensor_tensor(out=vh[:, 0:512], in0=xv, in1=rowv[:, 2560:3072], op=ALU.add),
                         reads=bl_gx + [b_rowv], writes=bl_vh)
                    sd["vh"] = (vh, bl_vh)

                def stage_C(b):
                    gp = gp_of[b]
                    sd = stt.pop(b)
                    gx, bl_gx = sd["gx"]
                    gu = gx[:, 0:512]
                    xv = gx[:, 512:1024]
                    vh, bl_vh = sd["vh"]
                    at, bl_at, rs_a, b_rs_a = sd["at"]
                    heads, b_heads = headsp.get()
                    ps_sp, bb_sp = bank1()
                    for hh in range(8):
                        P.op("pe", lambda e, hh=hh: e.matmul(ps_sp[:, hh * 64:(hh + 1) * 64], lhsT=wsT[:, hh, :], rhs=vh[:, hh * 64:(hh + 1) * 64],
                                                             start=True, stop=True),
                             reads=[b_wsT] + bl_vh, writes=bb_sp, signal=(hh == 7))
                    P.op("act", lambda e: e.activation(out=heads[:, 0:512], in_=at[:, 0:512], func=AF.Copy, scale=rs_a),
                         reads=bl_at + [b_rs_a], writes=[b_heads])
                    P.op("dve", lambda e: e.tensor_tensor(out=xv.rearrange("p (h c) -> p h c", h=8),
                                                          in0=ps_sp.rearrange("p (h c) -> p h c", h=8),
                                                          in1=bsT[:, 0:8].unsqueeze(2).to_broadcast([128, 8, 64]), op=ALU.add),
                         reads=bb_sp + [b_bsT] + bl_gx, writes=bl_gx)
                    P.op("dve", lambda e: e.tensor_tensor(out=gu, in0=gu, in1=xv, op=ALU.mult),
                         reads=bl_gx, writes=bl_gx)
                    ss_g, b_ss_g = sumsq(gu, bl_gx, 512)
                    rs_g, b_rs_g = rstd_from(ss_g, b_ss_g, 512.0)
                    P.op("act", lambda e: e.activation(out=heads[:, 512:1024], in_=gu, func=AF.Copy, scale=rs_g),
                         reads=bl_gx + [b_rs_g], writes=[b_heads])
                    transpose_to(heads, b_heads, xTb, b_xTb[gp], gp, 8)

                nf = len(full)
                for step in range(nf + 2):
                    if step < nf:
                        stage_A(full[step])
                    if 0 <= step - 1 < nf:
                        stage_B(full[step - 1])
                    if 0 <= step - 2 < nf:
                        stage_C(full[step - 2])
                for it in [it_kv] + it_zu + it_zv:
                    R.release(it)

                hoisted = False
                fin_m1 = None
                if nxt is not None:
                    fin_m1 = norm_transpose_group(nxt, list(range(len(nxt))), xTa, [b_xTa[i] for i in range(len(nxt))], 0, len(nxt), defer=True)
                    hoisted = True

                it_wo = [R.acquire(l, 7 + i) for i in range(4)]

                def m6_block(b):
                    gp = gp_of[b]
                    ps2, bb2 = bank2()
                    for i, it in enumerate(it_wo):
                        s = it % NS
                        c, q = i // 2, i % 2
                        for kcl in range(4):
                            P.op("pe", lambda e, kcl=kcl, s=s, c=c, q=q: e.matmul(ps2[:, c * 512:(c + 1) * 512], lhsT=xTb[:, 4 * q + kcl, gp * 128:(gp + 1) * 128],
                                                                                  rhs=ring[:, s, kcl * 512:(kcl + 1) * 512],
                                                                                  start=(q == 0 and kcl == 0), stop=(q == 1 and kcl == 3)),
                                 reads=[b_ring[s], b_xTb[gp]], writes=[bb2[c]], signal=(q == 1 and kcl == 3))
                    residual_epilogue(ps2, bb2, b, 0)

                for b in full:
                    m6_block(b)
                for it in it_wo:
                    R.release(it)
                if fin_m1 is not None:
                    fin_m1()

                return hoisted

            def part_ffn():
                if not pre_done:
                    norm_transpose_group(full, [gp_of[b] for b in full], xTa, [b_xTa[gp_of[b]] for b in full], 16, 1)
                c0 = gp_of[full[0]] * 128
                Ft = len(full) * 128
                b_xf = [b_xTa[gp_of[b]] for b in full]

                def f2_chunk(ci, sub, it_g, it_u):
                    pss = []
                    for it in (it_g, it_u):
                        s = it % NS
                        ps, bb = bank1()
                        for kc in range(KC):
                            P.op("pe", lambda e, kc=kc, s=s, ps=ps: e.matmul(ps[:, 0:Ft], lhsT=ring[:, s, kc * 256 + sub * 128:kc * 256 + sub * 128 + 128],
                                                                             rhs=xTa[:, kc, c0:c0 + Ft], start=(kc == 0), stop=(kc == KC - 1)),
                                 reads=[b_ring[s]] + b_xf, writes=bb, signal=(kc == KC - 1))
                        pss.append((ps, bb))
                    sl, b_sl = scr32.get()
                    ps_g, bb_g = pss[0]
                    ps_u, bb_u = pss[1]
                    P.op("act", lambda e: e.activation(out=sl[:, 0:Ft], in_=ps_g[:, 0:Ft], func=AF.Silu), reads=bb_g, writes=[b_sl])
                    P.op("dve", lambda e: e.tensor_tensor(out=actT[:, ci, c0:c0 + Ft], in0=ps_u[:, 0:Ft], in1=sl[:, 0:Ft], op=ALU.mult),
                         reads=bb_u + [b_sl], writes=[b_actT[ci]])

                for i in range(11):
                    it_g = R.acquire(l, 11 + 2 * i)
                    it_u = R.acquire(l, 12 + 2 * i)
                    for sub in range(2):
                        f2_chunk(2 * i + sub, sub, it_g, it_u)
                    conv_pump(2)
                    R.release(it_g)
                    R.release(it_u)

                hoisted = False
                fin_f1 = None
                if nxt is not None:
                    nblks, nfull = nxt
                    npos = {b: i for i, b in enumerate(nblks)}
                    fin_f1 = norm_transpose_group(nfull, [npos[b] for b in nfull], xTa, [b_xTa[npos[b]] for b in nfull], 16, len(nfull), d16_pool, defer=True)
                    hoisted = True

                acc = [bank2() for _ in full]
                for i in range(11):
                    it = R.acquire(l, 33 + i)
                    s = it % NS
                    for bi, b in enumerate(full):
                        gp = gp_of[b]
                        ps2, bb2 = acc[bi]
                        for kcl in range(2):
                            for half in range(2):
                                P.op("pe", lambda e, kcl=kcl, half=half, s=s, ps2=ps2, gp=gp, i=i: e.matmul(
                                    ps2[:, half * 512:(half + 1) * 512], lhsT=actT[:, 2 * i + kcl, gp * 128:(gp + 1) * 128],
                                    rhs=ring[:, s, kcl * 1024 + half * 512:kcl * 1024 + (half + 1) * 512],
                                    start=(i == 0 and kcl == 0), stop=(i == 10 and kcl == 1)),
                                    reads=[b_ring[s], b_actT[2 * i + kcl]], writes=[bb2[half]],
                                    signal=(i == 10 and kcl == 1 and half == 1))
                    R.release(it)
                for bi, b in enumerate(full):
                    ps2, bb2 = acc[bi]
                    residual_epilogue(ps2, bb2, b, 1024)
                if fin_f1 is not None:
                    fin_f1()

                return hoisted

            def part_ple():
                pti = do_group.pt_i % 2
                do_group.pt_i += 1
                pt, b_pt = pTt[pti], b_pTt[pti]
                t0 = full[0] * 128
                P.dma("pool", lambda e: e.dma_start(out=pt[:, :, 0:Ft], in_=pT_d[l, :, :, t0:t0 + Ft].rearrange("k p t -> p k t")),
                      f"pT{pti}", writes=b_pt)
                xTp, b_xTp = (xTb, b_xTb) if gidx % 2 == 0 else (xTa, b_xTa)
                xTn, b_xTn = (xTa, b_xTa) if gidx % 2 == 0 else (xTb, b_xTb)
                if not pre_done:
                    norm_transpose_group(full, [gp_of[b] for b in full], xTp, [b_xTp[gp_of[b]] for b in full], 24, 1)
                hoisted = False
                fin_p1 = None
                if nxt is not None:
                    nblks, nfull = nxt
                    npos = {b: i for i, b in enumerate(nblks)}
                    fin_p1 = norm_transpose_group(nfull, [npos[b] for b in nfull], xTn, [b_xTn[npos[b]] for b in nfull], 24, len(nfull), d16_pool, defer=True)
                    hoisted = True
                it_ple = R.acquire(l, 44)
                s_ple = it_ple % NS
                it_pg = [R.acquire(l, 45 + i) for i in range(4)]

                def p_block(bi, b):
                    gp = gp_of[b]
                    psg, bbg = bank2()
                    for i, it in enumerate(it_pg):
                        s = it % NS
                        c, q = i // 2, i % 2
                        for kcl in range(4):
                            P.op("pe", lambda e, kcl=kcl, s=s, c=c, q=q: e.matmul(psg[:, c * 512:(c + 1) * 512], lhsT=xTp[:, 4 * q + kcl, gp * 128:(gp + 1) * 128],
                                                                                  rhs=ring[:, s, kcl * 512:(kcl + 1) * 512],
                                                                                  start=(q == 0 and kcl == 0), stop=(q == 1 and kcl == 3)),
                                 reads=[b_ring[s], b_xTp[gp]], writes=[bbg[c]], signal=(q == 1 and kcl == 3))
                    psp, bbp = bank2()
                    for half in range(2):
                        for kc in range(2):
                            P.op("pe", lambda e, kc=kc, half=half: e.matmul(psp[:, half * 512:(half + 1) * 512], lhsT=pt[:, kc, bi * 128:(bi + 1) * 128],
                                                                            rhs=ring[:, s_ple, kc * 1024 + half * 512:kc * 1024 + (half + 1) * 512],
                                                                            start=(kc == 0), stop=(kc == 1)),
                                 reads=[b_ring[s_ple]] + b_pt, writes=[bbp[half]], signal=(kc == 1))
                    sg, b_sg = scr32.get()
                    P.op("act", lambda e: e.activation(out=sg[:], in_=psg, func=AF.Sigmoid), reads=bbg, writes=[b_sg])
                    P.op("dve", lambda e: e.tensor_tensor(out=sg[:], in0=psp, in1=sg[:], op=ALU.mult), reads=bbp + [b_sg], writes=[b_sg])
                    P.op("dve", lambda e: e.tensor_tensor(out=h[:, b, :], in0=h[:, b, :], in1=sg[:], op=ALU.add),
                         reads=[b_sg, b_h[b]], writes=[b_h[b]])
                    if last_layer and b >= H:
                        P.dma("sp", lambda e: e.dma_start(out=y_d[(b - H) * 128:(b - H + 1) * 128, :], in_=h[:, b, :]),
                              "yout", reads=[b_h[b]])

                for bi, b in enumerate(full):
                    p_block(bi, b)
                if fin_p1 is not None:
                    fin_p1()
                R.release(it_ple)
                for it in it_pg:
                    R.release(it)
                return hoisted

            Gt_unused = None
            return {"mix": part_mix, "ffn": part_ffn, "ple": part_ple}[phase]()

        do_group.pt_i = 0

        for l in range(L):
            layer_setup(l)
            kvb, groups = layer_groups(l)
            if l + 1 < L:
                conv_layer(l + 1)
            fulls = [[b for b in blks if b > kvb] for blks in groups]
            fg = [gi for gi in range(len(groups)) if fulls[gi]]
            done = False
            for gi, blks in enumerate(groups):
                nxt = groups[gi + 1] if gi + 1 < len(groups) else None
                done = do_group(l, kvb, blks, l == L - 1, "mix", nxt, done, gi)
            done = False
            for k, gi in enumerate(fg):
                nxt = (groups[fg[k + 1]], fulls[fg[k + 1]]) if k + 1 < len(fg) else None
                done = do_group(l, kvb, groups[gi], l == L - 1, "ffn", nxt, done, k)
            conv_pump(10 ** 6)
            done = False
            for k, gi in enumerate(fg):
                nxt = (groups[fg[k + 1]], fulls[fg[k + 1]]) if k + 1 < len(fg) else None
                done = do_group(l, kvb, groups[gi], l == L - 1, "ple", nxt, done, k)
        P.wait_all("sp", b_h)
        P.emit()
    return nc


_PROG_CACHE = {}


def _get_prog(L, H):
    if (L, H) not in _PROG_CACHE:
        _PROG_CACHE[(L, H)] = build_program(L, H)
    return _PROG_CACHE[(L, H)]


def _constants():
    ident = np.eye(128, dtype=np.float32)
    s = np.arange(128)[:, None].astype(np.float32)
    t = np.arange(128)[None, :].astype(np.float32)
    slopes = np.exp2(-8.0 * (np.arange(8, dtype=np.float32) + 1.0) / 8.0).astype(np.float32)
    bias = np.zeros((128, 2, 8, 128), np.float32)
    for hh in range(8):
        dprev = t + 128.0 - s
        bias[:, 0, hh, :] = np.where(dprev < 128, -slopes[hh] * dprev, -30000.0)
        dcur = t - s
        bias[:, 1, hh, :] = np.where(dcur >= 0, -slopes[hh] * dcur, -30000.0)
    tril = (np.arange(128)[None, :] <= np.arange(128)[:, None]).astype(np.float32)
    return ident, bias.reshape(128, 2048), tril


def _run(hfull, layers, H, inp):
    L = len(layers)
    NB = H + OWN
    nc = _get_prog(L, H)
    ident, bias, tril = _constants()
    sl = slice(layers[0], layers[-1] + 1)
    c32 = lambda a: np.ascontiguousarray(a, dtype=np.float32)
    colg = np.stack([
        np.concatenate([
            inp["ln_mix_pre"][l].reshape(8, 128).T,
            np.concatenate([inp["g_attn_out"][l], inp["g_gm_out"][l]]).reshape(8, 128).T,
            inp["ln_ffn_pre"][l].reshape(8, 128).T,
            inp["ln_ple_gate"][l].reshape(8, 128).T], axis=1) for l in layers])
    rowv = np.stack([np.concatenate([inp["ln_mix_post"][l], inp["ln_ffn_post"][l], inp["gm_ln_g"][l], inp["gm_ln_b"][l]])[None, :]
                     for l in layers])
    bsT = np.stack([inp["gm_bs"][l].T for l in layers])
    sinks = np.stack([inp["attn_sinks"][l][None, :] for l in layers])
    shared = {
        "w_in": c32(inp["w_in"][sl]), "w_out": c32(inp["w_out"][sl]), "w_gate": c32(inp["w_ffn_gate"][sl]),
        "w_up": c32(inp["w_ffn_up"][sl]), "w_down": c32(inp["w_ffn_down"][sl]), "w_ple": c32(inp["w_ple"][sl]),
        "w_pg": c32(inp["w_ple_gate"][sl]), "gm_ws": c32(inp["gm_ws"][sl]),
        "colg": c32(colg), "rowv": c32(rowv), "bsT": c32(bsT), "sinks": c32(sinks),
        "ident": ident, "bias": bias, "tril": tril,
    }
    in_maps = []
    for c in range(NCORES):
        bt, j = c // 4, c % 4
        t0 = j * 2048 - H * 128
        xs = np.zeros((NB * 128, D), np.float32)
        ps = np.zeros((L, 2, 128, NB * 128), np.float32)
        lo = max(t0, 0)
        xs[lo - t0:] = hfull[bt, lo:(j + 1) * 2048]
        pp = inp["p"][sl, bt, lo:(j + 1) * 2048, :]
        ps[:, :, :, lo - t0:] = pp.transpose(0, 2, 1).reshape(L, 2, 128, -1)
        m = dict(shared)
        m["x"] = xs
        m["pT"] = ps
        m["flag"] = np.full((128, 1), 0.0 if j == 0 else 1.0, np.float32)
        in_maps.append(m)
    res = run_bass_kernel_spmd(nc, in_maps, core_ids=list(range(NCORES)))
    out = np.empty((2, 8192, D), np.float32)
    for c in range(NCORES):
        bt, j = c // 4, c % 4
        out[bt, j * 2048:(j + 1) * 2048] = res.results[c]["y"]
    return out


def kernel(**inputs):
    inp = {k: np.asarray(v) for k, v in inputs.items()}
    hcur = np.asarray(inp["x"], dtype=np.float32)
    if FUSED:
        return _run(hcur, [0, 1, 2, 3], 4, inp)
    for l in range(4):
        hcur = _run(hcur, [l], 1, inp)
    return hcur
```

```python
import numpy as np
import concourse.bass as bass
import concourse.mybir as mybir
from concourse.bass_utils import run_bass_kernel_spmd
from contextlib import ExitStack

F32 = mybir.dt.float32
BF16 = mybir.dt.bfloat16
AF = mybir.ActivationFunctionType
ALU = mybir.AluOpType
AX = mybir.AxisListType

D = 1024
KC = 8
DFF = 2816
NFC = 22
EPS = 1e-6
NCORES = 8
OWN = 16
FUSED = True
NS = 9
NP = 49
NHEADS = 1


class Buf:
    __slots__ = ("name", "w", "r", "psum", "last")

    def __init__(self, name, psum=False):
        self.name = name
        self.psum = psum
        self.last = -1
        self.w = None
        self.r = []


class Prog:
    ENGS = ("pe", "act", "dve", "pool", "sp")

    def __init__(self, nc, stack):
        self.nc = nc
        self.stack = stack
        self.streams = {e: [] for e in self.ENGS}
        self.sems = {}
        for e in self.ENGS:
            self.sems[e] = stack.enter_context(nc.semaphore("sem_" + e))
        self.cnt = {e: 0 for e in self.ENGS}
        self.idx = {e: 0 for e in self.ENGS}
        self.unsig = {e: False for e in self.ENGS}
        self.known = {e: {} for e in self.ENGS}
        self.dma_sems = {}
        self.dma_cnt = {}
        self.gidx = 0

    def dma_sem(self, name):
        if name not in self.dma_sems:
            self.dma_sems[name] = self.stack.enter_context(self.nc.semaphore("dsem_" + name))
            self.dma_cnt[name] = 0
            self.sems["dma:" + name] = self.dma_sems[name]
        return name

    def _need(self, eng, dep):
        key, val, deng, didx = dep
        if deng == eng:
            if eng in ("pe", "sp"):
                return
        if self.known[eng].get(key, 0) >= val:
            return
        self.known[eng][key] = val
        self.streams[eng].append(("wait", key, val))

    def _deps(self, eng, reads, writes):
        for b in reads:
            if b.w is not None:
                self._need(eng, b.w)
            if b.psum:
                for r in b.r:
                    if r[2] != eng:
                        self._need(eng, r)
        for b in writes:
            if b.w is not None:
                self._need(eng, b.w)
            for r in b.r:
                self._need(eng, r)

    def op(self, eng, fn, reads=(), writes=(), signal=True):
        self._deps(eng, reads, writes)
        if signal:
            self.cnt[eng] += 1
            self.unsig[eng] = False
        else:
            self.unsig[eng] = True
        tick = self.cnt[eng] if signal else self.cnt[eng] + 1
        me = (eng, tick, eng, self.idx[eng])
        self.idx[eng] += 1
        self.streams[eng].append(("op", fn, signal))
        self.gidx += 1
        for b in reads:
            b.last = self.gidx
            if not b.r or b.r[-1][:2] != me[:2]:
                b.r.append(me)
        for b in writes:
            b.last = self.gidx
            b.w = me
            b.r = []

    def dma(self, eng, fn, semname, reads=(), writes=()):
        self.dma_sem(semname)
        self._deps(eng, reads, writes)
        self.dma_cnt[semname] += 16
        key = "dma:" + semname
        me = (key, self.dma_cnt[semname], "dma", -1)
        self.idx[eng] += 1
        self.streams[eng].append(("dma", fn, semname))
        for b in reads:
            b.r.append(me)
        for b in writes:
            b.w = me
            b.r = []

    def force_signal(self, eng):
        if not self.unsig[eng]:
            return
        st = self.streams[eng]
        for k in range(len(st) - 1, -1, -1):
            if st[k][0] == "op":
                assert not st[k][2]
                st[k] = ("op", st[k][1], True)
                break
        self.cnt[eng] += 1
        self.unsig[eng] = False

    def wait_all(self, eng, bufs):
        best = {}
        for b in bufs:
            for dep in ([b.w] if b.w is not None else []) + list(b.r):
                if dep[0] not in best or best[dep[0]][1] < dep[1]:
                    best[dep[0]] = dep
        for dep in best.values():
            self._need(eng, dep)

    def emit(self):
        for e in ("pe", "act", "dve", "pool"):
            assert not self.unsig[e], f"engine {e} ends with unsignalled instruction"
        nc = self.nc
        with nc.Block() as block:
            def run(engname):
                def body(eng):
                    mysem = self.sems[engname]
                    for ent in self.streams[engname]:
                        if ent[0] == "wait":
                            eng.wait_ge(self.sems[ent[1]], ent[2])
                        elif ent[0] == "op":
                            ins = ent[1](eng)
                            if ent[2]:
                                ins.then_inc(mysem, 1)
                        else:
                            ins = ent[1](eng)
                            ins.then_inc(self.dma_sems[ent[2]], 16)
                return body
            block.tensor(run("pe"))
            block.scalar(run("act"))
            block.vector(run("dve"))
            block.gpsimd(run("pool"))
            block.sync(run("sp"))


class RR:
    def __init__(self, items):
        self.items = items
        self.i = 0

    def get(self):
        it = self.items[self.i % len(self.items)]
        self.i += 1
        return it


def build_program(L, H, OWN=OWN):
    NB = H + OWN
    NT = NB * 128
    nc = bass.Bass("TRN2", target_bir_lowering=False)
    dt = lambda name, shape, dtype=F32, kind="ExternalInput": nc.dram_tensor(name, shape, dtype, kind=kind)
    x_d = dt("x", [NT, D])
    pT_d = dt("pT", [L, 2, 128, NT])
    w_in_d = dt("w_in", [L, D, 1792])
    w_out_d = dt("w_out", [L, D, D])
    w_gate_d = dt("w_gate", [L, D, DFF])
    w_up_d = dt("w_up", [L, D, DFF])
    w_down_d = dt("w_down", [L, DFF, D])
    w_ple_d = dt("w_ple", [L, 256, D])
    w_pg_d = dt("w_pg", [L, D, D])
    ws_d = dt("gm_ws", [L, 8, 128, 128])
    colg_d = dt("colg", [L, 128, 32])
    rowv_d = dt("rowv", [L, 1, 3072])
    bsT_d = dt("bsT", [L, 128, 8])
    sinks_d = dt("sinks", [L, 1, 8])
    ident_d = dt("ident", [128, 128])
    bias_d = dt("bias", [128, 2048])
    tril_d = dt("tril", [128, 128])
    flag_d = dt("flag", [128, 1])
    y_d = dt("y", [OWN * 128, D], F32, "ExternalOutput")
    wsc_d = dt("wsc", [L, NP, 128, 2048], BF16, "Internal")

    with ExitStack() as st:
        P = Prog(nc, st)
        sb = lambda name, shape, dtype: st.enter_context(nc.sbuf_tensor(name, shape, dtype))

        h = sb("h", [128, NB, D], F32)
        b_h = [Buf(f"h{i}") for i in range(NB)]
        ring = sb("ring", [128, NS, 2048], BF16)
        b_ring = [Buf(f"ring{i}") for i in range(NS)]
        xTa = sb("xTa", [128, KC, 512], BF16)
        b_xTa = [Buf(f"xTa{i}") for i in range(4)]
        xTb = sb("xTb", [128, KC, 512], BF16)
        b_xTb = [Buf(f"xTb{i}") for i in range(4)]
        qT = sb("qT", [128, 4, 512], BF16)
        b_qT = [Buf(f"qT{i}") for i in range(4)]
        kT = sb("kT", [128, 8 * 128], BF16)
        b_kT = [Buf(f"kT{i}") for i in range(8)]
        vA = sb("vA", [128, 8, 130], BF16)
        b_vA = [Buf(f"vA{i}") for i in range(8)]
        actT = sb("actT", [128, NFC, 512], BF16)
        b_actT = [Buf(f"actT{i}") for i in range(NFC)]
        pTt = [qT[:, 2 * i:2 * i + 2, :] for i in range(2)]
        b_pTt = [[b_qT[2 * i], b_qT[2 * i + 1]] for i in range(2)]
        scr32 = RR([(sb(f"s32_{i}", [128, D], F32), Buf(f"s32_{i}")) for i in range(4)])
        scr16 = RR([(sb(f"s16_{i}", [128, D], BF16), Buf(f"s16_{i}")) for i in range(3)])
        headsp = RR([(sb(f"heads_{i}", [128, D], BF16), Buf(f"heads_{i}")) for i in range(NHEADS)])
        stat = RR([(sb(f"st_{i}", [128, 8], F32), Buf(f"st_{i}")) for i in range(24)])
        def _a32(i):
            return (actT[:, 4 * i:4 * i + 4, :].rearrange("p a b -> p (a b)").bitcast(F32), b_actT[4 * i:4 * i + 4])

        def _a16(i):
            return (actT[:, 16 + 2 * i:18 + 2 * i, :].rearrange("p a b -> p (a b)"), b_actT[16 + 2 * i:18 + 2 * i])

        d32 = [(t_[:], [b_]) for (t_, b_) in scr32.items]
        d16 = [(t_[:], [b_]) for (t_, b_) in scr16.items]
        gx_pool = RR([_a32(0), _a32(1), _a32(2)])
        stmp_pool = RR([_a32(3), d32[0], d32[1]])
        at_pool = RR([d32[2], d32[3]])
        E_pool = RR([_a16(0), _a16(1), _a16(2), d16[0]])
        vh_pool = RR([d16[1], d16[2]])
        d16_pool = RR([d16[0], d16[1], d16[2]])
        junk = RR([(sb(f"junk_{i}", [128, D], BF16), Buf(f"junk_{i}")) for i in range(1)])
        ident = sb("identb", [128, 128], BF16); b_ident = Buf("ident")
        bias = sb("biasb", [128, 2048], BF16); b_bias = Buf("bias")
        tril = sb("trilb", [128, 128], F32); b_tril = Buf("tril")
        flag = sb("flagb", [128, 1], F32); b_flag = Buf("flag")
        mhalf = sb("mhalf", [128, 1], F32); b_mhalf = Buf("mhalf")
        colg = sb("colgb", [128, 32], F32); b_colg = Buf("colg")
        rowv = sb("rowvb", [128, 3072], F32); b_rowv = Buf("rowv")
        bsT = sb("bsTb", [128, 8], F32); b_bsT = Buf("bsT")
        esink = sb("esink", [128, 8], F32); b_esink = Buf("esink")
        wsT = sb("wsT", [128, 8, 128], BF16); b_wsT = Buf("wsT")

        psd = [st.enter_context(nc.psum_tensor(f"ps{i}", [128, 1024], F32)) for i in range(4)]
        b_bank = [Buf(f"bank{i}", True) for i in range(8)]
        bank_ptr = [0]

        def bank1():
            i = min(range(8), key=lambda k: b_bank[k].last)
            b_bank[i].last = P.gidx + 0.5
            return psd[i // 2][:, (i % 2) * 512:(i % 2) * 512 + 512], [b_bank[i]]

        def bank2():
            d = min(range(4), key=lambda k: max(b_bank[2 * k].last, b_bank[2 * k + 1].last))
            b_bank[2 * d].last = b_bank[2 * d + 1].last = P.gidx + 0.5
            return psd[d][:, :], [b_bank[2 * d], b_bank[2 * d + 1]]

        b_wsc = {}

        conv_pending = []

        def conv_pump(n):
            for _ in range(min(n, len(conv_pending))):
                conv_pending.pop(0)()

        def conv_layer(l):
            def cv(batch, piece, dst_view, src_ap):
                def thunk():
                    key = (l, batch)
                    if key not in b_wsc:
                        b_wsc[key] = Buf(f"wsc{l}{batch}")
                    P.dma("pool", lambda e, d=dst_view, s=src_ap: e.dma_start(out=d, in_=s),
                          f"cv{l}{batch}", writes=[])
                    b_wsc[key].w = ("dma:" + f"cv{l}{batch}", P.dma_cnt[f"cv{l}{batch}"], "dma", -1)
                conv_pending.append(thunk)

            def colpiece(batch, piece, wd, c0, ncols=256, dst_off=0, kcn=KC):
                dst = wsc_d[l, piece, :, :].rearrange("p (k n) -> p k n", k=kcn)[:, :, dst_off:dst_off + ncols]
                src = wd[l, :, :].rearrange("(k p) n -> p k n", p=128)[:, :, c0:c0 + ncols]
                cv(batch, piece, dst, src)

            def kpiece(batch, piece, wd, c0, q):
                dst = wsc_d[l, piece, :, :].rearrange("p (k n) -> p k n", k=4)
                src = wd[l, :, :].rearrange("(k p) n -> p k n", p=128)[:, 4 * q:4 * q + 4, c0:c0 + 512]
                cv(batch, piece, dst, src)

            for pc in range(2):
                for cc in range(2):
                    c = pc * 2 + cc
                    for j in range(2):
                        colpiece("A", pc, w_in_d, 64 * (c + 4 * j), 64, cc * 128 + j * 64)
            colpiece("A", 2, w_in_d, 512)
            for i in range(4):
                kpiece("A", 3 + i, w_in_d, 768 + 512 * (i // 2), i % 2)
            for i in range(4):
                kpiece("A", 7 + i, w_out_d, 512 * (i // 2), i % 2)
            for i in range(11):
                colpiece("B", 11 + 2 * i, w_gate_d, 256 * i)
                colpiece("B", 12 + 2 * i, w_up_d, 256 * i)
            for i in range(11):
                dst = wsc_d[l, 33 + i, :, :].rearrange("p (k n) -> p k n", k=2)
                src = w_down_d[l, 256 * i:256 * i + 256, :].rearrange("(k p) n -> p k n", p=128)
                cv("B", 33 + i, dst, src)
            dst = wsc_d[l, 44, :, :].rearrange("p (k n) -> p k n", k=2)
            src = w_ple_d[l, :, :].rearrange("(k p) n -> p k n", p=128)
            cv("C", 44, dst, src)
            for i in range(4):
                kpiece("C", 45 + i, w_pg_d, 512 * (i // 2), i % 2)

        def piece_batch(piece):
            return "A" if piece < 11 else ("B" if piece < 44 else "C")

        def layer_groups(l):
            kvb = H - (L - l)
            groups = []
            g0 = (kvb // 4) * 4
            for s in range(g0, NB, 4):
                blks = [b for b in range(s, min(s + 4, NB)) if b >= kvb]
                groups.append(blks)
            return kvb, groups

        seq = []
        for l in range(L):
            kvb, groups = layer_groups(l)
            nfull = 0
            for blks in groups:
                full = [b for b in blks if b > kvb]
                if not full:
                    seq.append((l, 2))
                    continue
                nfull += 1
                seq += [(l, p) for p in range(0, 11)]
            for _ in range(nfull):
                seq += [(l, p) for p in range(11, 44)]
            seq += [(l, p) for p in range(44, 49)]

        class Ring:
            def __init__(self):
                self.load_ptr = 0
                self.use_ptr = 0
                self.released = [False] * len(seq)

            def pump(self):
                while self.load_ptr < len(seq) and (self.load_ptr < NS or self.released[self.load_ptr - NS]):
                    i = self.load_ptr
                    l, p = seq[i]
                    s = i % NS
                    P.dma("sp", lambda e, s=s, l=l, p=p: e.dma_start(out=ring[:, s, :], in_=wsc_d[l, p, :, :]),
                          f"ring{s}", reads=[b_wsc[(l, piece_batch(p))]], writes=[b_ring[s]])
                    self.load_ptr += 1

            def acquire(self, l, p):
                i = self.use_ptr
                assert seq[i] == (l, p), (seq[i], l, p)
                self.pump()
                assert self.load_ptr > i, "ring resident set exceeds NS"
                self.use_ptr += 1
                return i

            def release(self, i):
                P.force_signal("pe")
                self.released[i] = True
                self.pump()

        R = Ring()

        def rstd_from(ss_ap, b_ss, n, eps=EPS):
            t, b = stat.get()
            P.op("pool", lambda e: e.tensor_scalar(out=t[:, 0:1], in0=ss_ap, scalar1=1.0 / n, scalar2=eps,
                                                   op0=ALU.mult, op1=ALU.add), reads=[b_ss], writes=[b])
            P.op("pool", lambda e: e.tensor_tensor(out=t[:, 1:2], in0=t[:, 0:1], in1=mhalf[:, 0:1], op=ALU.pow),
                 reads=[b, b_mhalf], writes=[b])
            return t[:, 1:2], b

        def sumsq(in_ap, rd_bufs, ncols):
            t, b = stat.get()
            jt, b_jt = junk.get()
            P.op("act", lambda e: e.activation(out=jt[:, 0:ncols], in_=in_ap, func=AF.Square,
                                               accum_out=t[:, 0:1]), reads=rd_bufs, writes=[b, b_jt])
            return t[:, 0:1], b

        def transpose_to(src16, b_src, dstT, b_dst, gp, gsel):
            ps, bb = bank1()
            psb = ps.bitcast(BF16)
            for kc in range(KC):
                P.op("pe", lambda e, kc=kc: e.transpose(out=psb[:, kc * 128:(kc + 1) * 128],
                                                        in_=src16[:, kc * 128:(kc + 1) * 128], identity=ident[:]),
                     reads=(b_src if isinstance(b_src, list) else [b_src]) + [b_ident], writes=bb, signal=(kc == KC - 1))
            P.op("dve", lambda e: e.tensor_tensor(out=dstT[:, :, gp * 128:(gp + 1) * 128],
                                                  in0=psb.rearrange("p (k t) -> p k t", k=KC),
                                                  in1=colg[:, gsel:gsel + 8].unsqueeze(2).to_broadcast([128, KC, 128]),
                                                  op=ALU.mult),
                 reads=bb + [b_colg], writes=[b_dst])

        def norm_transpose_group(blist, gps, dstT, b_dsts, gsel, lag, a16pool=None):
            n = len(blist)
            rss = {}
            for step in range(n + lag):
                if step < n:
                    blk = blist[step]
                    ss, b_ss = sumsq(h[:, blk, :], [b_h[blk]], D)
                    rss[step] = rstd_from(ss, b_ss, D)
                i = step - lag
                if 0 <= i < n:
                    blk = blist[i]
                    rs, b_rs = rss[i]
                    a16, bl_a16 = (a16pool or E_pool).get()
                    if i % 2 == 0:
                        P.op("act", lambda e, a16=a16, blk=blk, rs=rs: e.activation(out=a16, in_=h[:, blk, :], func=AF.Copy, scale=rs),
                             reads=[b_h[blk], b_rs], writes=bl_a16)
                    else:
                        P.op("dve", lambda e, a16=a16, blk=blk, rs=rs: e.tensor_scalar(out=a16, in0=h[:, blk, :], scalar1=rs, scalar2=None, op0=ALU.mult),
                             reads=[b_h[blk], b_rs], writes=bl_a16)
                    transpose_to(a16, bl_a16, dstT, b_dsts[i], gps[i], gsel)

        def residual_epilogue(ps2, bb2, blk, goff):
            ss, b_ss = sumsq(ps2, bb2, D)
            rs, b_rs = rstd_from(ss, b_ss, D)
            t, b_t = scr32.get()
            P.op("dve", lambda e: e.scalar_tensor_tensor(out=t[:], in0=ps2, scalar=rs, in1=rowv[:, goff:goff + D],
                                                         op0=ALU.mult, op1=ALU.mult),
                 reads=bb2 + [b_rs, b_rowv], writes=[b_t])
            P.op("dve", lambda e: e.tensor_tensor(out=h[:, blk, :], in0=h[:, blk, :], in1=t[:], op=ALU.add),
                 reads=[b_t, b_h[blk]], writes=[b_h[blk]])

        conv_layer(0)
        conv_pump(10 ** 6)
        P.dma("sp", lambda e: e.dma_start(out=h[:, :, :], in_=x_d[:, :].rearrange("(b p) d -> p b d", p=128)),
              "xin", writes=b_h)
        P.dma("pool", lambda e: e.dma_start(out=ident[:], in_=ident_d[:, :]), "c_ident", writes=[b_ident])
        P.dma("pool", lambda e: e.dma_start(out=bias[:], in_=bias_d[:, :]), "c_bias", writes=[b_bias])
        P.dma("sp", lambda e: e.dma_start(out=tril[:], in_=tril_d[:, :]), "c_tril", writes=[b_tril])
        P.dma("sp", lambda e: e.dma_start(out=flag[:], in_=flag_d[:, :]), "c_flag", writes=[b_flag])
        P.op("pool", lambda e: e.memset(mhalf[:], -0.5), writes=[b_mhalf])

        def layer_setup(l):
            P.dma("sp", lambda e: e.dma_start(out=colg[:], in_=colg_d[l, :, :]), "l_colg", writes=[b_colg])
            P.dma("sp", lambda e: e.dma_start(out=rowv[:], in_=rowv_d[l, 0:1, :].partition_broadcast(128)),
                  "l_rowv", writes=[b_rowv])
            P.dma("sp", lambda e: e.dma_start(out=bsT[:], in_=bsT_d[l, :, :]), "l_bsT", writes=[b_bsT])
            P.dma("sp", lambda e: e.dma_start(out=esink[:], in_=sinks_d[l, 0:1, :].partition_broadcast(128)),
                  "l_sink", writes=[b_esink])
            P.op("act", lambda e: e.activation(out=esink[:], in_=esink[:], func=AF.Exp), reads=[b_esink], writes=[b_esink])
            wt, b_wt = scr32.get()
            P.dma("sp", lambda e: e.dma_start(out=wt[:].rearrange("p (h s) -> p h s", h=8),
                                              in_=ws_d[l, :, :, :].rearrange("h t s -> t h s")),
                  "l_ws", writes=[b_wt])
            wm, b_wm = scr16.get()
            P.op("dve", lambda e: e.tensor_tensor(out=wm[:].rearrange("p (h s) -> p h s", h=8),
                                                  in0=wt[:].rearrange("p (h s) -> p h s", h=8),
                                                  in1=tril[:].unsqueeze(1).to_broadcast([128, 8, 128]), op=ALU.mult),
                 reads=[b_wt, b_tril], writes=[b_wm])
            ps, bb = bank1()
            psb = ps.bitcast(BF16)
            for hh in range(8):
                P.op("pe", lambda e, hh=hh: e.transpose(out=psb[:, hh * 128:(hh + 1) * 128],
                                                        in_=wm[:, hh * 128:(hh + 1) * 128], identity=ident[:]),
                     reads=[b_wm, b_ident], writes=bb, signal=(hh == 7))
            P.op("act", lambda e: e.activation(out=wsT[:, :, :], in_=psb.rearrange("p (h t) -> p h t", h=8), func=AF.Copy),
                 reads=bb, writes=[b_wsT])

        def do_group(l, kvb, blks, last_layer, phase, nxt, pre_done, gidx):
            G = len(blks)
            Gt = G * 128
            full = [b for b in blks if b > kvb]
            gp_of = {b: i for i, b in enumerate(blks)}
            if full:
                c0 = gp_of[full[0]] * 128
                Ft = len(full) * 128
                b_xf = [b_xTa[gp_of[b]] for b in full]

            def part_mix():
                if not pre_done:
                    norm_transpose_group(blks, [gp_of[b] for b in blks], xTa, [b_xTa[gp_of[b]] for b in blks], 0, len(blks))
                b_xa = [b_xTa[gp_of[b]] for b in blks]

                def fm_chunk(slot, coff, evac):
                    ps, bb = bank1()
                    for kc in range(KC):
                        P.op("pe", lambda e, kc=kc: e.matmul(ps[:, 0:Gt], lhsT=ring[:, slot, kc * 256 + coff:kc * 256 + coff + 128],
                                                             rhs=xTa[:, kc, 0:Gt], start=(kc == 0), stop=(kc == KC - 1)),
                             reads=[b_ring[slot]] + b_xa, writes=bb, signal=(kc == KC - 1))
                    evac(ps, bb)

                if full:
                    for pc in range(2):
                        it = R.acquire(l, pc)
                        slot = it % NS
                        for cc in range(2):
                            c = pc * 2 + cc
                            fm_chunk(slot, cc * 128,
                                     lambda ps, bb, c=c: P.op("act", lambda e: e.activation(out=qT[:, c, 0:Gt], in_=ps[:, 0:Gt], func=AF.Copy),
                                                              reads=bb, writes=[b_qT[c]]))
                        R.release(it)
                it_kv = R.acquire(l, 2)
                s_kv = it_kv % NS
                b0 = blks[0]
                fm_chunk(s_kv, 0,
                         lambda ps, bb: P.op("act", lambda e: e.activation(out=kT[:, (b0 % 8) * 128:(b0 % 8) * 128 + Gt], in_=ps[:, 0:Gt], func=AF.Copy),
                                             reads=bb, writes=[b_kT[b % 8] for b in blks]))

                def do_v(b):
                    gp = gp_of[b]
                    ps, bb = bank1()
                    for kc in range(KC):
                        P.op("pe", lambda e, kc=kc: e.matmul(ps[:, 0:128], lhsT=xTa[:, kc, gp * 128:(gp + 1) * 128],
                                                             rhs=ring[:, s_kv, kc * 256 + 128:kc * 256 + 256],
                                                             start=(kc == 0), stop=(kc == KC - 1)),
                             reads=[b_ring[s_kv], b_xTa[gp]], writes=bb, signal=(kc == KC - 1))
                    outv = vA[:, b % 8, :].rearrange("p (j e) -> p j e", j=2)[:, :, 0:64]
                    onev = vA[:, b % 8, :].rearrange("p (j e) -> p j e", j=2)[:, :, 64:65]
                    inv = ps[:, 0:128].rearrange("p (j e) -> p j e", j=2)
                    if b == H - 1:
                        P.op("dve", lambda e: e.tensor_scalar(out=outv, in0=inv, scalar1=flag[:, 0:1], scalar2=None, op0=ALU.mult),
                             reads=bb + [b_flag], writes=[b_vA[b % 8]])
                        P.op("dve", lambda e: e.tensor_copy(out=onev, in_=flag[:, 0:1].unsqueeze(1).to_broadcast([128, 2, 1])),
                             reads=[b_flag], writes=[b_vA[b % 8]])
                    else:
                        P.op("dve", lambda e: e.tensor_copy(out=outv, in_=inv), reads=bb, writes=[b_vA[b % 8]])
                        P.op("dve", lambda e: e.memset(onev, 1.0), writes=[b_vA[b % 8]])

                if not full:
                    for b in blks:
                        do_v(b)
                    R.release(it_kv)
                    return False

                it_zu = [R.acquire(l, 3), R.acquire(l, 4)]
                it_zv = [R.acquire(l, 5), R.acquire(l, 6)]

                def tm_proj(gp, its):
                    ps, bb = bank1()
                    for q, it in enumerate(its):
                        s = it % NS
                        for kcl in range(4):
                            P.op("pe", lambda e, kcl=kcl, s=s, q=q: e.matmul(ps, lhsT=xTa[:, 4 * q + kcl, gp * 128:(gp + 1) * 128],
                                                                             rhs=ring[:, s, kcl * 512:(kcl + 1) * 512],
                                                                             start=(q == 0 and kcl == 0), stop=(q == 1 and kcl == 3)),
                                 reads=[b_ring[s], b_xTa[gp]], writes=bb, signal=(q == 1 and kcl == 3))
                    return ps, bb

                for b in blks:
                    do_v(b)

                stt = {}

                def stage_A(b):
                    gp = gp_of[b]
                    sd = stt[b] = {}
                    E = []
                    for j in range(2):
                        stmp, bl_stmp = stmp_pool.get()
                        for kb, kblk in enumerate((b - 1, b)):
                            ps, bb = bank1()
                            P.op("pe", lambda e, j=j, kblk=kblk, ps=ps: e.matmul(ps, lhsT=kT[64 * j:64 * j + 64, (kblk % 8) * 128:(kblk % 8 + 1) * 128],
                                                                                 rhs=qT[64 * j:64 * j + 64, :, gp * 128:(gp + 1) * 128],
                                                                                 start=True, stop=True),
                                 reads=[b_kT[kblk % 8]] + b_qT, writes=bb)
                            P.op("dve", lambda e, j=j, kb=kb, ps=ps, stmp=stmp: e.scalar_tensor_tensor(
                                out=stmp[:, kb * 512:(kb + 1) * 512], in0=ps, scalar=0.125,
                                in1=bias[:, kb * 1024 + j * 512:kb * 1024 + (j + 1) * 512], op0=ALU.mult, op1=ALU.add),
                                reads=bb + [b_bias], writes=bl_stmp)
                        Ej, bl_Ej = E_pool.get()
                        P.op("act", lambda e, Ej=Ej, stmp=stmp: e.activation(out=Ej, in_=stmp, func=AF.Exp), reads=bl_stmp, writes=bl_Ej)
                        E.append((Ej, bl_Ej))
                    sd["E"] = E
                    gx, bl_gx = gx_pool.get()
                    sd["gx"] = (gx, bl_gx)
                    ps_zu, bb_zu = tm_proj(gp, it_zu)
                    P.op("act", lambda e: e.activation(out=gx[:, 0:512], in_=ps_zu, func=AF.Gelu_apprx_tanh), reads=bb_zu, writes=bl_gx)
                    ps_zv, bb_zv = tm_proj(gp, it_zv)
                    P.op("act", lambda e: e.activation(out=gx[:, 512:1024], in_=ps_zv, func=AF.Gelu_apprx_tanh), reads=bb_zv, writes=bl_gx)

                def stage_B(b):
                    sd = stt[b]
                    gx, bl_gx = sd["gx"]
                    xv = gx[:, 512:1024]
                    po = []
                    for j in range(2):
                        Ej, bl_Ej = sd["E"][j]
                        ps, bb = bank1()
                        for c in range(4):
                            P.op("pe", lambda e, j=j, c=c, ps=ps, Ej=Ej: e.matmul(ps[:, c * 65:(c + 1) * 65], lhsT=Ej[:, c * 128:(c + 1) * 128],
                                                                                  rhs=vA[:, (b - 1) % 8, 65 * j:65 * j + 65], start=True, stop=False),
                                 reads=bl_Ej + [b_vA[(b - 1) % 8]], writes=bb, signal=False)
                            P.op("pe", lambda e, j=j, c=c, ps=ps, Ej=Ej: e.matmul(ps[:, c * 65:(c + 1) * 65], lhsT=Ej[:, 512 + c * 128:512 + (c + 1) * 128],
                                                                                  rhs=vA[:, b % 8, 65 * j:65 * j + 65], start=False, stop=True),
                                 reads=bl_Ej + [b_vA[b % 8]], writes=bb, signal=(c == 3))
                        po.append((ps, bb))
                    st6, b_st6 = stat.get()
                    P.op("dve", lambda e: e.bn_stats(out=st6[:, 0:6], in_=xv), reads=bl_gx, writes=[b_st6])
                    mv, b_mv = stat.get()
                    P.op("dve", lambda e: e.bn_aggr(out=mv[:, 0:2], in_=st6[:, 0:6]), reads=[b_st6], writes=[b_mv])
                    rs_ln, b_rs_ln = rstd_from(mv[:, 1:2], b_mv, 1.0)
                    den, b_den = stat.get()
                    for j in range(2):
                        ps, bb = po[j]
                        P.op("dve", lambda e, j=j, ps=ps: e.tensor_tensor(out=den[:, 4 * j:4 * j + 4].unsqueeze(2),
                                                                          in0=ps[:, 0:260].rearrange("p (c e) -> p c e", e=65)[:, :, 64:65],
                                                                          in1=esink[:, 4 * j:4 * j + 4].unsqueeze(2), op=ALU.add),
                             reads=bb + [b_esink], writes=[b_den])
                    rden, b_rden = stat.get()
                    P.op("dve", lambda e: e.reciprocal(out=rden[:, 0:8], in_=den[:, 0:8]), reads=[b_den], writes=[b_rden])
                    at, bl_at = at_pool.get()
                    for j in range(2):
                        ps, bb = po[j]
                        P.op("dve", lambda e, j=j, ps=ps: e.tensor_tensor(out=at[:, j * 256:(j + 1) * 256].rearrange("p (c d) -> p c d", c=4),
                                                                          in0=ps[:, 0:260].rearrange("p (c e) -> p c e", e=65)[:, :, 0:64],
                                                                          in1=rden[:, 4 * j:4 * j + 4].unsqueeze(2).to_broadcast([128, 4, 64]),
                                                                          op=ALU.mult),
                             reads=bb + [b_rden], writes=bl_at)
                    ss_a, b_ss_a = sumsq(at[:, 0:512], bl_at, 512)
                    rs_a, b_rs_a = rstd_from(ss_a, b_ss_a, 512.0)
                    sd["at"] = (at, bl_at, rs_a, b_rs_a)
                    P.op("dve", lambda e: e.tensor_scalar(out=xv, in0=xv, scalar1=mv[:, 0:1], scalar2=rs_ln,
                                                          op0=ALU.subtract, op1=ALU.mult), reads=bl_gx + [b_mv, b_rs_ln], writes=bl_gx)
                    P.op("dve", lambda e: e.tensor_tensor(out=xv, in0=xv, in1=rowv[:, 2048:2560], op=ALU.mult),
                         reads=bl_gx + [b_rowv], writes=bl_gx)
                    vh, bl_vh = vh_pool.get()
                    P.op("dve", lambda e: e.tensor_tensor(out=vh[:, 0:512], in0=xv, in1=rowv[:, 2560:3072], op=ALU.add),
                         reads=bl_gx + [b_rowv], writes=bl_vh)
                    sd["vh"] = (vh, bl_vh)

                def stage_C(b):
                    gp = gp_of[b]
                    sd = stt.pop(b)
                    gx, bl_gx = sd["gx"]
                    gu = gx[:, 0:512]
                    xv = gx[:, 512:1024]
                    vh, bl_vh = sd["vh"]
                    at, bl_at, rs_a, b_rs_a = sd["at"]
                    heads, b_heads = headsp.get()
                    ps_sp, bb_sp = bank1()
                    for hh in range(8):
                        P.op("pe", lambda e, hh=hh: e.matmul(ps_sp[:, hh * 64:(hh + 1) * 64], lhsT=wsT[:, hh, :], rhs=vh[:, hh * 64:(hh + 1) * 64],
                                                             start=True, stop=True),
                             reads=[b_wsT] + bl_vh, writes=bb_sp, signal=(hh == 7))
                    P.op("act", lambda e: e.activation(out=heads[:, 0:512], in_=at[:, 0:512], func=AF.Copy, scale=rs_a),
                         reads=bl_at + [b_rs_a], writes=[b_heads])
                    P.op("dve", lambda e: e.tensor_tensor(out=xv.rearrange("p (h c) -> p h c", h=8),
                                                          in0=ps_sp.rearrange("p (h c) -> p h c", h=8),
                                                          in1=bsT[:, 0:8].unsqueeze(2).to_broadcast([128, 8, 64]), op=ALU.add),
                         reads=bb_sp + [b_bsT] + bl_gx, writes=bl_gx)
                    P.op("dve", lambda e: e.tensor_tensor(out=gu, in0=gu, in1=xv, op=ALU.mult),
                         reads=bl_gx, writes=bl_gx)
                    ss_g, b_ss_g = sumsq(gu, bl_gx, 512)
                    rs_g, b_rs_g = rstd_from(ss_g, b_ss_g, 512.0)
                    P.op("act", lambda e: e.activation(out=heads[:, 512:1024], in_=gu, func=AF.Copy, scale=rs_g),
                         reads=bl_gx + [b_rs_g], writes=[b_heads])
                    transpose_to(heads, b_heads, xTb, b_xTb[gp], gp, 8)

                nf = len(full)
                for step in range(nf + 2):
                    if step < nf:
                        stage_A(full[step])
                    if 0 <= step - 1 < nf:
                        stage_B(full[step - 1])
                    if 0 <= step - 2 < nf:
                        stage_C(full[step - 2])
                for it in [it_kv] + it_zu + it_zv:
                    R.release(it)

                hoisted = False
                if nxt is not None:
                    norm_transpose_group(nxt, list(range(len(nxt))), xTa, [b_xTa[i] for i in range(len(nxt))], 0, len(nxt))
                    hoisted = True

                it_wo = [R.acquire(l, 7 + i) for i in range(4)]

                def m6_block(b):
                    gp = gp_of[b]
                    ps2, bb2 = bank2()
                    for i, it in enumerate(it_wo):
                        s = it % NS
                        c, q = i // 2, i % 2
                        for kcl in range(4):
                            P.op("pe", lambda e, kcl=kcl, s=s, c=c, q=q: e.matmul(ps2[:, c * 512:(c + 1) * 512], lhsT=xTb[:, 4 * q + kcl, gp * 128:(gp + 1) * 128],
                                                                                  rhs=ring[:, s, kcl * 512:(kcl + 1) * 512],
                                                                                  start=(q == 0 and kcl == 0), stop=(q == 1 and kcl == 3)),
                                 reads=[b_ring[s], b_xTb[gp]], writes=[bb2[c]], signal=(q == 1 and kcl == 3))
                    residual_epilogue(ps2, bb2, b, 0)

                for b in full:
                    m6_block(b)
                for it in it_wo:
                    R.release(it)

                return hoisted

            def part_ffn():
                if not pre_done:
                    norm_transpose_group(full, [gp_of[b] for b in full], xTa, [b_xTa[gp_of[b]] for b in full], 16, 1)
                c0 = gp_of[full[0]] * 128
                Ft = len(full) * 128
                b_xf = [b_xTa[gp_of[b]] for b in full]

                def f2_chunk(ci, sub, it_g, it_u):
                    pss = []
                    for it in (it_g, it_u):
                        s = it % NS
                        ps, bb = bank1()
                        for kc in range(KC):
                            P.op("pe", lambda e, kc=kc, s=s, ps=ps: e.matmul(ps[:, 0:Ft], lhsT=ring[:, s, kc * 256 + sub * 128:kc * 256 + sub * 128 + 128],
                                                                             rhs=xTa[:, kc, c0:c0 + Ft], start=(kc == 0), stop=(kc == KC - 1)),
                                 reads=[b_ring[s]] + b_xf, writes=bb, signal=(kc == KC - 1))
                        pss.append((ps, bb))
                    sl, b_sl = scr32.get()
                    ps_g, bb_g = pss[0]
                    ps_u, bb_u = pss[1]
                    P.op("act", lambda e: e.activation(out=sl[:, 0:Ft], in_=ps_g[:, 0:Ft], func=AF.Silu), reads=bb_g, writes=[b_sl])
                    P.op("dve", lambda e: e.tensor_tensor(out=actT[:, ci, c0:c0 + Ft], in0=ps_u[:, 0:Ft], in1=sl[:, 0:Ft], op=ALU.mult),
                         reads=bb_u + [b_sl], writes=[b_actT[ci]])

                for i in range(11):
                    it_g = R.acquire(l, 11 + 2 * i)
                    it_u = R.acquire(l, 12 + 2 * i)
                    for sub in range(2):
                        f2_chunk(2 * i + sub, sub, it_g, it_u)
                    conv_pump(2)
                    R.release(it_g)
                    R.release(it_u)

                hoisted = False
                if nxt is not None:
                    nblks, nfull = nxt
                    npos = {b: i for i, b in enumerate(nblks)}
                    norm_transpose_group(nfull, [npos[b] for b in nfull], xTa, [b_xTa[npos[b]] for b in nfull], 16, len(nfull), d16_pool)
                    hoisted = True

                acc = [bank2() for _ in full]
                for i in range(11):
                    it = R.acquire(l, 33 + i)
                    s = it % NS
                    for bi, b in enumerate(full):
                        gp = gp_of[b]
                        ps2, bb2 = acc[bi]
                        for kcl in range(2):
                            for half in range(2):
                                P.op("pe", lambda e, kcl=kcl, half=half, s=s, ps2=ps2, gp=gp, i=i: e.matmul(
                                    ps2[:, half * 512:(half + 1) * 512], lhsT=actT[:, 2 * i + kcl, gp * 128:(gp + 1) * 128],
                                    rhs=ring[:, s, kcl * 1024 + half * 512:kcl * 1024 + (half + 1) * 512],
                                    start=(i == 0 and kcl == 0), stop=(i == 10 and kcl == 1)),
                                    reads=[b_ring[s], b_actT[2 * i + kcl]], writes=[bb2[half]],
                                    signal=(i == 10 and kcl == 1 and half == 1))
                    R.release(it)
                for bi, b in enumerate(full):
                    ps2, bb2 = acc[bi]
                    residual_epilogue(ps2, bb2, b, 1024)

                return hoisted

            def part_ple():
                pti = do_group.pt_i % 2
                do_group.pt_i += 1
                pt, b_pt = pTt[pti], b_pTt[pti]
                t0 = full[0] * 128
                P.dma("pool", lambda e: e.dma_start(out=pt[:, :, 0:Ft], in_=pT_d[l, :, :, t0:t0 + Ft].rearrange("k p t -> p k t")),
                      f"pT{pti}", writes=b_pt)
                xTp, b_xTp = (xTb, b_xTb) if gidx % 2 == 0 else (xTa, b_xTa)
                xTn, b_xTn = (xTa, b_xTa) if gidx % 2 == 0 else (xTb, b_xTb)
                if not pre_done:
                    norm_transpose_group(full, [gp_of[b] for b in full], xTp, [b_xTp[gp_of[b]] for b in full], 24, 1)
                hoisted = False
                if nxt is not None:
                    nblks, nfull = nxt
                    npos = {b: i for i, b in enumerate(nblks)}
                    norm_transpose_group(nfull, [npos[b] for b in nfull], xTn, [b_xTn[npos[b]] for b in nfull], 24, len(nfull), d16_pool)
                    hoisted = True
                if l not in ple_handles:
                    ple_handles[l] = [R.acquire(l, 44 + i) for i in range(5)]
                it_ple = ple_handles[l][0]
                s_ple = it_ple % NS
                it_pg = ple_handles[l][1:]

                def p_block(bi, b):
                    gp = gp_of[b]
                    psg, bbg = bank2()
                    for i, it in enumerate(it_pg):
                        s = it % NS
                        c, q = i // 2, i % 2
                        for kcl in range(4):
                            P.op("pe", lambda e, kcl=kcl, s=s, c=c, q=q: e.matmul(psg[:, c * 512:(c + 1) * 512], lhsT=xTp[:, 4 * q + kcl, gp * 128:(gp + 1) * 128],
                                                                                  rhs=ring[:, s, kcl * 512:(kcl + 1) * 512],
                                                                                  start=(q == 0 and kcl == 0), stop=(q == 1 and kcl == 3)),
                                 reads=[b_ring[s], b_xTp[gp]], writes=[bbg[c]], signal=(q == 1 and kcl == 3))
                    psp, bbp = bank2()
                    for half in range(2):
                        for kc in range(2):
                            P.op("pe", lambda e, kc=kc, half=half: e.matmul(psp[:, half * 512:(half + 1) * 512], lhsT=pt[:, kc, bi * 128:(bi + 1) * 128],
                                                                            rhs=ring[:, s_ple, kc * 1024 + half * 512:kc * 1024 + (half + 1) * 512],
                                                                            start=(kc == 0), stop=(kc == 1)),
                                 reads=[b_ring[s_ple]] + b_pt, writes=[bbp[half]], signal=(kc == 1))
                    sg, b_sg = scr32.get()
                    P.op("act", lambda e: e.activation(out=sg[:], in_=psg, func=AF.Sigmoid), reads=bbg, writes=[b_sg])
                    P.op("dve", lambda e: e.tensor_tensor(out=sg[:], in0=psp, in1=sg[:], op=ALU.mult), reads=bbp + [b_sg], writes=[b_sg])
                    P.op("dve", lambda e: e.tensor_tensor(out=h[:, b, :], in0=h[:, b, :], in1=sg[:], op=ALU.add),
                         reads=[b_sg, b_h[b]], writes=[b_h[b]])
                    if last_layer and b >= H:
                        P.dma("sp", lambda e: e.dma_start(out=y_d[(b - H) * 128:(b - H + 1) * 128, :], in_=h[:, b, :]),
                              "yout", reads=[b_h[b]])

                for bi, b in enumerate(full):
                    p_block(bi, b)
                if nxt is None:
                    R.release(it_ple)
                    for it in it_pg:
                        R.release(it)
                return hoisted

            Gt_unused = None
            return {"mix": part_mix, "ffn": part_ffn, "ple": part_ple}[phase]()

        do_group.pt_i = 0
        ple_handles = {}

        for l in range(L):
            layer_setup(l)
            kvb, groups = layer_groups(l)
            if l + 1 < L:
                conv_layer(l + 1)
            fulls = [[b for b in blks if b > kvb] for blks in groups]
            fg = [gi for gi in range(len(groups)) if fulls[gi]]
            done = False
            for gi, blks in enumerate(groups):
                nxt = groups[gi + 1] if gi + 1 < len(groups) else None
                done = do_group(l, kvb, blks, l == L - 1, "mix", nxt, done, gi)
            done = False
            for k, gi in enumerate(fg):
                nxt = (groups[fg[k + 1]], fulls[fg[k + 1]]) if k + 1 < len(fg) else None
                done = do_group(l, kvb, groups[gi], l == L - 1, "ffn", nxt, done, k)
            conv_pump(10 ** 6)
            done = False
            for k, gi in enumerate(fg):
                nxt = (groups[fg[k + 1]], fulls[fg[k + 1]]) if k + 1 < len(fg) else None
                done = do_group(l, kvb, groups[gi], l == L - 1, "ple", nxt, done, k)
        P.wait_all("sp", b_h)
        P.emit()
    return nc


_PROG_CACHE = {}


def _get_prog(L, H):
    if (L, H) not in _PROG_CACHE:
        _PROG_CACHE[(L, H)] = build_program(L, H)
    return _PROG_CACHE[(L, H)]


def _constants():
    ident = np.eye(128, dtype=np.float32)
    s = np.arange(128)[:, None].astype(np.float32)
    t = np.arange(128)[None, :].astype(np.float32)
    slopes = np.exp2(-8.0 * (np.arange(8, dtype=np.float32) + 1.0) / 8.0).astype(np.float32)
    bias = np.zeros((128, 2, 8, 128), np.float32)
    for hh in range(8):
        dprev = t + 128.0 - s
        bias[:, 0, hh, :] = np.where(dprev < 128, -slopes[hh] * dprev, -30000.0)
        dcur = t - s
        bias[:, 1, hh, :] = np.where(dcur >= 0, -slopes[hh] * dcur, -30000.0)
    tril = (np.arange(128)[None, :] <= np.arange(128)[:, None]).astype(np.float32)
    return ident, bias.reshape(128, 2048), tril


def _run(hfull, layers, H, inp):
    L = len(layers)
    NB = H + OWN
    nc = _get_prog(L, H)
    ident, bias, tril = _constants()
    sl = slice(layers[0], layers[-1] + 1)
    c32 = lambda a: np.ascontiguousarray(a, dtype=np.float32)
    colg = np.stack([
        np.concatenate([
            inp["ln_mix_pre"][l].reshape(8, 128).T,
            np.concatenate([inp["g_attn_out"][l], inp["g_gm_out"][l]]).reshape(8, 128).T,
            inp["ln_ffn_pre"][l].reshape(8, 128).T,
            inp["ln_ple_gate"][l].reshape(8, 128).T], axis=1) for l in layers])
    rowv = np.stack([np.concatenate([inp["ln_mix_post"][l], inp["ln_ffn_post"][l], inp["gm_ln_g"][l], inp["gm_ln_b"][l]])[None, :]
                     for l in layers])
    bsT = np.stack([inp["gm_bs"][l].T for l in layers])
    sinks = np.stack([inp["attn_sinks"][l][None, :] for l in layers])
    shared = {
        "w_in": c32(inp["w_in"][sl]), "w_out": c32(inp["w_out"][sl]), "w_gate": c32(inp["w_ffn_gate"][sl]),
        "w_up": c32(inp["w_ffn_up"][sl]), "w_down": c32(inp["w_ffn_down"][sl]), "w_ple": c32(inp["w_ple"][sl]),
        "w_pg": c32(inp["w_ple_gate"][sl]), "gm_ws": c32(inp["gm_ws"][sl]),
        "colg": c32(colg), "rowv": c32(rowv), "bsT": c32(bsT), "sinks": c32(sinks),
        "ident": ident, "bias": bias, "tril": tril,
    }
    in_maps = []
    for c in range(NCORES):
        bt, j = c // 4, c % 4
        t0 = j * 2048 - H * 128
        xs = np.zeros((NB * 128, D), np.float32)
        ps = np.zeros((L, 2, 128, NB * 128), np.float32)
        lo = max(t0, 0)
        xs[lo - t0:] = hfull[bt, lo:(j + 1) * 2048]
        pp = inp["p"][sl, bt, lo:(j + 1) * 2048, :]
        ps[:, :, :, lo - t0:] = pp.transpose(0, 2, 1).reshape(L, 2, 128, -1)
        m = dict(shared)
        m["x"] = xs
        m["pT"] = ps
        m["flag"] = np.full((128, 1), 0.0 if j == 0 else 1.0, np.float32)
        in_maps.append(m)
    res = run_bass_kernel_spmd(nc, in_maps, core_ids=list(range(NCORES)))
    out = np.empty((2, 8192, D), np.float32)
    for c in range(NCORES):
        bt, j = c // 4, c % 4
        out[bt, j * 2048:(j + 1) * 2048] = res.results[c]["y"]
    return out


def kernel(**inputs):
    inp = {k: np.asarray(v) for k, v in inputs.items()}
    hcur = np.asarray(inp["x"], dtype=np.float32)
    if FUSED:
        return _run(hcur, [0, 1, 2, 3], 4, inp)
    for l in range(4):
        hcur = _run(hcur, [l], 1, inp)
    return hcur
```
